# Optimizing a Trainium2 kernel written in Bass

```python
import math
import jax, jax.numpy as jnp
from jax import lax
import numpy as np

D_MODEL = 1024
BATCH = 8
SEQ = 8192
DEPTH = 1

CHUNK = 64
D_CONV = D_MODEL
CONV_WIDTH = 31
SSM_EXPAND = 2
D_INNER = SSM_EXPAND * D_MODEL
HEAD_DIM = 64
N_SSM_HEADS = D_INNER // HEAD_DIM
D_STATE = 128
N_GROUPS = 4
SSM_CONV_WIDTH = 4
D_XBC = D_INNER + 2 * N_GROUPS * D_STATE
D_FF = ((8 * D_MODEL // 3 + 255) // 256) * 256
IN_COLS = 2 * D_CONV + D_INNER + D_XBC + N_SSM_HEADS + 2 * D_MODEL
NORM_EPS = 1e-6
LN_EPS = 1e-5

kernel_name = "gated_conformer_conv_mamba2_hybrid_block"


def rms_norm(x, w, eps=NORM_EPS):
    xf = x.astype(jnp.float32)
    y = xf * lax.rsqrt(jnp.mean(xf * xf, axis=-1, keepdims=True) + eps)
    return (y * w.astype(jnp.float32)).astype(x.dtype)


def layer_norm(x, w, b, eps=LN_EPS):
    xf = x.astype(jnp.float32)
    mu = jnp.mean(xf, axis=-1, keepdims=True)
    var = jnp.mean(jnp.square(xf - mu), axis=-1, keepdims=True)
    y = (xf - mu) * lax.rsqrt(var + eps)
    return (y * w.astype(jnp.float32) + b.astype(jnp.float32)).astype(x.dtype)


def causal_depthwise_conv(u, w, b):
    k, ch = w.shape
    out = lax.conv_general_dilated(
        u, w[:, None, :].astype(u.dtype), window_strides=(1,), padding=[(k - 1, 0)],
        dimension_numbers=("NWC", "WIO", "NWC"), feature_group_count=ch)
    return out + b.astype(u.dtype)


def adaln(c, w, b):
    mod = jax.nn.silu(c) @ w + b
    shift, scale, gate = jnp.split(mod, 3, axis=-1)
    return shift[:, None, :], scale[:, None, :], gate[:, None, :]


def ssd_chunked_scan(xs, dt, a, bm, cm):
    b, L, H, P = xs.shape
    G, N = bm.shape[2], bm.shape[3]
    J = H // G
    nc = L // CHUNK

    def to_chunks(t):
        return jnp.moveaxis(t.reshape(b, nc, CHUNK, *t.shape[2:]), 1, 0)

    xdt = (xs * dt[..., None]).reshape(b, L, G, J, P)
    da = (dt * a).reshape(b, L, G, J)
    xc, ac, bc, cc = to_chunks(xdt), to_chunks(da), to_chunks(bm), to_chunks(cm)
    causal = jnp.tril(jnp.ones((CHUNK, CHUNK), dtype=bool))[None, :, :, None, None]

    def step(state, inp):
        xk, ak, bk, ck = inp
        acs = jnp.cumsum(ak, axis=1)
        seg = acs[:, :, None] - acs[:, None, :]
        decay = jnp.exp(jnp.where(causal, seg, -jnp.inf))
        cb = jnp.einsum("blgn,bsgn->blsg", ck, bk)
        y_diag = jnp.einsum("blsg,blsgj,bsgjp->blgjp", cb, decay, xk)
        y_off = jnp.einsum("blgn,bgjpn,blgj->blgjp", ck, state, jnp.exp(acs))
        last = acs[:, -1]
        w_in = jnp.exp(last[:, None] - acs)
        new_state = state * jnp.exp(last)[..., None, None] + jnp.einsum(
            "bsgn,bsgj,bsgjp->bgjpn", bk, w_in, xk)
        return new_state, y_diag + y_off

    init = jnp.zeros((b, G, J, P, N), jnp.float32)
    _, ys = lax.scan(step, init, (xc, ac, bc, cc))
    return jnp.moveaxis(ys, 0, 1).reshape(b, L, H, P)


def mixer_sublayer(h, w_in, conv_dw_w, conv_dw_b, conv_ln_w, conv_ln_b, w_conv_out,
                   ssm_conv_w, ssm_conv_b, dt_bias, a_log, d_skip, ssm_norm_w, w_ssm_out, w_out):
    b, L, _ = h.shape
    proj = h @ w_in
    splits = np.cumsum([2 * D_CONV, D_INNER, D_XBC, N_SSM_HEADS, D_MODEL]).tolist()
    conv_in, z, xbc, dt_raw, g_conv, g_ssm = jnp.split(proj, splits, axis=-1)

    a_half, b_half = jnp.split(conv_in, 2, axis=-1)
    u = a_half * jax.nn.sigmoid(b_half)
    u = causal_depthwise_conv(u, conv_dw_w, conv_dw_b)
    u = jax.nn.silu(layer_norm(u, conv_ln_w, conv_ln_b))
    y_conv = u @ w_conv_out

    xbc = jax.nn.silu(causal_depthwise_conv(xbc, ssm_conv_w, ssm_conv_b))
    xs, bm, cm = jnp.split(xbc, [D_INNER, D_INNER + N_GROUPS * D_STATE], axis=-1)
    xs_f = xs.astype(jnp.float32).reshape(b, L, N_SSM_HEADS, HEAD_DIM)
    bm_f = bm.astype(jnp.float32).reshape(b, L, N_GROUPS, D_STATE)
    cm_f = cm.astype(jnp.float32).reshape(b, L, N_GROUPS, D_STATE)
    dt = jax.nn.softplus(dt_raw.astype(jnp.float32) + dt_bias.astype(jnp.float32))
    a = -jnp.exp(a_log.astype(jnp.float32))
    y = ssd_chunked_scan(xs_f, dt, a, bm_f, cm_f) + d_skip.astype(jnp.float32)[:, None] * xs_f
    y = y.reshape(b, L, D_INNER) * jax.nn.silu(z.astype(jnp.float32))
    yg = y.reshape(b, L, N_GROUPS, D_INNER // N_GROUPS)
    yg = yg * lax.rsqrt(jnp.mean(yg * yg, axis=-1, keepdims=True) + NORM_EPS)
    y = (yg.reshape(b, L, D_INNER) * ssm_norm_w.astype(jnp.float32)).astype(h.dtype)
    y_ssm = y @ w_ssm_out

    merged = jax.nn.sigmoid(g_conv) * y_conv + jax.nn.sigmoid(g_ssm) * y_ssm
    return merged @ w_out


def ffn_sublayer(h, w_ffn_in, w_ffn_down):
    gate, up = jnp.split(h @ w_ffn_in, 2, axis=-1)
    return (jax.nn.silu(gate) * up) @ w_ffn_down


def setup_inputs(seed: int = 0) -> dict:
    key = jax.random.key(seed)
    ks = jax.random.split(key, 32)
    f32 = jnp.float32
    nrm = lambda k, shape, s: jax.random.normal(k, shape, f32) * s
    gain = lambda k, shape: 1.0 + 0.05 * jax.random.normal(k, shape, f32)
    Ld = DEPTH
    u_dt = jax.random.uniform(ks[13], (Ld, N_SSM_HEADS), f32)
    dt0 = jnp.exp(u_dt * (math.log(0.1) - math.log(0.001)) + math.log(0.001))
    dt_bias = dt0 + jnp.log(-jnp.expm1(-dt0))
    a_log = jnp.log(jax.random.uniform(ks[14], (Ld, N_SSM_HEADS), f32, 1.0, 16.0))
    return {
        "x": jax.random.normal(ks[0], (BATCH, SEQ, D_MODEL), f32),
        "c": jax.random.normal(ks[1], (BATCH, D_MODEL), f32),
        "w_ada_mix": nrm(ks[2], (Ld, D_MODEL, 3 * D_MODEL), D_MODEL ** -0.5),
        "b_ada_mix": nrm(ks[3], (Ld, 3 * D_MODEL), 0.01),
        "norm_mix_w": gain(ks[4], (Ld, D_MODEL)),
        "w_in": nrm(ks[5], (Ld, D_MODEL, IN_COLS), D_MODEL ** -0.5),
        "conv_dw_w": nrm(ks[6], (Ld, CONV_WIDTH, D_CONV), CONV_WIDTH ** -0.5),
        "conv_dw_b": nrm(ks[7], (Ld, D_CONV), 0.01),
        "conv_ln_w": gain(ks[8], (Ld, D_CONV)),
        "conv_ln_b": nrm(ks[9], (Ld, D_CONV), 0.01),
        "w_conv_out": nrm(ks[10], (Ld, D_CONV, D_MODEL), D_CONV ** -0.5),
        "ssm_conv_w": nrm(ks[11], (Ld, SSM_CONV_WIDTH, D_XBC), SSM_CONV_WIDTH ** -0.5),
        "ssm_conv_b": nrm(ks[12], (Ld, D_XBC), 0.01),
        "dt_bias": dt_bias,
        "a_log": a_log,
        "d_skip": gain(ks[15], (Ld, N_SSM_HEADS)),
        "ssm_norm_w": gain(ks[16], (Ld, D_INNER)),
        "w_ssm_out": nrm(ks[17], (Ld, D_INNER, D_MODEL), D_INNER ** -0.5),
        "w_out": nrm(ks[18], (Ld, D_MODEL, D_MODEL), D_MODEL ** -0.5),
        "w_ada_ffn": nrm(ks[19], (Ld, D_MODEL, 3 * D_MODEL), D_MODEL ** -0.5),
        "b_ada_ffn": nrm(ks[20], (Ld, 3 * D_MODEL), 0.01),
        "norm_ffn_w": gain(ks[21], (Ld, D_MODEL)),
        "w_ffn_in": nrm(ks[22], (Ld, D_MODEL, 2 * D_FF), D_MODEL ** -0.5),
        "w_ffn_down": nrm(ks[23], (Ld, D_FF, D_MODEL), D_FF ** -0.5),
        "final_norm_w": gain(ks[24], (D_MODEL,)),
    }


def reference(x, c, w_ada_mix, b_ada_mix, norm_mix_w, w_in, conv_dw_w, conv_dw_b, conv_ln_w,
              conv_ln_b, w_conv_out, ssm_conv_w, ssm_conv_b, dt_bias, a_log, d_skip, ssm_norm_w,
              w_ssm_out, w_out, w_ada_ffn, b_ada_ffn, norm_ffn_w, w_ffn_in, w_ffn_down,
              final_norm_w):
    for i in range(DEPTH):
        shift, scale, gate = adaln(c, w_ada_mix[i], b_ada_mix[i])
        h = rms_norm(x, norm_mix_w[i]) * (1.0 + scale) + shift
        x = x + gate * mixer_sublayer(
            h, w_in[i], conv_dw_w[i], conv_dw_b[i], conv_ln_w[i], conv_ln_b[i], w_conv_out[i],
            ssm_conv_w[i], ssm_conv_b[i], dt_bias[i], a_log[i], d_skip[i], ssm_norm_w[i],
            w_ssm_out[i], w_out[i])
        shift, scale, gate = adaln(c, w_ada_ffn[i], b_ada_ffn[i])
        h = rms_norm(x, norm_ffn_w[i]) * (1.0 + scale) + shift
        x = x + gate * ffn_sublayer(h, w_ffn_in[i], w_ffn_down[i])
    return rms_norm(x, final_norm_w)
```

```python
import contextlib
import numpy as np
import concourse.bass as bass
import concourse.mybir as mybir
from concourse.bass_utils import run_bass_kernel_spmd

F32 = mybir.dt.float32
BF16 = mybir.dt.bfloat16
AF = mybir.ActivationFunctionType
ALU = mybir.AluOpType

D = 1024
KC = 8
TT = 512
NB = 4
SEQ = 8192
DIN = 2048
NH = 32
HD = 64
NG = 4
NST = 128
DFF = 2816
NFF = 22
INC = 9248
NSLOT = 4
SAME_ENG_SYNC = True
DBG_STOP = None
EPS = 1e-6
LN_EPS = 1e-5


class _Stop(Exception):
    pass


class SemObj:
    def __init__(self, sem, owner, sid):
        self.sem = sem
        self.owner = owner
        self.id = sid
        self.cnt = 0


class Eng:
    def __init__(self, name, h, so):
        self.name = name
        self.h = h
        self.so = so
        so.owner = self
        self.waited = {}
        self.last = None


class V:
    __slots__ = ("ap", "sp", "lo", "hi")

    def __init__(self, ap, sp, lo, hi):
        self.ap = ap
        self.sp = sp
        self.lo = lo
        self.hi = hi


class Buf:
    def __init__(self, K, name, shape, dt, at=None):
        self.esz = 4 if dt == F32 else 2
        nfree = int(np.prod(shape[1:]))
        self.nbytes = nfree * self.esz
        if at is None:
            at = K.sb_alloc(self.nbytes)
        self.off = at
        self.shape = list(shape)
        self.dt = dt
        self.nfree = nfree
        self.t = K.nc.alloc_sbuf_tensor_at(name, list(shape), dt, offset=at)
        st = []
        acc = 1
        for d in reversed(shape[1:]):
            st.append(acc)
            acc *= d
        self.strides = list(reversed(st))

    def __getitem__(self, idx):
        if not isinstance(idx, tuple):
            idx = (idx,)
        ap = self.t[idx]
        lo = 0
        hi = 0
        fr = idx[1:]
        for i, d in enumerate(self.shape[1:]):
            if i < len(fr):
                x = fr[i]
                if isinstance(x, int):
                    a, b = x, x + 1
                else:
                    a = 0 if x.start is None else x.start
                    b = d if x.stop is None else x.stop
            else:
                a, b = 0, d
            lo += a * self.strides[i]
            hi += (b - 1) * self.strides[i]
        hi += 1
        return V(ap, "sb", self.off + lo * self.esz, self.off + hi * self.esz)

    def cust(self, off, dims, p0=0, npart=128, lo=None, hi=None):
        ap = bass.AP(self.t, p0 * self.nfree + off, [[self.nfree, npart]] + [list(d) for d in dims])
        if lo is None:
            lo = 0
            hi = self.nfree
        return V(ap, "sb", self.off + lo * self.esz, self.off + hi * self.esz)


class PsBank:
    def __init__(self, K, name, idx, dt):
        n = 512 if dt == F32 else 1024
        self.t = K.nc.alloc_psum_tensor(name, [128, n], dt)
        self.idx = idx
        self.n = n

    def __getitem__(self, idx):
        return V(self.t[idx], "ps", self.idx, self.idx + 1)

    def cust(self, off, dims, p0=0, npart=128):
        ap = bass.AP(self.t, p0 * self.n + off, [[self.n, npart]] + [list(d) for d in dims])
        return V(ap, "ps", self.idx, self.idx + 1)


class K:
    def __init__(self, nc, es):
        self.nc = nc
        self.es = es
        self.sb_top = 16640
        self.nsem = 0
        self.recs = {"sb": [], "ps": []}
        self.PE = Eng("pe", nc.tensor, self.newsem("pe"))
        self.ACT = Eng("act", nc.scalar, self.newsem("act"))
        self.DVE = Eng("dve", nc.vector, self.newsem("dve"))
        self.POOL = Eng("pool", nc.gpsimd, self.newsem("pool"))
        self.SP = Eng("sp", nc.sync, self.newsem("sp"))

    def newsem(self, name):
        s = self.es.enter_context(self.nc.semaphore(name + "_%d" % self.nsem))
        so = SemObj(s, None, self.nsem)
        self.nsem += 1
        return so

    def sb_alloc(self, nbytes):
        a = self.sb_top
        self.sb_top += (nbytes + 63) // 64 * 64
        return a

    def _need(self, eng, so, val):
        if so.owner is eng and not (SAME_ENG_SYNC and eng is not self.PE):
            return
        if so.owner is not None and so.cnt < val:
            o = so.owner
            o.last.then_inc(so.sem, 1)
            so.cnt += 1
            assert so.cnt >= val, (o.name, so.cnt, val)
        if eng.waited.get(so.id, 0) < val:
            eng.h.wait_ge(so.sem, val)
            eng.waited[so.id] = val

    def _sync(self, eng, reads, writes):
        needs = {}
        for v in reads:
            if v is None or v.sp is None:
                continue
            for r in self.recs[v.sp]:
                if (r[2] == "W" or (v.sp == "ps" and r[3].owner is not eng)) and r[0] < v.hi and v.lo < r[1]:
                    if needs.get(r[3], 0) < r[4]:
                        needs[r[3]] = r[4]
        for v in writes:
            if v is None or v.sp is None:
                continue
            for r in self.recs[v.sp]:
                if r[0] < v.hi and v.lo < r[1]:
                    if needs.get(r[3], 0) < r[4]:
                        needs[r[3]] = r[4]
        for so, val in needs.items():
            self._need(eng, so, val)

    def _record(self, reads, writes, so, val):
        for v in writes:
            if v is None or v.sp is None:
                continue
            L = self.recs[v.sp]
            L[:] = [r for r in L if not (v.lo <= r[0] and r[1] <= v.hi)]
            L.append([v.lo, v.hi, "W", so, val])
        for v in reads:
            if v is None or v.sp is None:
                continue
            L = self.recs[v.sp]
            L[:] = [r for r in L if not (r[2] == "R" and r[3] is so and v.lo <= r[0] and r[1] <= v.hi)]
            L.append([v.lo, v.hi, "R", so, val])

    def emit(self, eng, fn, reads, writes, mark=True):
        self._sync(eng, reads, writes)
        ins = fn()
        so = eng.so
        if mark:
            ins.then_inc(so.sem, 1)
            so.cnt += 1
            val = so.cnt
        else:
            val = so.cnt + 1
        eng.last = ins
        self._record(reads, writes, so, val)
        return ins

    def dma(self, q, out, in_, dsem, out_ap=None, in_ap=None):
        self._sync(q, [in_], [out])
        oa = out.ap if out is not None else out_ap
        ia = in_.ap if in_ is not None else in_ap
        ins = q.h.dma_start(out=oa, in_=ia)
        ins.then_inc(dsem.sem, 16)
        dsem.cnt += 16
        self._record([in_], [out], dsem, dsem.cnt)
        return ins

    def mm(self, out, lhsT, rhs, start=True, stop=True, mark=False):
        return self.emit(self.PE, lambda: self.nc.tensor.matmul(out.ap, lhsT.ap, rhs.ap, start=start, stop=stop),
                         [lhsT, rhs], [out], mark=mark)

    def tr(self, out, in_, ident, mark=False):
        return self.emit(self.PE, lambda: self.nc.tensor.transpose(out.ap, in_.ap, ident.ap),
                         [in_, ident], [out], mark=mark)

    def act(self, out, in_, func, bias=None, scale=1.0, accum=None):
        rd = [in_]
        kw = {}
        if isinstance(bias, V):
            rd.append(bias)
            kw["bias"] = bias.ap
        elif bias is not None:
            kw["bias"] = float(bias)
        if isinstance(scale, V):
            rd.append(scale)
            kw["scale"] = scale.ap
        else:
            kw["scale"] = float(scale)
        wr = [out]
        if accum is not None:
            wr.append(accum)
            kw["accum_out"] = accum.ap
        return self.emit(self.ACT, lambda: self.nc.scalar.activation(out=out.ap, in_=in_.ap, func=func, **kw), rd, wr)

    def ts(self, eng, out, in_, s1, s2=None, op0=ALU.mult, op1=None):
        rd = [in_]
        a1 = s1
        a2 = s2
        if isinstance(s1, V):
            rd.append(s1)
            a1 = s1.ap
        if isinstance(s2, V):
            rd.append(s2)
            a2 = s2.ap
        if op1 is None:
            fn = lambda: eng.h.tensor_scalar(out.ap, in_.ap, a1, None, op0)
        else:
            fn = lambda: eng.h.tensor_scalar(out.ap, in_.ap, a1, a2, op0, op1)
        return self.emit(eng, fn, rd, [out])

    def tt(self, eng, out, a, b, op):
        return self.emit(eng, lambda: eng.h.tensor_tensor(out.ap, a.ap, b.ap, op), [a, b], [out])

    def stt(self, eng, out, in0, sc, in1, op0, op1):
        rd = [in0, in1]
        a = sc
        if isinstance(sc, V):
            rd.append(sc)
            a = sc.ap
        return self.emit(eng, lambda: eng.h.scalar_tensor_tensor(out.ap, in0.ap, a, in1.ap, op0, op1), rd, [out])

    def cp(self, eng, out, in_):
        if eng is self.ACT:
            return self.emit(eng, lambda: self.nc.scalar.copy(out.ap, in_.ap), [in_], [out])
        return self.emit(eng, lambda: eng.h.tensor_copy(out.ap, in_.ap), [in_], [out])

    def memset(self, eng, out, val):
        return self.emit(eng, lambda: eng.h.memset(out.ap, val), [], [out])


def piece_table():
    P = []

    def add(kind, **kw):
        kw["kind"] = kind
        P.append(kw)
        return len(P) - 1

    ids = {}
    c_a, c_b, c_z, c_xs, c_B, c_C, c_dt, c_gc, c_gs = 0, 1024, 2048, 4096, 6144, 6656, 7168, 7200, 8224
    for i in range(2):
        ids["a%d" % i] = add("w", w="w_in", k0=0, nk=8, segs=[(c_a + 512 * i, 512)])
        ids["b%d" % i] = add("w", w="w_in", k0=0, nk=8, segs=[(c_b + 512 * i, 512)])
    for i in range(8):
        ids["c31_%d" % i] = add("c31", fc=i)
    for i in range(2):
        ids["co%d" % i] = add("w", w="w_conv_out", k0=0, nk=8, segs=[(512 * i, 512)])
        ids["gc%d" % i] = add("w", w="w_in", k0=0, nk=8, segs=[(c_gc + 512 * i, 512)])
    for i in range(6):
        ids["c4_%d" % i] = add("c4", g=i)
    for i in range(4):
        ids["xs%d" % i] = add("w", w="w_in", k0=0, nk=8, segs=[(c_xs + 512 * i, 512)])
    ids["B"] = add("w", w="w_in", k0=0, nk=8, segs=[(c_B, 512)])
    ids["C"] = add("w", w="w_in", k0=0, nk=8, segs=[(c_C, 512)])
    ids["dt"] = add("w", w="w_in", k0=0, nk=8, segs=[(c_dt, 32)])
    for i in range(4):
        ids["z%d" % i] = add("w", w="w_in", k0=0, nk=8, segs=[(c_z + 512 * i, 512)])
    ids["dsk"] = add("dsk")
    for cg in range(2):
        for kh in range(2):
            ids["so%d%d" % (cg, kh)] = add("w", w="w_ssm_out", k0=8 * kh, nk=8, segs=[(512 * cg, 512)], rs="snw")
        ids["gs%d" % cg] = add("w", w="w_in", k0=0, nk=8, segs=[(c_gs + 512 * cg, 512)])
    for i in range(2):
        ids["wo%d" % i] = add("w", w="w_out", k0=0, nk=8, segs=[(512 * i, 512)], cs=("gate1", 512 * i))
    for m in range(11):
        ids["fi%d" % m] = add("w", w="w_ffn_in", k0=0, nk=8, segs=[(256 * m, 256), (DFF + 256 * m, 256)])
    for cg in range(2):
        for ks in range(3):
            nk = 8 if ks < 2 else 6
            ids["fd%d%d" % (cg, ks)] = add("w", w="w_ffn_down", k0=8 * ks, nk=nk, segs=[(512 * cg, 512)],
                                          cs=("gate2", 512 * cg))
    return P, ids


def tile_order(ids):
    o = ["a0", "b0", "a1", "b1"] + ["c31_%d" % i for i in range(8)] + ["z0", "z1", "z2", "z3", "co0", "gc0", "co1", "gc1"]
    o += ["c4_0", "xs0", "c4_1", "xs1", "c4_2", "xs2", "c4_3", "xs3", "c4_4", "B", "c4_5", "C", "dt", "dsk"]
    o += ["so00", "so01", "gs0", "so10", "so11", "gs1", "wo0", "wo1"]
    o += ["fi%d" % m for m in range(11)]
    o += ["fd00", "fd01", "fd02", "fd10", "fd11", "fd12"]
    return o


def build(NT):
    nc = bass.Bass("TRN2", target_bir_lowering=False)
    Lc = NT * TT
    dram = {}

    def din(name, shape):
        dram[name] = nc.dram_tensor(name, list(shape), F32, kind="ExternalInput").ap()
        return dram[name]

    x_d = din("x", [Lc, D])
    cT_d = din("cT", [128, 8])
    din("w_ada_mix", [D, 3072])
    din("w_ada_ffn", [D, 3072])
    din("bT_ada_mix", [128, 24])
    din("bT_ada_ffn", [128, 24])
    din("brow_ada_mix", [1, 3072])
    din("brow_ada_ffn", [1, 3072])
    din("nw1T", [128, 8])
    din("nw2T", [128, 8])
    din("w_in", [D, INC])
    din("w31T", [128, 8, 31])
    din("cb31T", [128, 8])
    din("lnwT", [128, 8])
    din("lnbT", [128, 8])
    din("w_conv_out", [D, D])
    din("w4T", [128, 24, 4])
    din("cb4T", [128, 24])
    din("cb4mat", [32, 128])
    din("dtb_bc", [128, 128])
    din("alog_bc", [128, 128])
    din("dsk_bc", [128, 32])
    din("snwT", [128, 16])
    din("w_ssm_out", [DIN, D])
    din("w_out", [D, D])
    din("w_ffn_in", [D, 2 * DFF])
    din("w_ffn_down", [DFF, D])
    din("fw_bc", [128, D])
    din("ident", [128, 128])
    din("tri", [128, 128])
    din("negmask4", [128, 512])
    din("causal4", [128, 512])
    din("sel", [64, 4096])
    y_d = nc.dram_tensor("y", [Lc, D], F32, kind="ExternalOutput").ap()

    pieces, ids = piece_table()
    NP = len(pieces)
    wscr = nc.dram_tensor("wscr", [NP, 128, 4096], BF16).ap()

    es = contextlib.ExitStack()
    with es:
      try:
        k = K(nc, es)
        PE, ACT, DVE, POOL, SP = k.PE, k.ACT, k.DVE, k.POOL, k.SP
        EW = [DVE, POOL]
        rr = [0]

        def ew():
            rr[0] += 1
            return EW[rr[0] % 2]

        def B(name, shape, dt, at=None):
            return Buf(k, name, shape, dt, at)

        ident_f = B("ident_f", [128, 128], F32)
        ident_b = B("ident_b", [128, 128], BF16)
        tri_f = B("tri_f", [128, 128], F32)
        ones_f = B("ones_f", [128, 128], F32)
        ones_b = B("ones_b", [128, 128], BF16)
        negm = B("negm", [128, 512], BF16)
        caus = B("caus", [128, 512], BF16)
        sel = B("sel", [64, 4096], BF16)
        fw_bc = B("fw_bc", [128, D], F32)
        dtb = B("dtb", [128, 128], F32)
        a_bc = B("a_bc", [128, 128], F32)
        cb4mat = B("cb4mat", [32, 128], BF16)
        prm = B("prm", [128, 160], F32)
        G1, SH1, G2, SH2, CB31, LNW, LNB, CB4, SNW = 0, 8, 16, 24, 32, 40, 48, 56, 80
        x_res = B("x_res", [128, NB, D], F32)
        hT = B("hT", [128, KC, TT], BF16)
        S = B("S", [128, DIN], F32)
        Sb = B("Sb", [128, DIN], BF16)
        uhalo = B("uhalo", [128, 8, 32], BF16)
        xhalo = B("xhalo", [128, 24, 4], BF16)
        slots = [B("slot%d" % i, [128, 8, 512], BF16) for i in range(NSLOT)]
        slot_sem = [k.newsem("slot") for _ in range(NSLOT)]
        arB = k.sb_alloc(33 * 1024)
        x_in = B("x_in", [128, NB, D], F32, at=arB)
        uT = B("uT", [128, 8, 544], BF16, at=arB)
        uv = B("uv", [128, 8, TT], BF16, at=arB + 8704)
        sgt = [B("sgt%d" % i, [128, TT], F32, at=arB + 16896 + 2048 * i) for i in range(2)]
        usq = [B("usq%d" % i, [128, TT], BF16, at=arB + 20992 + 1024 * i) for i in range(2)]
        mean_t = B("mean_t", [128, TT], F32, at=arB + 23040)
        rstd_t = B("rstd_t", [128, TT], F32, at=arB + 25088)
        msq_t = B("msq_t", [128, TT], F32, at=arB + 27136)
        lnt = [B("lnt%d" % i, [128, TT], F32, at=arB + 29184 + 2048 * i) for i in range(2)]
        assert 29184 + 4096 <= 33 * 1024
        ynT = B("ynT", [128, 16, TT], BF16, at=arB)
        yc = B("yc", [128, DIN], F32, at=arB + 16384)
        mg = B("mg", [128, KC, TT], BF16, at=arB + 24576)
        arC = k.sb_alloc(32 * 1024)
        xs = B("xs", [128, NB, DIN], BF16, at=arC)
        siluz = B("siluz", [128, NB, DIN], BF16, at=arC + 16384)
        actT = B("actT", [128, NFF, TT], BF16, at=arC)
        sl = [B("sl%d" % i, [128, TT], F32, at=arC + 22528 + 2048 * i) for i in range(2)]
        xb = [B("xb%d" % i, [128, D], BF16) for i in range(2)]
        junk = B("junk", [128, D], BF16)
        st4 = B("st4", [128, 16], F32)
        t1 = B("t1", [128, KC, TT], BF16)
        xst = [B("xst%d" % i, [128, 520], BF16) for i in range(3)]
        BT = B("BT", [128, NG, TT], BF16)
        CT = B("CT", [128, NG, TT], BF16)
        Btok = B("Btok", [128, NB, 512], BF16)
        dtp = B("dtp", [128, 128], F32)
        dtt = B("dtt", [128, 128], F32)
        lndt = B("lndt", [128, 128], F32)
        da2 = B("da2", [128, NB, 64], F32)
        biasA = B("biasA", [128, 128], F32)
        Ee = B("Ee", [128, 128], F32)
        wdt = B("wdt", [128, 128], F32)
        decA = B("decA", [128, 128], F32)
        tmpd = B("tmpd", [128, 128], F32)
        acsT = B("acsT", [64, 512], BF16)
        acsTt = B("acsTt", [64, 512], BF16)
        cbm = B("cbm", [128, NG, 128], BF16)
        expo = [B("expo%d" % i, [128, 8, 128], BF16) for i in range(2)] + [B("expo2", [128, 8, 128], BF16, at=arB + 16384 + 4096)]
        xw = [B("xw%d" % i, [128, 512], BF16) for i in range(2)] + [B("xw2", [128, 512], BF16, at=arB + 16384 + 4096 + 2048)]
        otmp = [B("otmp%d" % i, [128, 512], F32) for i in range(2)]
        yn = B("yn", [128, DIN], BF16)
        stg_f = [B("stgf%d" % i, [128, 8, 512], F32, at=arB + 16384 * i) for i in range(2)]
        stg_f += [B("stgf%d" % (2 + i), [128, 8, 512], F32, at=slots[2 * i].off) for i in range(2)]
        stg_b = [B("stgb%d" % i, [128, 8, 512], BF16, at=arC + 8192 * i) for i in range(4)]
        NSF, NSB = 4, 4
        gate_bc = [B("gate_bc%d" % i, [128, D], F32, at=x_res.off + 4096 * i) for i in range(2)]
        sc_rep = B("sc_rep", [128, KC, 128], F32, at=arC + 24576)
        scv = B("scv", [128, 8], F32)
        print("SBUF bytes/partition used:", k.sb_top)
        assert k.sb_top <= 224 * 1024 - 2048, k.sb_top

        psf = [PsBank(k, "psf%d" % i, i, F32) for i in range(6)]
        psb = [PsBank(k, "psb%d" % i, 6 + i, BF16) for i in range(2)]

        ld_sem = k.newsem("ld")
        stgf_sem = [k.newsem("stgf") for _ in range(4)]
        stgb_sem = [k.newsem("stgb") for _ in range(4)]
        st_sem = k.newsem("st")
        xin_sem = k.newsem("xin")
        xres_sem = k.newsem("xres")
        y_sem = k.newsem("ysem")

        def load(dst, src_ap, q=SP, sem=None):
            return k.dma(q, dst, None, sem or k.newsem("l"), in_ap=src_ap)

        c_stage = stg_f[0]
        for (dstb, name, np_, ncol) in [(ident_b, "ident", 128, 128), (ones_b, None, 128, 128), (negm, "negmask4", 128, 512),
                                        (caus, "causal4", 128, 512)]:
            if name is None:
                continue
            v = c_stage.cust(0, [[1, ncol]], 0, np_, 0, ncol)
            load(v, dram[name][:, :])
            k.cp(DVE, dstb[:, :], v)
        load(ident_f[:, :], dram["ident"][:, :])
        load(tri_f[:, :], dram["tri"][:, :])
        k.memset(DVE, ones_f[:, :], 1.0)
        k.memset(DVE, ones_b[:, :], 1.0)
        for h in range(4):
            v = c_stage.cust(0, [[1, 1024]], 0, 64, 0, 1024)
            load(v, dram["sel"][:, h * 1024:(h + 1) * 1024])
            k.cp(DVE, sel[:, h * 1024:(h + 1) * 1024], v)
        v = c_stage.cust(0, [[1, 128]], 0, 32, 0, 128)
        load(v, dram["cb4mat"][:, :])
        k.cp(DVE, cb4mat[:, :], v)
        load(fw_bc[:, :], dram["fw_bc"][:, :])
        load(dtb[:, :], dram["dtb_bc"][:, :])
        load(a_bc[:, :], dram["alog_bc"][:, :])
        k.act(a_bc[:, :], a_bc[:, :], AF.Exp)
        k.ts(DVE, a_bc[:, :], a_bc[:, :], -1.0)
        for (col, name, n) in [(CB31, "cb31T", 8), (LNW, "lnwT", 8), (LNB, "lnbT", 8), (CB4, "cb4T", 24), (SNW, "snwT", 16)]:
            load(prm[:, col:col + n], dram[name][:, :])
        w31 = B("w31", [128, 8, 31], F32)
        w4 = B("w4", [128, 24, 4], F32)
        dskb = B("dskb", [128, 32], F32)
        load(w31[:, :, :], dram["w31T"][:, :, :])
        load(w4[:, :, :], dram["w4T"][:, :, :])
        load(dskb[:, :], dram["dsk_bc"][:, :])
        k.memset(DVE, S[:, :], 0.0)
        k.memset(DVE, Sb[:, :], 0.0)
        k.memset(DVE, uhalo[:, :, :], 0.0)
        k.memset(DVE, xhalo[:, :, :], 0.0)

        if DBG_STOP == "s0":
            raise _Stop()
        load(scv[:, :], cT_d[:, :])
        k.act(scv[:, :], scv[:, :], AF.Silu)
        for kc in range(KC):
            k.ts(DVE, sc_rep[:, kc, :], ones_f[:, :], scv[:, kc:kc + 1])
        nwt = B("nwt", [128, 16], F32)
        load(nwt[:, 0:8], dram["nw1T"][:, :])
        load(nwt[:, 8:16], dram["nw2T"][:, :])
        bTt = B("bTt", [128, 48], F32)
        load(bTt[:, 0:24], dram["bT_ada_mix"][:, :])
        load(bTt[:, 24:48], dram["bT_ada_ffn"][:, :])
        browt = B("browt", [1, 2048], F32, at=x_res.off + 8192)
        rowsb = B("rowsb", [1, 512], F32, at=hT.off)
        load(browt[:, 0:1024], dram["brow_ada_mix"][:, 2048:3072])
        load(browt[:, 1024:2048], dram["brow_ada_ffn"][:, 2048:3072])
        for li, wn in enumerate(["w_ada_mix", "w_ada_ffn"]):
            wv = dram[wn].rearrange("(kc p) f -> p kc f", p=128)
            gcol, scol = (G1, SH1) if li == 0 else (G2, SH2)
            for pc in range(6):
                sf = stg_f[pc % 2]
                load(sf[:, :, :], wv[:, :, pc * 512:(pc + 1) * 512], sem=stgf_sem[pc % 2])
                if pc < 4:
                    pr = psf[4 + pc % 2]
                    for kc in range(KC):
                        k.mm(pr[0:1, :], scv[:, kc:kc + 1], sf[:, kc, :], start=(kc == 0), stop=(kc == KC - 1), mark=(kc == KC - 1))
                    k.cp(DVE, rowsb[0:1, :], pr[0:1, :])
                    ps = psf[pc % 2]
                    for fcl in range(4):
                        k.mm(ps[:, fcl:fcl + 1], rowsb[0:1, fcl * 128:(fcl + 1) * 128], ones_f[0:1, 0:1], start=True, stop=True,
                             mark=(fcl == 3))
                    fc0 = pc * 4
                    if pc < 2:
                        k.tt(DVE, prm[:, scol + fc0:scol + fc0 + 4], ps[:, 0:4], bTt[:, li * 24 + fc0:li * 24 + fc0 + 4], ALU.add)
                    else:
                        f8 = fc0 - 8
                        k.tt(DVE, st4[:, 0:4], ps[:, 0:4], bTt[:, li * 24 + fc0:li * 24 + fc0 + 4], ALU.add)
                        k.stt(DVE, prm[:, gcol + f8:gcol + f8 + 4], st4[:, 0:4], 1.0, nwt[:, li * 8 + f8:li * 8 + f8 + 4],
                              ALU.add, ALU.mult)
                else:
                    ps = psf[2 + pc % 2]
                    h = pc - 4
                    for kc in range(KC):
                        k.mm(ps[:, :], sc_rep[:, kc, :], sf[:, kc, :], start=(kc == 0), stop=False)
                    k.mm(ps[:, :], ones_f[0:1, :], browt[0:1, li * 1024 + h * 512:li * 1024 + (h + 1) * 512], start=False,
                         stop=True, mark=True)
                    k.cp(DVE, gate_bc[li][:, h * 512:(h + 1) * 512], ps[:, :])

        if DBG_STOP == "s1":
            raise _Stop()
        ceng = [DVE, ACT, DVE]

        def diag(j, dst, sc):
            if j % 2 == 0:
                k.ts(DVE, dst, ident_f[:, :], sc)
            else:
                k.act(dst, ident_f[:, :], AF.Identity, scale=sc)
        stq = POOL
        w_ids = [i for i, p_ in enumerate(pieces) if p_["kind"] == "w"]
        d_ids = [i for i, p_ in enumerate(pieces) if p_["kind"] != "w"]
        prep_order = []
        while w_ids or d_ids:
            prep_order += w_ids[:3]
            w_ids = w_ids[3:]
            if d_ids:
                prep_order.append(d_ids.pop(0))
        fcnt = [0]
        piece_done = {}
        for sb0 in stg_b:
            k.memset(DVE, sb0[:, :, :], 0.0)
        for bi, pi in enumerate(prep_order):
            pc = pieces[pi]
            sb_ = stg_b[bi % NSB]
            if pc["kind"] == "w":
                fi_ = fcnt[0] % NSF
                fcnt[0] += 1
                sf = stg_f[fi_]
                wv = dram[pc["w"]].rearrange("(kc p) f -> p kc f", p=128)
                nk = pc["nk"]
                c = 0
                ncols = sum(s[1] for s in pc["segs"])
                for (c0, n) in pc["segs"]:
                    load(sf[:, 0:nk, c:c + n], wv[:, pc["k0"]:pc["k0"] + nk, c0:c0 + n], sem=stgf_sem[fi_])
                    c += n
                eng = ceng[bi % 3]
                if "rs" in pc:
                    base = {"g1": G1, "g2": G2, "snw": SNW + pc["k0"]}[pc["rs"]]
                    for kc in range(nk):
                        if eng is ACT:
                            k.act(sb_[:, kc, 0:ncols], sf[:, kc, 0:ncols], AF.Identity, scale=prm[:, base + kc:base + kc + 1])
                        else:
                            k.ts(eng, sb_[:, kc, 0:ncols], sf[:, kc, 0:ncols], prm[:, base + kc:base + kc + 1])
                elif "cs" in pc:
                    gb = gate_bc[0] if pc["cs"][0] == "gate1" else gate_bc[1]
                    c0 = pc["cs"][1]
                    e2 = DVE if eng is ACT else eng
                    gv = gb.cust(c0, [[0, nk], [1, 512]], 0, 128, c0, c0 + 512)
                    k.tt(e2, sb_[:, 0:nk, :], sf[:, 0:nk, :], gv, ALU.mult)
                else:
                    k.cp(eng, sb_[:, 0:nk, :], sf[:, 0:nk, :])
            elif pc["kind"] == "c31":
                fc = pc["fc"]
                sflat = sb_.cust(0, [[1, 4096]])
                for j in range(31):
                    diag(j, sb_.cust(j * 128, [[1, 128]], 0, 128, j * 128, (j + 1) * 128), w31[:, fc, j:j + 1])
            elif pc["kind"] == "c4":
                for cl in range(4):
                    ch = pc["g"] * 4 + cl
                    for j in range(4):
                        diag(j, sb_[:, cl, j * 128:(j + 1) * 128], w4[:, ch, j:j + 1])
            elif pc["kind"] == "dsk":
                for j in range(NH):
                    diag(j, sb_.cust(j * 128, [[1, 128]], 0, 128, j * 128, (j + 1) * 128), dskb[:, j:j + 1])
            k.dma(stq, None, sb_[:, :, :], stgb_sem[bi % NSB], out_ap=wscr[pi].rearrange("p (a b) -> p a b", b=512))
            piece_done[pi] = (stgb_sem[bi % NSB], stgb_sem[bi % NSB].cnt)

        if DBG_STOP == "s2":
            raise _Stop()
        order = tile_order(ids)
        NPT = len(order)
        wstate = {"next": 0}
        total_loads = NT * NPT

        def issue_loads(upto):
            while wstate["next"] < min(upto, total_loads):
                n = wstate["next"]
                pid = ids[order[n % NPT]]
                s = n % NSLOT
                if n < NPT:
                    so_, v_ = piece_done[pid]
                    if SP.waited.get(("st", so_.id), 0) < v_:
                        SP.h.wait_ge(so_.sem, v_)
                        SP.waited[("st", so_.id)] = v_
                k.dma(SP, slots[s][:, :, :], None, slot_sem[s], in_ap=wscr[pid].rearrange("p (a b) -> p a b", b=512))
                wstate["next"] += 1

        wpos = {"n": 0}

        def W(name):
            n = wpos["n"]
            assert order[n % NPT] == name, (order[n % NPT], name)
            issue_loads(n + NSLOT - 1)
            wpos["n"] += 1
            return slots[n % NSLOT]

        def normA(src):
            for q in range(NB):
                k.act(junk[:, :], src[:, q, :], AF.Square, accum=st4[:, q:q + 1])
            k.act(st4[:, 4:8], st4[:, 0:4], AF.Ln, bias=EPS, scale=1.0 / D)
            k.act(st4[:, 4:8], st4[:, 4:8], AF.Exp, scale=-0.5)
            for q in range(2):
                k.ts(DVE, xb[q % 2][:, :], src[:, q, :], st4[:, 4 + q:5 + q])

        def normB(src, gcol, scol):
            for q in range(NB):
                if q >= 2:
                    k.ts(DVE, xb[q % 2][:, :], src[:, q, :], st4[:, 4 + q:5 + q])
                for kc in range(KC):
                    pb = psb[q % 2]
                    k.tr(pb[:, kc * 128:(kc + 1) * 128], xb[q % 2][:, kc * 128:(kc + 1) * 128], ident_b[:, :], mark=(kc == KC - 1))
                for kc in range(KC):
                    e = DVE if q % 2 == 0 else ACT
                    o = hT[:, kc, q * 128:(q + 1) * 128]
                    i = psb[q % 2][:, kc * 128:(kc + 1) * 128]
                    if e is DVE:
                        k.ts(DVE, o, i, prm[:, gcol + kc:gcol + kc + 1], prm[:, scol + kc:scol + kc + 1], ALU.mult, ALU.add)
                    else:
                        k.act(o, i, AF.Identity, bias=prm[:, scol + kc:scol + kc + 1], scale=prm[:, gcol + kc:gcol + kc + 1])

        def norm_block(src, q, gcol, scol):
            k.act(junk[:, :], src[:, q, :], AF.Square, accum=st4[:, q:q + 1])
            k.act(st4[:, 4 + q:5 + q], st4[:, q:q + 1], AF.Ln, bias=EPS, scale=1.0 / D)
            k.act(st4[:, 4 + q:5 + q], st4[:, 4 + q:5 + q], AF.Exp, scale=-0.5)
            k.ts(DVE, xb[q % 2][:, :], src[:, q, :], st4[:, 4 + q:5 + q])
            for kc in range(KC):
                k.tr(psb[q % 2][:, kc * 128:(kc + 1) * 128], xb[q % 2][:, kc * 128:(kc + 1) * 128], ident_b[:, :], mark=(kc == KC - 1))
            for kc in range(KC):
                o = hT[:, kc, q * 128:(q + 1) * 128]
                i = psb[q % 2][:, kc * 128:(kc + 1) * 128]
                if q % 2 == 0:
                    k.ts(DVE, o, i, prm[:, gcol + kc:gcol + kc + 1], prm[:, scol + kc:scol + kc + 1], ALU.mult, ALU.add)
                else:
                    k.act(o, i, AF.Identity, bias=prm[:, scol + kc:scol + kc + 1], scale=prm[:, gcol + kc:gcol + kc + 1])

        def rmsnorm_to_hT(src, gcol, scol):
            normA(src)
            normB(src, gcol, scol)

        xv = x_d.rearrange("(t q p) d -> t p q d", p=128, q=NB)
        yv = y_d.rearrange("(t q p) d -> t p q d", p=128, q=NB)

        k.dma(SP, x_in[:, :, :], None, xin_sem, in_ap=xv[0])
        for ti in range(NT):
            if ti == 0:
                rmsnorm_to_hT(x_in, G1, SH1)
            if DBG_STOP == "p1":
                break
            k.cp(POOL, uT[:, :, 0:30], uhalo[:, :, 0:30])
            for half in range(2):
                wa = W("a%d" % half)
                wb = W("b%d" % half)
                for fl in range(4):
                    fc = half * 4 + fl
                    pa = psf[(fc % 3) * 2]
                    pb_ = psf[(fc % 3) * 2 + 1]
                    for kc in range(KC):
                        k.mm(pa[:, :], wa[:, kc, fl * 128:(fl + 1) * 128], hT[:, kc, :], start=(kc == 0), stop=(kc == KC - 1))
                    for kc in range(KC):
                        k.mm(pb_[:, :], wb[:, kc, fl * 128:(fl + 1) * 128], hT[:, kc, :], start=(kc == 0), stop=(kc == KC - 1),
                             mark=(kc == KC - 1))
                    k.act(sgt[fc % 2][:, :], pb_[:, :], AF.Sigmoid)
                    k.tt(DVE, uT[:, fc, 30:30 + TT], pa[:, :], sgt[fc % 2][:, :], ALU.mult)
            k.cp(POOL, uhalo[:, :, 0:30], uT[:, :, TT:TT + 30])
            if DBG_STOP == "p2a":
                break
            pS, pQ = psf[4], psf[5]

            def ln_stats(fc_):
                k.mm(pS[:, :], ones_b[:, :], uv[:, fc_, :], start=(fc_ == 0), stop=(fc_ == 7))
                k.mm(pQ[:, :], ones_b[:, :], usq[fc_ % 2][:, :], start=(fc_ == 0), stop=(fc_ == 7), mark=True)

            for fc in range(8):
                wd = W("c31_%d" % fc)
                pu = psf[fc % 4]
                for j in range(31):
                    k.mm(pu[:, :], wd.cust(j * 128, [[1, 128]], 0, 128, j * 128, (j + 1) * 128), uT[:, fc, j:j + TT],
                         start=(j == 0), stop=(j == 30), mark=(j == 30))
                k.ts(DVE, uv[:, fc, :], pu[:, :], prm[:, CB31 + fc:CB31 + fc + 1], None, ALU.add)
                k.act(usq[fc % 2][:, :], pu[:, :], AF.Square, bias=prm[:, CB31 + fc:CB31 + fc + 1])
                if fc > 0:
                    ln_stats(fc - 1)
            ln_stats(7)
            k.ts(DVE, mean_t[:, :], pS[:, :], 1.0 / D)
            k.tt(DVE, msq_t[:, :], mean_t[:, :], mean_t[:, :], ALU.mult)
            k.stt(DVE, rstd_t[:, :], pQ[:, :], 1.0 / D, msq_t[:, :], ALU.mult, ALU.subtract)
            k.act(rstd_t[:, :], rstd_t[:, :], AF.Ln, bias=LN_EPS)
            k.act(rstd_t[:, :], rstd_t[:, :], AF.Exp, scale=-0.5)
            def zproj(zi):
                wz = W("z%d" % zi)
                for q in range(NB):
                    pz = psf[(zi * NB + q) % 6]
                    for kc in range(KC):
                        k.mm(pz[:, :], hT[:, kc, q * 128:(q + 1) * 128], wz[:, kc, :], start=(kc == 0), stop=(kc == KC - 1),
                             mark=(kc == KC - 1))
                    k.act(siluz[:, q, zi * 512:(zi + 1) * 512], pz[:, :], AF.Silu)

            zproj(0)
            zproj(1)
            for fc in range(8):
                lt = lnt[fc % 2]
                k.tt(DVE, lt[:, :], uv[:, fc, :], mean_t[:, :], ALU.subtract)
                k.tt(POOL, lt[:, :], lt[:, :], rstd_t[:, :], ALU.mult)
                k.act(uv[:, fc, :], lt[:, :], AF.Silu, bias=prm[:, LNB + fc:LNB + fc + 1], scale=prm[:, LNW + fc:LNW + fc + 1])
            zproj(2)
            zproj(3)
            if DBG_STOP == "p2b":
                break
            for half in range(2):
                wc = W("co%d" % half)
                wg = W("gc%d" % half)
                for fl in range(4):
                    fo = half * 4 + fl
                    py = psf[(fo % 3) * 2]
                    pg = psf[(fo % 3) * 2 + 1]
                    for c in range(KC):
                        k.mm(py[:, :], wc[:, c, fl * 128:(fl + 1) * 128], uv[:, c, :], start=(c == 0), stop=(c == KC - 1))
                    for kc in range(KC):
                        k.mm(pg[:, :], wg[:, kc, fl * 128:(fl + 1) * 128], hT[:, kc, :], start=(kc == 0), stop=(kc == KC - 1),
                             mark=(kc == KC - 1))
                    k.act(sgt[fo % 2][:, :], pg[:, :], AF.Sigmoid)
                    k.tt(DVE, t1[:, fo, :], py[:, :], sgt[fo % 2][:, :], ALU.mult)
            if DBG_STOP == "p2c":
                break
            xw_ = {}

            def xproj(ch):
                if ch % 4 == 0:
                    xw_["c4"] = W("c4_%d" % (ch // 4))
                    if ch < 16:
                        xw_["x"] = W("xs%d" % (ch // 4))
                    elif ch == 16:
                        xw_["x"] = W("B")
                    else:
                        xw_["x"] = W("C")
                wx = xw_["x"]
                fl = ch % 4
                px = psf[ch % 2]
                for kc in range(KC):
                    k.mm(px[:, :], wx[:, kc, fl * 128:(fl + 1) * 128], hT[:, kc, :], start=(kc == 0), stop=(kc == KC - 1),
                         mark=(kc == KC - 1))
                st = xst[ch % 3]
                k.cp(POOL, st[:, 0:3], xhalo[:, ch, 0:3])
                if ch % 2 == 0:
                    k.cp(DVE, st[:, 3:3 + TT], px[:, :])
                else:
                    k.cp(ACT, st[:, 3:3 + TT], px[:, :])
                k.cp(POOL, xhalo[:, ch, 0:3], st[:, TT:TT + 3])

            def xconv(ch):
                wc4 = xw_["c4"]
                fl = ch % 4
                cl = ch % 4
                st = xst[ch % 3]
                if ch < 20:
                    for q in range(NB):
                        pt = psf[2 + q]
                        for j in range(4):
                            k.mm(pt[:, fl * 128:(fl + 1) * 128], st[:, q * 128 + j:q * 128 + j + 128],
                                 wc4[:, cl, j * 128:(j + 1) * 128], start=(j == 0), stop=False)
                        k.mm(pt[:, fl * 128:(fl + 1) * 128], sel[0:32, ch * 128:(ch + 1) * 128], cb4mat[0:32, :], start=False,
                             stop=True, mark=(fl == 3))
                    if fl == 3:
                        for q in range(NB):
                            if ch < 16:
                                k.act(xs[:, q, (ch // 4) * 512:(ch // 4 + 1) * 512], psf[2 + q][:, :], AF.Silu)
                            else:
                                k.act(Btok[:, q, :], psf[2 + q][:, :], AF.Silu)
                if ch >= 16:
                    pf = psf[ch % 2]
                    for j in range(4):
                        k.mm(pf[:, :], wc4[:, cl, j * 128:(j + 1) * 128], st[:, j:j + TT], start=(j == 0), stop=(j == 3),
                             mark=(j == 3))
                    dst = BT if ch < 20 else CT
                    k.act(dst[:, ch % 4, :], pf[:, :], AF.Silu, bias=prm[:, CB4 + ch:CB4 + ch + 1])

            for ch in range(24):
                if ch % 4 == 0:
                    xproj(ch)
                if ch % 4 != 3:
                    xproj(ch + 1)
                xconv(ch)
            if DBG_STOP == "p2d":
                break
            wdtp = W("dt")
            pD, pA, pAT, pTot = psf[0], psf[1], psf[2], psf[3]
            for q in range(NB):
                for kc in range(KC):
                    k.mm(pD[:, q * 32:(q + 1) * 32], hT[:, kc, q * 128:(q + 1) * 128], wdtp[:, kc, 0:32], start=(kc == 0),
                         stop=(kc == KC - 1), mark=(kc == KC - 1 and q == NB - 1))
            k.tt(DVE, dtp[:, :], pD[:, 0:128], dtb[:, :], ALU.add)
            k.act(dtt[:, :], dtp[:, :], AF.Exp)
            k.act(dtt[:, :], dtt[:, :], AF.Ln, bias=1.0)
            k.act(lndt[:, :], dtt[:, :], AF.Ln)
            for q in range(NB):
                for r in range(2):
                    k.tt(DVE, da2[:, q, r * 32:(r + 1) * 32], dtt[:, q * 32:(q + 1) * 32], a_bc[:, q * 32:(q + 1) * 32], ALU.mult)
            for q in range(NB):
                k.mm(pA[:, q * 32:(q + 1) * 32], tri_f[:, :], da2[:, q, 0:32], start=True, stop=True)
                k.mm(pTot[:, q * 32:(q + 1) * 32], ones_f[:, :], da2[:, q, 0:32], start=True, stop=True)
                k.mm(pAT[0:64, q * 128:(q + 1) * 128], da2[:, q, :], tri_f[:, :], start=True, stop=True, mark=(q == NB - 1))
            k.tt(DVE, biasA[:, :], lndt[:, :], pA[:, 0:128], ALU.subtract)
            k.act(Ee[:, :], pA[:, 0:128], AF.Exp)
            k.tt(DVE, tmpd[:, :], pTot[:, 0:128], biasA[:, :], ALU.add)
            k.act(wdt[:, :], tmpd[:, :], AF.Exp)
            k.act(decA[:, :], pTot[:, 0:128], AF.Exp)
            k.cp(DVE, acsT[0:32, :], pAT[0:32, :])
            k.cp(DVE, acsTt[32:64, :], pAT[32:64, :])
            k.tt(DVE, acsT[32:64, :], pAT[32:64, :], acsTt[32:64, :], ALU.subtract)
            if DBG_STOP == "p2e":
                break
            if DBG_STOP == "p2f":
                break
            wdsk = W("dsk")

            def ssd_front(si, q, g):
                if g == 0:
                    pC = psf[0]
                    for g2 in range(NG):
                        k.mm(pC[:, g2 * 128:(g2 + 1) * 128], BT[:, g2, q * 128:(q + 1) * 128], CT[:, g2, q * 128:(q + 1) * 128],
                             start=True, stop=True, mark=(g2 == NG - 1))
                    k.tt(DVE, cbm.cust(0, [[1, 512]]), pC[:, :], caus[:, :], ALU.mult)
                ex = expo[si % 3]
                for hh in range(2):
                    pR = psf[1 + hh]
                    k.mm(pR[:, :], ident_b[:, :], negm[:, :], start=True, stop=False)
                    for jl in range(4):
                        j = g * 8 + hh * 4 + jl
                        k.mm(pR[:, jl * 128:(jl + 1) * 128], sel[0:64, j * 128:(j + 1) * 128], acsT[0:64, q * 128:(q + 1) * 128],
                             start=False, stop=(jl == 3), mark=(jl == 3))
                    for jl in range(4):
                        j = g * 8 + hh * 4 + jl
                        k.act(ex[:, hh * 4 + jl, :], pR[:, jl * 128:(jl + 1) * 128], AF.Exp,
                              bias=biasA[:, q * 32 + j:q * 32 + j + 1])

            def ssd_front_b(si, q, g):
                ex = expo[si % 3]
                cbv = cbm.cust(g * 128, [[0, 8], [1, 128]], 0, 128, g * 128, (g + 1) * 128)
                k.tt(POOL, ex[:, :, :], ex[:, :, :], cbv, ALU.mult)
                wv_ = wdt.cust(q * 32 + g * 8, [[1, 8], [0, 64]], 0, 128, q * 32 + g * 8, q * 32 + g * 8 + 8)
                k.tt(POOL, xw[si % 3].cust(0, [[64, 8], [1, 64]]), xs.cust(q * DIN + g * 512, [[64, 8], [1, 64]], 0, 128,
                                                                     q * DIN + g * 512, q * DIN + (g + 1) * 512), wv_, ALU.mult)

            def ssd_back(si, q, g):
                ex = expo[si % 3]
                pY, pO, pDl = psf[3], psf[4], psf[5]
                k.mm(pO[:, :], CT[:, g, q * 128:(q + 1) * 128], Sb[:, g * 512:(g + 1) * 512], start=True, stop=True, mark=True)
                k.mm(pDl[:, :], Btok[:, q, g * 128:(g + 1) * 128], xw[si % 3][:, :], start=True, stop=True, mark=True)
                for jl in range(8):
                    j = g * 8 + jl
                    xsl = xs[:, q, j * 64:(j + 1) * 64]
                    k.mm(pY[:, jl * 64:(jl + 1) * 64], ex[:, jl, :], xsl, start=True, stop=False)
                    k.mm(pY[:, jl * 64:(jl + 1) * 64], wdsk.cust(j * 128, [[1, 128]], 0, 128, j * 128, (j + 1) * 128), xsl,
                         start=False, stop=True, mark=(jl == 7))
                ev = Ee.cust(q * 32 + g * 8, [[1, 8], [0, 64]], 0, 128, q * 32 + g * 8, q * 32 + g * 8 + 8)
                ot = otmp[g % 2]
                ycg = yc[:, (g % 2) * 512:(g % 2 + 1) * 512]
                k.tt(DVE, ot.cust(0, [[64, 8], [1, 64]]), pO.cust(0, [[64, 8], [1, 64]]), ev, ALU.mult)
                k.tt(DVE, ycg, pY[:, :], ot[:, :], ALU.add)
                k.tt(DVE, ycg, ycg, siluz[:, q, g * 512:(g + 1) * 512], ALU.mult)
                k.tt(DVE, S[:, g * 512:(g + 1) * 512], S[:, g * 512:(g + 1) * 512], pDl[:, :], ALU.add)
                k.cp(DVE, Sb[:, g * 512:(g + 1) * 512], S[:, g * 512:(g + 1) * 512])

            def ssd_back2(si, q, g):
                ycg = yc[:, (g % 2) * 512:(g % 2 + 1) * 512]
                k.act(junk[:, 0:512], ycg, AF.Square, accum=st4[:, 8 + g:9 + g])
                k.act(st4[:, 12 + g:13 + g], st4[:, 8 + g:9 + g], AF.Ln, bias=EPS, scale=1.0 / 512)
                k.act(st4[:, 12 + g:13 + g], st4[:, 12 + g:13 + g], AF.Exp, scale=-0.5)
                yng = yn[:, (g % 2) * 512:(g % 2 + 1) * 512]
                k.ts(DVE, yng, ycg, st4[:, 12 + g:13 + g])

            def ssd_back3(si, q, g):
                pb = psb[g % 2]
                for c4 in range(4):
                    k.tr(pb[:, c4 * 128:(c4 + 1) * 128], yn[:, (g % 2) * 512 + c4 * 128:(g % 2) * 512 + (c4 + 1) * 128], ident_b[:, :],
                         mark=(c4 == 3))
                dstv = ynT.cust(g * 4 * TT + q * 128, [[TT, 4], [1, 128]], 0, 128, g * 4 * TT, (g + 1) * 4 * TT)
                srcv = pb.cust(0, [[128, 4], [1, 128]])
                k.cp(DVE, dstv, srcv)

            def ssd_back_pool(si, q, g):
                dv = decA.cust(q * 32 + g * 8, [[1, 8], [0, 64]], 0, 128, q * 32 + g * 8, q * 32 + g * 8 + 8)
                Sg = S.cust(g * 512, [[64, 8], [1, 64]], 0, 128, g * 512, (g + 1) * 512)
                k.tt(POOL, Sg, Sg, dv, ALU.mult)

            steps = [(q, g) for q in range(NB) for g in range(NG)]
            for s0 in range(2):
                ssd_front(s0, *steps[s0])
                ssd_front_b(s0, *steps[s0])
            for si, (q, g) in enumerate(steps):
                if si + 2 < len(steps):
                    ssd_front(si + 2, *steps[si + 2])
                ssd_back_pool(si, q, g)
                if si + 2 < len(steps):
                    ssd_front_b(si + 2, *steps[si + 2])
                ssd_back(si, q, g)
                if si >= 1:
                    ssd_back2(si - 1, *steps[si - 1])
                if si >= 2:
                    ssd_back3(si - 2, *steps[si - 2])
            nS = len(steps)
            ssd_back2(nS - 1, *steps[nS - 1])
            ssd_back3(nS - 2, *steps[nS - 2])
            ssd_back3(nS - 1, *steps[nS - 1])
            if DBG_STOP == "p3":
                break
            k.dma(SP, x_res[:, :, :], None, xres_sem, in_ap=xv[ti])
            for cg in range(2):
                for kh in range(2):
                    ws = W("so%d%d" % (cg, kh))
                    for fl in range(4):
                        for c8 in range(8):
                            k.mm(psf[fl][:, :], ws[:, c8, fl * 128:(fl + 1) * 128], ynT[:, kh * 8 + c8, :],
                                 start=(kh == 0 and c8 == 0), stop=(kh == 1 and c8 == 7), mark=(kh == 1 and c8 == 7))
                wg = W("gs%d" % cg)
                for fl in range(4):
                    fo = cg * 4 + fl
                    pg = psf[4 + fl % 2]
                    for kc in range(KC):
                        k.mm(pg[:, :], wg[:, kc, fl * 128:(fl + 1) * 128], hT[:, kc, :], start=(kc == 0), stop=(kc == KC - 1),
                             mark=(kc == KC - 1))
                    k.act(sl[fl % 2][:, :], pg[:, :], AF.Sigmoid)
                    k.tt(DVE, sl[fl % 2][:, :], psf[fl][:, :], sl[fl % 2][:, :], ALU.mult)
                    k.tt(ew(), mg[:, fo, :], sl[fl % 2][:, :], t1[:, fo, :], ALU.add)
            wos = [W("wo0"), W("wo1")]
            for q in range(NB):
                for pc_ in range(2):
                    wo = wos[pc_]
                    pw = psf[(q * 2 + pc_) % 6]
                    for kc in range(KC):
                        k.mm(pw[:, :], mg[:, kc, q * 128:(q + 1) * 128], wo[:, kc, :], start=(kc == 0), stop=(kc == KC - 1),
                             mark=(kc == KC - 1))
                    k.tt(DVE, x_res[:, q, pc_ * 512:(pc_ + 1) * 512], x_res[:, q, pc_ * 512:(pc_ + 1) * 512], pw[:, :], ALU.add)
                norm_block(x_res, q, G2, SH2)
            if ti + 1 < NT:
                k.dma(SP, x_in[:, :, :], None, xin_sem, in_ap=xv[ti + 1])
            if DBG_STOP == "p4":
                break
            for m in range(11):
                wf = W("fi%d" % m)
                for il in range(2):
                    i = 2 * m + il
                    pgt = psf[(i % 3) * 2]
                    pup = psf[(i % 3) * 2 + 1]
                    for kc in range(KC):
                        k.mm(pgt[:, :], wf[:, kc, il * 128:(il + 1) * 128], hT[:, kc, :], start=(kc == 0), stop=(kc == KC - 1),
                             mark=(kc == KC - 1))
                    for kc in range(KC):
                        k.mm(pup[:, :], wf[:, kc, 256 + il * 128:256 + (il + 1) * 128], hT[:, kc, :], start=(kc == 0),
                             stop=(kc == KC - 1), mark=(kc == KC - 1))
                    k.act(sl[i % 2][:, :], pgt[:, :], AF.Silu)
                    k.tt(DVE, actT[:, i, :], sl[i % 2][:, :], pup[:, :], ALU.mult)
            nxt = (ti + 1 < NT) and DBG_STOP is None
            if nxt:
                normA(x_in)
            for cg in range(2):
                if cg == 1 and nxt:
                    normB(x_in, G1, SH1)
                for ks in range(3):
                    wd_ = W("fd%d%d" % (cg, ks))
                    nk = 8 if ks < 2 else 6
                    for q in range(NB):
                        for c8 in range(nk):
                            c = ks * 8 + c8
                            k.mm(psf[q][:, :], actT[:, c, q * 128:(q + 1) * 128], wd_[:, c8, :], start=(c == 0), stop=(c == NFF - 1),
                                 mark=(c == NFF - 1))
                for q in range(NB):
                    k.tt(DVE, x_res[:, q, cg * 512:(cg + 1) * 512], x_res[:, q, cg * 512:(cg + 1) * 512], psf[q][:, :], ALU.add)
            if DBG_STOP == "p5":
                break
            for q in range(NB):
                k.act(junk[:, :], x_res[:, q, :], AF.Square, accum=st4[:, q:q + 1])
            k.act(st4[:, 4:8], st4[:, 0:4], AF.Ln, bias=EPS, scale=1.0 / D)
            k.act(st4[:, 4:8], st4[:, 4:8], AF.Exp, scale=-0.5)
            for q in range(NB):
                k.stt(DVE, x_res[:, q, :], x_res[:, q, :], st4[:, 4 + q:5 + q], fw_bc[:, :], ALU.mult, ALU.mult)
            k.dma(SP, None, x_res[:, :, :], y_sem, out_ap=yv[ti])
        SP.h.wait_ge(y_sem.sem, y_sem.cnt)
      except _Stop:
        pass
    return nc


def host_consts():
    ident = np.eye(128, dtype=np.float32)
    s = np.arange(128)[:, None]
    l = np.arange(128)[None, :]
    tri = (s <= l).astype(np.float32)
    neg1 = np.where(l < s, -30000.0, 0.0).astype(np.float32)
    caus1 = (l >= s).astype(np.float32)
    negmask4 = np.tile(neg1, (1, 4))
    causal4 = np.tile(caus1, (1, 4))
    sel = np.zeros((64, 32, 128), np.float32)
    for j in range(32):
        sel[j, j, :] = 1.0
        sel[32 + j, j, :] = 1.0
    return {"ident": ident, "tri": tri, "negmask4": negmask4, "causal4": causal4, "sel": sel.reshape(64, 4096)}


def host_layout(inp, b, Lc):
    f = lambda a: np.ascontiguousarray(np.asarray(a, dtype=np.float32))
    pT = lambda v, n: f(np.asarray(v).reshape(n, 128).T)
    m = {}
    m["x"] = f(inp["x"][b, :Lc])
    m["cT"] = pT(inp["c"][b], 8)
    m["w_ada_mix"] = f(inp["w_ada_mix"][0])
    m["w_ada_ffn"] = f(inp["w_ada_ffn"][0])
    m["bT_ada_mix"] = pT(inp["b_ada_mix"][0], 24)
    m["bT_ada_ffn"] = pT(inp["b_ada_ffn"][0], 24)
    m["brow_ada_mix"] = f(np.asarray(inp["b_ada_mix"][0]).reshape(1, 3072))
    m["brow_ada_ffn"] = f(np.asarray(inp["b_ada_ffn"][0]).reshape(1, 3072))
    m["nw1T"] = pT(inp["norm_mix_w"][0], 8)
    m["nw2T"] = pT(inp["norm_ffn_w"][0], 8)
    m["w_in"] = f(inp["w_in"][0])
    m["w31T"] = f(np.asarray(inp["conv_dw_w"][0]).reshape(31, 8, 128).transpose(2, 1, 0))
    m["cb31T"] = pT(inp["conv_dw_b"][0], 8)
    m["lnwT"] = pT(inp["conv_ln_w"][0], 8)
    m["lnbT"] = pT(inp["conv_ln_b"][0], 8)
    m["w_conv_out"] = f(inp["w_conv_out"][0])
    m["w4T"] = f(np.asarray(inp["ssm_conv_w"][0]).reshape(4, 24, 128).transpose(2, 1, 0))
    m["cb4T"] = pT(inp["ssm_conv_b"][0], 24)
    cbm_ = np.zeros((32, 128), np.float32)
    cbm_[:24] = np.asarray(inp["ssm_conv_b"][0]).reshape(24, 128)
    m["cb4mat"] = cbm_
    m["dtb_bc"] = f(np.tile(np.asarray(inp["dt_bias"][0]).reshape(1, 32), (128, 4)))
    m["alog_bc"] = f(np.tile(np.asarray(inp["a_log"][0]).reshape(1, 32), (128, 4)))
    m["dsk_bc"] = f(np.tile(np.asarray(inp["d_skip"][0]).reshape(1, 32), (128, 1)))
    m["snwT"] = pT(inp["ssm_norm_w"][0], 16)
    m["w_ssm_out"] = f(inp["w_ssm_out"][0])
    m["w_out"] = f(inp["w_out"][0])
    m["w_ffn_in"] = f(inp["w_ffn_in"][0])
    m["w_ffn_down"] = f(inp["w_ffn_down"][0])
    m["fw_bc"] = f(np.tile(np.asarray(inp["final_norm_w"]).reshape(1, D), (128, 1)))
    m.update(host_consts())
    return m


def kernel(**inputs):
    nb = inputs["x"].shape[0]
    Lc = inputs["x"].shape[1]
    NT = Lc // TT
    nc = build(NT)
    in_maps = [host_layout(inputs, b, Lc) for b in range(nb)]
    res = run_bass_kernel_spmd(nc, in_maps, core_ids=list(range(nb)))
    out = np.stack([np.asarray(r["y"], dtype=np.float32) for r in res.results], axis=0)
    return out
```

```python
import contextlib
import numpy as np
import concourse.bass as bass
import concourse.mybir as mybir
from concourse.bass_utils import run_bass_kernel_spmd

F32 = mybir.dt.float32
BF16 = mybir.dt.bfloat16
AF = mybir.ActivationFunctionType
ALU = mybir.AluOpType

D = 1024
KC = 8
TT = 512
NB = 4
SEQ = 8192
DIN = 2048
NH = 32
HD = 64
NG = 4
NST = 128
DFF = 2816
NFF = 22
INC = 9248
NSLOT = 4
SAME_ENG_SYNC = True
DBG_STOP = None
EPS = 1e-6
LN_EPS = 1e-5


class _Stop(Exception):
    pass


class SemObj:
    def __init__(self, sem, owner, sid):
        self.sem = sem
        self.owner = owner
        self.id = sid
        self.cnt = 0


class Eng:
    def __init__(self, name, h, so):
        self.name = name
        self.h = h
        self.so = so
        so.owner = self
        self.waited = {}
        self.last = None


class V:
    __slots__ = ("ap", "sp", "lo", "hi")

    def __init__(self, ap, sp, lo, hi):
        self.ap = ap
        self.sp = sp
        self.lo = lo
        self.hi = hi


class Buf:
    def __init__(self, K, name, shape, dt, at=None):
        self.esz = 4 if dt == F32 else 2
        nfree = int(np.prod(shape[1:]))
        self.nbytes = nfree * self.esz
        if at is None:
            at = K.sb_alloc(self.nbytes)
        self.off = at
        self.shape = list(shape)
        self.dt = dt
        self.nfree = nfree
        self.t = K.nc.alloc_sbuf_tensor_at(name, list(shape), dt, offset=at)
        st = []
        acc = 1
        for d in reversed(shape[1:]):
            st.append(acc)
            acc *= d
        self.strides = list(reversed(st))

    def __getitem__(self, idx):
        if not isinstance(idx, tuple):
            idx = (idx,)
        ap = self.t[idx]
        lo = 0
        hi = 0
        fr = idx[1:]
        for i, d in enumerate(self.shape[1:]):
            if i < len(fr):
                x = fr[i]
                if isinstance(x, int):
                    a, b = x, x + 1
                else:
                    a = 0 if x.start is None else x.start
                    b = d if x.stop is None else x.stop
            else:
                a, b = 0, d
            lo += a * self.strides[i]
            hi += (b - 1) * self.strides[i]
        hi += 1
        return V(ap, "sb", self.off + lo * self.esz, self.off + hi * self.esz)

    def cust(self, off, dims, p0=0, npart=128, lo=None, hi=None):
        ap = bass.AP(self.t, p0 * self.nfree + off, [[self.nfree, npart]] + [list(d) for d in dims])
        if lo is None:
            lo = 0
            hi = self.nfree
        return V(ap, "sb", self.off + lo * self.esz, self.off + hi * self.esz)


class PsBank:
    def __init__(self, K, name, idx, dt):
        n = 512 if dt == F32 else 1024
        self.t = K.nc.alloc_psum_tensor(name, [128, n], dt)
        self.idx = idx
        self.n = n

    def __getitem__(self, idx):
        return V(self.t[idx], "ps", self.idx, self.idx + 1)

    def cust(self, off, dims, p0=0, npart=128):
        ap = bass.AP(self.t, p0 * self.n + off, [[self.n, npart]] + [list(d) for d in dims])
        return V(ap, "ps", self.idx, self.idx + 1)


class K:
    def __init__(self, nc, es):
        self.nc = nc
        self.es = es
        self.sb_top = 16640
        self.nsem = 0
        self.recs = {"sb": [], "ps": []}
        self.PE = Eng("pe", nc.tensor, self.newsem("pe"))
        self.ACT = Eng("act", nc.scalar, self.newsem("act"))
        self.DVE = Eng("dve", nc.vector, self.newsem("dve"))
        self.POOL = Eng("pool", nc.gpsimd, self.newsem("pool"))
        self.SP = Eng("sp", nc.sync, self.newsem("sp"))

    def newsem(self, name):
        s = self.es.enter_context(self.nc.semaphore(name + "_%d" % self.nsem))
        so = SemObj(s, None, self.nsem)
        self.nsem += 1
        return so

    def sb_alloc(self, nbytes):
        a = self.sb_top
        self.sb_top += (nbytes + 63) // 64 * 64
        return a

    def _need(self, eng, so, val):
        if so.owner is eng and not (SAME_ENG_SYNC and eng is not self.PE):
            return
        if so.owner is not None and so.cnt < val:
            o = so.owner
            o.last.then_inc(so.sem, 1)
            so.cnt += 1
            assert so.cnt >= val, (o.name, so.cnt, val)
        if eng.waited.get(so.id, 0) < val:
            eng.h.wait_ge(so.sem, val)
            eng.waited[so.id] = val

    def _sync(self, eng, reads, writes):
        needs = {}
        for v in reads:
            if v is None or v.sp is None:
                continue
            for r in self.recs[v.sp]:
                if (r[2] == "W" or (v.sp == "ps" and r[3].owner is not eng)) and r[0] < v.hi and v.lo < r[1]:
                    if needs.get(r[3], 0) < r[4]:
                        needs[r[3]] = r[4]
        for v in writes:
            if v is None or v.sp is None:
                continue
            for r in self.recs[v.sp]:
                if r[0] < v.hi and v.lo < r[1]:
                    if needs.get(r[3], 0) < r[4]:
                        needs[r[3]] = r[4]
        for so, val in needs.items():
            self._need(eng, so, val)

    def _record(self, reads, writes, so, val):
        for v in writes:
            if v is None or v.sp is None:
                continue
            L = self.recs[v.sp]
            L[:] = [r for r in L if not (v.lo <= r[0] and r[1] <= v.hi)]
            L.append([v.lo, v.hi, "W", so, val])
        for v in reads:
            if v is None or v.sp is None:
                continue
            L = self.recs[v.sp]
            L[:] = [r for r in L if not (r[2] == "R" and r[3] is so and v.lo <= r[0] and r[1] <= v.hi)]
            L.append([v.lo, v.hi, "R", so, val])

    def emit(self, eng, fn, reads, writes, mark=True):
        self._sync(eng, reads, writes)
        ins = fn()
        so = eng.so
        if mark:
            ins.then_inc(so.sem, 1)
            so.cnt += 1
            val = so.cnt
        else:
            val = so.cnt + 1
        eng.last = ins
        self._record(reads, writes, so, val)
        return ins

    def dma(self, q, out, in_, dsem, out_ap=None, in_ap=None):
        self._sync(q, [in_], [out])
        oa = out.ap if out is not None else out_ap
        ia = in_.ap if in_ is not None else in_ap
        ins = q.h.dma_start(out=oa, in_=ia)
        ins.then_inc(dsem.sem, 16)
        dsem.cnt += 16
        self._record([in_], [out], dsem, dsem.cnt)
        return ins

    def mm(self, out, lhsT, rhs, start=True, stop=True, mark=False):
        return self.emit(self.PE, lambda: self.nc.tensor.matmul(out.ap, lhsT.ap, rhs.ap, start=start, stop=stop),
                         [lhsT, rhs], [out], mark=mark)

    def tr(self, out, in_, ident, mark=False):
        return self.emit(self.PE, lambda: self.nc.tensor.transpose(out.ap, in_.ap, ident.ap),
                         [in_, ident], [out], mark=mark)

    def act(self, out, in_, func, bias=None, scale=1.0, accum=None):
        rd = [in_]
        kw = {}
        if isinstance(bias, V):
            rd.append(bias)
            kw["bias"] = bias.ap
        elif bias is not None:
            kw["bias"] = float(bias)
        if isinstance(scale, V):
            rd.append(scale)
            kw["scale"] = scale.ap
        else:
            kw["scale"] = float(scale)
        wr = [out]
        if accum is not None:
            wr.append(accum)
            kw["accum_out"] = accum.ap
        return self.emit(self.ACT, lambda: self.nc.scalar.activation(out=out.ap, in_=in_.ap, func=func, **kw), rd, wr)

    def ts(self, eng, out, in_, s1, s2=None, op0=ALU.mult, op1=None):
        rd = [in_]
        a1 = s1
        a2 = s2
        if isinstance(s1, V):
            rd.append(s1)
            a1 = s1.ap
        if isinstance(s2, V):
            rd.append(s2)
            a2 = s2.ap
        if op1 is None:
            fn = lambda: eng.h.tensor_scalar(out.ap, in_.ap, a1, None, op0)
        else:
            fn = lambda: eng.h.tensor_scalar(out.ap, in_.ap, a1, a2, op0, op1)
        return self.emit(eng, fn, rd, [out])

    def tt(self, eng, out, a, b, op):
        return self.emit(eng, lambda: eng.h.tensor_tensor(out.ap, a.ap, b.ap, op), [a, b], [out])

    def stt(self, eng, out, in0, sc, in1, op0, op1):
        rd = [in0, in1]
        a = sc
        if isinstance(sc, V):
            rd.append(sc)
            a = sc.ap
        return self.emit(eng, lambda: eng.h.scalar_tensor_tensor(out.ap, in0.ap, a, in1.ap, op0, op1), rd, [out])

    def cp(self, eng, out, in_):
        if eng is self.ACT:
            return self.emit(eng, lambda: self.nc.scalar.copy(out.ap, in_.ap), [in_], [out])
        return self.emit(eng, lambda: eng.h.tensor_copy(out.ap, in_.ap), [in_], [out])

    def memset(self, eng, out, val):
        return self.emit(eng, lambda: eng.h.memset(out.ap, val), [], [out])


def piece_table():
    P = []

    def add(kind, **kw):
        kw["kind"] = kind
        P.append(kw)
        return len(P) - 1

    ids = {}
    c_a, c_b, c_z, c_xs, c_B, c_C, c_dt, c_gc, c_gs = 0, 1024, 2048, 4096, 6144, 6656, 7168, 7200, 8224
    for i in range(2):
        ids["a%d" % i] = add("w", w="w_in", k0=0, nk=8, segs=[(c_a + 512 * i, 512)])
        ids["b%d" % i] = add("w", w="w_in", k0=0, nk=8, segs=[(c_b + 512 * i, 512)])
    for i in range(8):
        ids["c31_%d" % i] = add("c31", fc=i)
    for i in range(2):
        ids["co%d" % i] = add("w", w="w_conv_out", k0=0, nk=8, segs=[(512 * i, 512)])
        ids["gc%d" % i] = add("w", w="w_in", k0=0, nk=8, segs=[(c_gc + 512 * i, 512)])
    for i in range(6):
        ids["c4_%d" % i] = add("c4", g=i)
    for i in range(4):
        ids["xs%d" % i] = add("w", w="w_in", k0=0, nk=8, segs=[(c_xs + 512 * i, 512)])
    ids["B"] = add("w", w="w_in", k0=0, nk=8, segs=[(c_B, 512)])
    ids["C"] = add("w", w="w_in", k0=0, nk=8, segs=[(c_C, 512)])
    ids["dt"] = add("w", w="w_in", k0=0, nk=8, segs=[(c_dt, 32)])
    for i in range(4):
        ids["z%d" % i] = add("w", w="w_in", k0=0, nk=8, segs=[(c_z + 512 * i, 512)])
    ids["dsk"] = add("dsk")
    for cg in range(2):
        for kh in range(2):
            ids["so%d%d" % (cg, kh)] = add("w", w="w_ssm_out", k0=8 * kh, nk=8, segs=[(512 * cg, 512)], rs="snw")
        ids["gs%d" % cg] = add("w", w="w_in", k0=0, nk=8, segs=[(c_gs + 512 * cg, 512)])
    for i in range(2):
        ids["wo%d" % i] = add("w", w="w_out", k0=0, nk=8, segs=[(512 * i, 512)], cs=("gate1", 512 * i))
    for m in range(11):
        ids["fi%d" % m] = add("w", w="w_ffn_in", k0=0, nk=8, segs=[(256 * m, 256), (DFF + 256 * m, 256)])
    for cg in range(2):
        for ks in range(3):
            nk = 8 if ks < 2 else 6
            ids["fd%d%d" % (cg, ks)] = add("w", w="w_ffn_down", k0=8 * ks, nk=nk, segs=[(512 * cg, 512)],
                                          cs=("gate2", 512 * cg))
    return P, ids


def tile_order(ids):
    o = ["a0", "b0", "a1", "b1"] + ["c31_%d" % i for i in range(8)] + ["z0", "z1", "z2", "z3", "co0", "gc0", "co1", "gc1"]
    o += ["c4_0", "xs0", "c4_1", "xs1", "c4_2", "xs2", "c4_3", "xs3", "c4_4", "B", "c4_5", "C", "dt", "dsk"]
    o += ["so00", "so01", "gs0", "so10", "so11", "gs1", "wo0", "wo1"]
    o += ["fi%d" % m for m in range(11)]
    o += ["fd00", "fd01", "fd02", "fd10", "fd11", "fd12"]
    return o


def build(NT):
    nc = bass.Bass("TRN2", target_bir_lowering=False)
    Lc = NT * TT
    dram = {}

    def din(name, shape):
        dram[name] = nc.dram_tensor(name, list(shape), F32, kind="ExternalInput").ap()
        return dram[name]

    x_d = din("x", [Lc, D])
    cT_d = din("cT", [128, 8])
    din("w_ada_mix", [D, 3072])
    din("w_ada_ffn", [D, 3072])
    din("bT_ada_mix", [128, 24])
    din("bT_ada_ffn", [128, 24])
    din("brow_ada_mix", [1, 3072])
    din("brow_ada_ffn", [1, 3072])
    din("nw1T", [128, 8])
    din("nw2T", [128, 8])
    din("w_in", [D, INC])
    din("w31T", [128, 8, 31])
    din("cb31T", [128, 8])
    din("lnwT", [128, 8])
    din("lnbT", [128, 8])
    din("w_conv_out", [D, D])
    din("w4T", [128, 24, 4])
    din("cb4T", [128, 24])
    din("cb4mat", [32, 128])
    din("dtb_bc", [128, 128])
    din("alog_bc", [128, 128])
    din("dsk_bc", [128, 32])
    din("snwT", [128, 16])
    din("w_ssm_out", [DIN, D])
    din("w_out", [D, D])
    din("w_ffn_in", [D, 2 * DFF])
    din("w_ffn_down", [DFF, D])
    din("fw_bc", [128, D])
    din("ident", [128, 128])
    din("tri", [128, 128])
    din("negmask4", [128, 512])
    din("causal4", [128, 512])
    din("sel", [64, 4096])
    y_d = nc.dram_tensor("y", [Lc, D], F32, kind="ExternalOutput").ap()

    pieces, ids = piece_table()
    NP = len(pieces)
    wscr = nc.dram_tensor("wscr", [NP, 128, 4096], BF16).ap()

    es = contextlib.ExitStack()
    with es:
      try:
        k = K(nc, es)
        PE, ACT, DVE, POOL, SP = k.PE, k.ACT, k.DVE, k.POOL, k.SP
        EW = [DVE, POOL]
        rr = [0]

        def ew():
            rr[0] += 1
            return EW[rr[0] % 2]

        def B(name, shape, dt, at=None):
            return Buf(k, name, shape, dt, at)

        ident_f = B("ident_f", [128, 128], F32)
        ident_b = B("ident_b", [128, 128], BF16)
        tri_f = B("tri_f", [128, 128], F32)
        ones_f = B("ones_f", [128, 128], F32)
        ones_b = B("ones_b", [128, 128], BF16)
        negm = B("negm", [128, 512], BF16)
        caus = B("caus", [128, 512], BF16)
        sel = B("sel", [64, 4096], BF16)
        fw_bc = B("fw_bc", [128, D], F32)
        dtb = B("dtb", [128, 128], F32)
        a_bc = B("a_bc", [128, 128], F32)
        cb4mat = B("cb4mat", [32, 128], BF16)
        prm = B("prm", [128, 160], F32)
        G1, SH1, G2, SH2, CB31, LNW, LNB, CB4, SNW = 0, 8, 16, 24, 32, 40, 48, 56, 80
        x_res = B("x_res", [128, NB, D], F32)
        hT = B("hT", [128, KC, TT], BF16)
        S = B("S", [128, DIN], F32)
        Sb = B("Sb", [128, DIN], BF16)
        uhalo = B("uhalo", [128, 8, 32], BF16)
        xhalo = B("xhalo", [128, 24, 4], BF16)
        slots = [B("slot%d" % i, [128, 8, 512], BF16) for i in range(NSLOT)]
        slot_sem = [k.newsem("slot") for _ in range(NSLOT)]
        arB = k.sb_alloc(33 * 1024)
        x_in = B("x_in", [128, NB, D], F32, at=arB)
        uT = B("uT", [128, 8, 544], BF16, at=arB)
        uv = B("uv", [128, 8, TT], BF16, at=arB + 8704)
        sgt = [B("sgt%d" % i, [128, TT], F32, at=arB + 16896 + 2048 * i) for i in range(2)]
        usq = [B("usq%d" % i, [128, TT], BF16, at=arB + 20992 + 1024 * i) for i in range(2)]
        mean_t = B("mean_t", [128, TT], F32, at=arB + 23040)
        rstd_t = B("rstd_t", [128, TT], F32, at=arB + 25088)
        msq_t = B("msq_t", [128, TT], F32, at=arB + 27136)
        lnt = [B("lnt%d" % i, [128, TT], F32, at=arB + 29184 + 2048 * i) for i in range(2)]
        assert 29184 + 4096 <= 33 * 1024
        ynT = B("ynT", [128, 16, TT], BF16, at=arB)
        yc = B("yc", [128, DIN], F32, at=arB + 16384)
        mg = B("mg", [128, KC, TT], BF16, at=arB + 24576)
        arC = k.sb_alloc(32 * 1024)
        xs = B("xs", [128, NB, DIN], BF16, at=arC)
        siluz = B("siluz", [128, NB, DIN], BF16, at=arC + 16384)
        actT = B("actT", [128, NFF, TT], BF16, at=arC)
        sl = [B("sl%d" % i, [128, TT], F32, at=arC + 22528 + 2048 * i) for i in range(2)]
        xb = [B("xb%d" % i, [128, D], BF16) for i in range(2)]
        junk = B("junk", [128, D], BF16)
        st4 = B("st4", [128, 16], F32)
        t1 = B("t1", [128, KC, TT], BF16)
        xst = [B("xst%d" % i, [128, 520], BF16) for i in range(3)]
        BT = B("BT", [128, NG, TT], BF16)
        CT = B("CT", [128, NG, TT], BF16)
        Btok = B("Btok", [128, NB, 512], BF16)
        dtp = B("dtp", [128, 128], F32)
        dtt = B("dtt", [128, 128], F32)
        lndt = B("lndt", [128, 128], F32)
        da2 = B("da2", [128, NB, 64], F32)
        biasA = B("biasA", [128, 128], F32)
        Ee = B("Ee", [128, 128], F32)
        wdt = B("wdt", [128, 128], F32)
        decA = B("decA", [128, 128], F32)
        tmpd = B("tmpd", [128, 128], F32)
        acsT = B("acsT", [64, 512], BF16)
        acsTt = B("acsTt", [64, 512], BF16)
        cbm = B("cbm", [128, NG, 128], BF16)
        expo = [B("expo%d" % i, [128, 8, 128], BF16) for i in range(2)] + [B("expo2", [128, 8, 128], BF16, at=arB + 16384 + 4096)]
        xw = [B("xw%d" % i, [128, 512], BF16) for i in range(2)] + [B("xw2", [128, 512], BF16, at=arB + 16384 + 4096 + 2048)]
        otmp = [B("otmp%d" % i, [128, 512], F32) for i in range(2)]
        yn = B("yn", [128, DIN], BF16)
        stg_f = [B("stgf%d" % i, [128, 8, 512], F32, at=arB + 16384 * i) for i in range(2)]
        stg_f += [B("stgf%d" % (2 + i), [128, 8, 512], F32, at=slots[2 * i].off) for i in range(2)]
        stg_b = [B("stgb%d" % i, [128, 8, 512], BF16, at=arC + 8192 * i) for i in range(4)]
        NSF, NSB = 4, 4
        gate_bc = [B("gate_bc%d" % i, [128, D], F32, at=x_res.off + 4096 * i) for i in range(2)]
        sc_rep = B("sc_rep", [128, KC, 128], F32, at=arC + 24576)
        scv = B("scv", [128, 8], F32)
        print("SBUF bytes/partition used:", k.sb_top)
        assert k.sb_top <= 224 * 1024 - 2048, k.sb_top

        psf = [PsBank(k, "psf%d" % i, i, F32) for i in range(6)]
        psb = [PsBank(k, "psb%d" % i, 6 + i, BF16) for i in range(2)]

        ld_sem = k.newsem("ld")
        stgf_sem = [k.newsem("stgf") for _ in range(4)]
        stgb_sem = [k.newsem("stgb") for _ in range(4)]
        st_sem = k.newsem("st")
        xin_sem = k.newsem("xin")
        xres_sem = k.newsem("xres")
        y_sem = k.newsem("ysem")

        def load(dst, src_ap, q=SP, sem=None):
            return k.dma(q, dst, None, sem or k.newsem("l"), in_ap=src_ap)

        c_stage = stg_f[0]
        for (dstb, name, np_, ncol) in [(ident_b, "ident", 128, 128), (ones_b, None, 128, 128), (negm, "negmask4", 128, 512),
                                        (caus, "causal4", 128, 512)]:
            if name is None:
                continue
            v = c_stage.cust(0, [[1, ncol]], 0, np_, 0, ncol)
            load(v, dram[name][:, :])
            k.cp(DVE, dstb[:, :], v)
        load(ident_f[:, :], dram["ident"][:, :])
        load(tri_f[:, :], dram["tri"][:, :])
        k.memset(DVE, ones_f[:, :], 1.0)
        k.memset(DVE, ones_b[:, :], 1.0)
        for h in range(4):
            v = c_stage.cust(0, [[1, 1024]], 0, 64, 0, 1024)
            load(v, dram["sel"][:, h * 1024:(h + 1) * 1024])
            k.cp(DVE, sel[:, h * 1024:(h + 1) * 1024], v)
        v = c_stage.cust(0, [[1, 128]], 0, 32, 0, 128)
        load(v, dram["cb4mat"][:, :])
        k.cp(DVE, cb4mat[:, :], v)
        load(fw_bc[:, :], dram["fw_bc"][:, :])
        load(dtb[:, :], dram["dtb_bc"][:, :])
        load(a_bc[:, :], dram["alog_bc"][:, :])
        k.act(a_bc[:, :], a_bc[:, :], AF.Exp)
        k.ts(DVE, a_bc[:, :], a_bc[:, :], -1.0)
        for (col, name, n) in [(CB31, "cb31T", 8), (LNW, "lnwT", 8), (LNB, "lnbT", 8), (CB4, "cb4T", 24), (SNW, "snwT", 16)]:
            load(prm[:, col:col + n], dram[name][:, :])
        w31 = B("w31", [128, 8, 31], F32)
        w4 = B("w4", [128, 24, 4], F32)
        dskb = B("dskb", [128, 32], F32)
        load(w31[:, :, :], dram["w31T"][:, :, :])
        load(w4[:, :, :], dram["w4T"][:, :, :])
        load(dskb[:, :], dram["dsk_bc"][:, :])
        k.memset(DVE, S[:, :], 0.0)
        k.memset(DVE, Sb[:, :], 0.0)
        k.memset(DVE, uhalo[:, :, :], 0.0)
        k.memset(DVE, xhalo[:, :, :], 0.0)

        if DBG_STOP == "s0":
            raise _Stop()
        load(scv[:, :], cT_d[:, :])
        k.act(scv[:, :], scv[:, :], AF.Silu)
        for kc in range(KC):
            k.ts(DVE, sc_rep[:, kc, :], ones_f[:, :], scv[:, kc:kc + 1])
        nwt = B("nwt", [128, 16], F32)
        load(nwt[:, 0:8], dram["nw1T"][:, :])
        load(nwt[:, 8:16], dram["nw2T"][:, :])
        bTt = B("bTt", [128, 48], F32)
        load(bTt[:, 0:24], dram["bT_ada_mix"][:, :])
        load(bTt[:, 24:48], dram["bT_ada_ffn"][:, :])
        browt = B("browt", [1, 2048], F32, at=x_res.off + 8192)
        rowsb = B("rowsb", [1, 512], F32, at=hT.off)
        load(browt[:, 0:1024], dram["brow_ada_mix"][:, 2048:3072])
        load(browt[:, 1024:2048], dram["brow_ada_ffn"][:, 2048:3072])
        for li, wn in enumerate(["w_ada_mix", "w_ada_ffn"]):
            wv = dram[wn].rearrange("(kc p) f -> p kc f", p=128)
            gcol, scol = (G1, SH1) if li == 0 else (G2, SH2)
            for pc in range(6):
                sf = stg_f[pc % 2]
                load(sf[:, :, :], wv[:, :, pc * 512:(pc + 1) * 512], sem=stgf_sem[pc % 2])
                if pc < 4:
                    pr = psf[4 + pc % 2]
                    for kc in range(KC):
                        k.mm(pr[0:1, :], scv[:, kc:kc + 1], sf[:, kc, :], start=(kc == 0), stop=(kc == KC - 1), mark=(kc == KC - 1))
                    k.cp(DVE, rowsb[0:1, :], pr[0:1, :])
                    ps = psf[pc % 2]
                    for fcl in range(4):
                        k.mm(ps[:, fcl:fcl + 1], rowsb[0:1, fcl * 128:(fcl + 1) * 128], ones_f[0:1, 0:1], start=True, stop=True,
                             mark=(fcl == 3))
                    fc0 = pc * 4
                    if pc < 2:
                        k.tt(DVE, prm[:, scol + fc0:scol + fc0 + 4], ps[:, 0:4], bTt[:, li * 24 + fc0:li * 24 + fc0 + 4], ALU.add)
                    else:
                        f8 = fc0 - 8
                        k.tt(DVE, st4[:, 0:4], ps[:, 0:4], bTt[:, li * 24 + fc0:li * 24 + fc0 + 4], ALU.add)
                        k.stt(DVE, prm[:, gcol + f8:gcol + f8 + 4], st4[:, 0:4], 1.0, nwt[:, li * 8 + f8:li * 8 + f8 + 4],
                              ALU.add, ALU.mult)
                else:
                    ps = psf[2 + pc % 2]
                    h = pc - 4
                    for kc in range(KC):
                        k.mm(ps[:, :], sc_rep[:, kc, :], sf[:, kc, :], start=(kc == 0), stop=False)
                    k.mm(ps[:, :], ones_f[0:1, :], browt[0:1, li * 1024 + h * 512:li * 1024 + (h + 1) * 512], start=False,
                         stop=True, mark=True)
                    k.cp(DVE, gate_bc[li][:, h * 512:(h + 1) * 512], ps[:, :])

        if DBG_STOP == "s1":
            raise _Stop()
        ceng = [DVE, ACT, DVE]

        def diag(j, dst, sc):
            if j % 2 == 0:
                k.ts(DVE, dst, ident_f[:, :], sc)
            else:
                k.act(dst, ident_f[:, :], AF.Identity, scale=sc)
        stq = POOL
        w_ids = [i for i, p_ in enumerate(pieces) if p_["kind"] == "w"]
        d_ids = [i for i, p_ in enumerate(pieces) if p_["kind"] != "w"]
        prep_order = []
        while w_ids or d_ids:
            prep_order += w_ids[:3]
            w_ids = w_ids[3:]
            if d_ids:
                prep_order.append(d_ids.pop(0))
        fcnt = [0]
        piece_done = {}
        for sb0 in stg_b:
            k.memset(DVE, sb0[:, :, :], 0.0)
        for bi, pi in enumerate(prep_order):
            pc = pieces[pi]
            sb_ = stg_b[bi % NSB]
            if pc["kind"] == "w":
                fi_ = fcnt[0] % NSF
                fcnt[0] += 1
                sf = stg_f[fi_]
                wv = dram[pc["w"]].rearrange("(kc p) f -> p kc f", p=128)
                nk = pc["nk"]
                c = 0
                ncols = sum(s[1] for s in pc["segs"])
                for (c0, n) in pc["segs"]:
                    load(sf[:, 0:nk, c:c + n], wv[:, pc["k0"]:pc["k0"] + nk, c0:c0 + n], sem=stgf_sem[fi_])
                    c += n
                eng = ceng[bi % 3]
                if "rs" in pc:
                    base = {"g1": G1, "g2": G2, "snw": SNW + pc["k0"]}[pc["rs"]]
                    for kc in range(nk):
                        if eng is ACT:
                            k.act(sb_[:, kc, 0:ncols], sf[:, kc, 0:ncols], AF.Identity, scale=prm[:, base + kc:base + kc + 1])
                        else:
                            k.ts(eng, sb_[:, kc, 0:ncols], sf[:, kc, 0:ncols], prm[:, base + kc:base + kc + 1])
                elif "cs" in pc:
                    gb = gate_bc[0] if pc["cs"][0] == "gate1" else gate_bc[1]
                    c0 = pc["cs"][1]
                    e2 = DVE if eng is ACT else eng
                    gv = gb.cust(c0, [[0, nk], [1, 512]], 0, 128, c0, c0 + 512)
                    k.tt(e2, sb_[:, 0:nk, :], sf[:, 0:nk, :], gv, ALU.mult)
                else:
                    k.cp(eng, sb_[:, 0:nk, :], sf[:, 0:nk, :])
            elif pc["kind"] == "c31":
                fc = pc["fc"]
                sflat = sb_.cust(0, [[1, 4096]])
                for j in range(31):
                    diag(j, sb_.cust(j * 128, [[1, 128]], 0, 128, j * 128, (j + 1) * 128), w31[:, fc, j:j + 1])
            elif pc["kind"] == "c4":
                for cl in range(4):
                    ch = pc["g"] * 4 + cl
                    for j in range(4):
                        diag(j, sb_[:, cl, j * 128:(j + 1) * 128], w4[:, ch, j:j + 1])
            elif pc["kind"] == "dsk":
                for j in range(NH):
                    diag(j, sb_.cust(j * 128, [[1, 128]], 0, 128, j * 128, (j + 1) * 128), dskb[:, j:j + 1])
            k.dma(stq, None, sb_[:, :, :], stgb_sem[bi % NSB], out_ap=wscr[pi].rearrange("p (a b) -> p a b", b=512))
            piece_done[pi] = (stgb_sem[bi % NSB], stgb_sem[bi % NSB].cnt)

        if DBG_STOP == "s2":
            raise _Stop()
        order = tile_order(ids)
        NPT = len(order)
        wstate = {"next": 0}
        total_loads = NT * NPT

        def issue_loads(upto):
            while wstate["next"] < min(upto, total_loads):
                n = wstate["next"]
                pid = ids[order[n % NPT]]
                s = n % NSLOT
                if n < NPT:
                    so_, v_ = piece_done[pid]
                    if SP.waited.get(("st", so_.id), 0) < v_:
                        SP.h.wait_ge(so_.sem, v_)
                        SP.waited[("st", so_.id)] = v_
                k.dma(SP, slots[s][:, :, :], None, slot_sem[s], in_ap=wscr[pid].rearrange("p (a b) -> p a b", b=512))
                wstate["next"] += 1

        wpos = {"n": 0}

        def W(name):
            n = wpos["n"]
            assert order[n % NPT] == name, (order[n % NPT], name)
            issue_loads(n + NSLOT - 1)
            wpos["n"] += 1
            return slots[n % NSLOT]

        def normA(src):
            for q in range(NB):
                k.act(junk[:, :], src[:, q, :], AF.Square, accum=st4[:, q:q + 1])
            k.act(st4[:, 4:8], st4[:, 0:4], AF.Ln, bias=EPS, scale=1.0 / D)
            k.act(st4[:, 4:8], st4[:, 4:8], AF.Exp, scale=-0.5)
            for q in range(2):
                k.ts(DVE, xb[q % 2][:, :], src[:, q, :], st4[:, 4 + q:5 + q])

        def normB(src, gcol, scol):
            for q in range(NB):
                if q >= 2:
                    k.ts(DVE, xb[q % 2][:, :], src[:, q, :], st4[:, 4 + q:5 + q])
                for kc in range(KC):
                    pb = psb[q % 2]
                    k.tr(pb[:, kc * 128:(kc + 1) * 128], xb[q % 2][:, kc * 128:(kc + 1) * 128], ident_b[:, :], mark=(kc == KC - 1))
                for kc in range(KC):
                    e = DVE if q % 2 == 0 else ACT
                    o = hT[:, kc, q * 128:(q + 1) * 128]
                    i = psb[q % 2][:, kc * 128:(kc + 1) * 128]
                    if e is DVE:
                        k.ts(DVE, o, i, prm[:, gcol + kc:gcol + kc + 1], prm[:, scol + kc:scol + kc + 1], ALU.mult, ALU.add)
                    else:
                        k.act(o, i, AF.Identity, bias=prm[:, scol + kc:scol + kc + 1], scale=prm[:, gcol + kc:gcol + kc + 1])

        def norm_block(src, q, gcol, scol):
            k.act(junk[:, :], src[:, q, :], AF.Square, accum=st4[:, q:q + 1])
            k.act(st4[:, 4 + q:5 + q], st4[:, q:q + 1], AF.Ln, bias=EPS, scale=1.0 / D)
            k.act(st4[:, 4 + q:5 + q], st4[:, 4 + q:5 + q], AF.Exp, scale=-0.5)
            k.ts(DVE, xb[q % 2][:, :], src[:, q, :], st4[:, 4 + q:5 + q])

        def norm_block2(q, gcol, scol):
            for kc in range(KC):
                k.tr(psb[q % 2][:, kc * 128:(kc + 1) * 128], xb[q % 2][:, kc * 128:(kc + 1) * 128], ident_b[:, :], mark=(kc == KC - 1))
            for kc in range(KC):
                o = hT[:, kc, q * 128:(q + 1) * 128]
                i = psb[q % 2][:, kc * 128:(kc + 1) * 128]
                if q % 2 == 0:
                    k.ts(DVE, o, i, prm[:, gcol + kc:gcol + kc + 1], prm[:, scol + kc:scol + kc + 1], ALU.mult, ALU.add)
                else:
                    k.act(o, i, AF.Identity, bias=prm[:, scol + kc:scol + kc + 1], scale=prm[:, gcol + kc:gcol + kc + 1])

        def rmsnorm_to_hT(src, gcol, scol):
            normA(src)
            normB(src, gcol, scol)

        xv = x_d.rearrange("(t q p) d -> t p q d", p=128, q=NB)
        yv = y_d.rearrange("(t q p) d -> t p q d", p=128, q=NB)

        k.dma(SP, x_in[:, :, :], None, xin_sem, in_ap=xv[0])
        for ti in range(NT):
            if ti == 0:
                rmsnorm_to_hT(x_in, G1, SH1)
            if DBG_STOP == "p1":
                break
            k.cp(POOL, uT[:, :, 0:30], uhalo[:, :, 0:30])
            for half in range(2):
                wa = W("a%d" % half)
                wb = W("b%d" % half)
                for fl in range(4):
                    fc = half * 4 + fl
                    pa = psf[(fc % 3) * 2]
                    pb_ = psf[(fc % 3) * 2 + 1]
                    for kc in range(KC):
                        k.mm(pa[:, :], wa[:, kc, fl * 128:(fl + 1) * 128], hT[:, kc, :], start=(kc == 0), stop=(kc == KC - 1))
                    for kc in range(KC):
                        k.mm(pb_[:, :], wb[:, kc, fl * 128:(fl + 1) * 128], hT[:, kc, :], start=(kc == 0), stop=(kc == KC - 1),
                             mark=(kc == KC - 1))
                    k.act(sgt[fc % 2][:, :], pb_[:, :], AF.Sigmoid)
                    k.tt(DVE, uT[:, fc, 30:30 + TT], pa[:, :], sgt[fc % 2][:, :], ALU.mult)
            k.cp(POOL, uhalo[:, :, 0:30], uT[:, :, TT:TT + 30])
            if DBG_STOP == "p2a":
                break
            pS, pQ = psf[4], psf[5]

            def ln_stats(fc_):
                k.mm(pS[:, :], ones_b[:, :], uv[:, fc_, :], start=(fc_ == 0), stop=(fc_ == 7))
                k.mm(pQ[:, :], ones_b[:, :], usq[fc_ % 2][:, :], start=(fc_ == 0), stop=(fc_ == 7), mark=True)

            for fc in range(8):
                wd = W("c31_%d" % fc)
                pu = psf[fc % 4]
                for j in range(31):
                    k.mm(pu[:, :], wd.cust(j * 128, [[1, 128]], 0, 128, j * 128, (j + 1) * 128), uT[:, fc, j:j + TT],
                         start=(j == 0), stop=(j == 30), mark=(j == 30))
                k.ts(DVE, uv[:, fc, :], pu[:, :], prm[:, CB31 + fc:CB31 + fc + 1], None, ALU.add)
                k.act(usq[fc % 2][:, :], pu[:, :], AF.Square, bias=prm[:, CB31 + fc:CB31 + fc + 1])
                if fc > 0:
                    ln_stats(fc - 1)
            ln_stats(7)
            k.ts(DVE, mean_t[:, :], pS[:, :], 1.0 / D)
            k.tt(DVE, msq_t[:, :], mean_t[:, :], mean_t[:, :], ALU.mult)
            k.stt(DVE, rstd_t[:, :], pQ[:, :], 1.0 / D, msq_t[:, :], ALU.mult, ALU.subtract)
            k.act(rstd_t[:, :], rstd_t[:, :], AF.Ln, bias=LN_EPS)
            k.act(rstd_t[:, :], rstd_t[:, :], AF.Exp, scale=-0.5)
            def zproj(zi):
                wz = W("z%d" % zi)
                for q in range(NB):
                    pz = psf[(zi * NB + q) % 6]
                    for kc in range(KC):
                        k.mm(pz[:, :], hT[:, kc, q * 128:(q + 1) * 128], wz[:, kc, :], start=(kc == 0), stop=(kc == KC - 1),
                             mark=(kc == KC - 1))
                    k.act(siluz[:, q, zi * 512:(zi + 1) * 512], pz[:, :], AF.Silu)

            zproj(0)
            zproj(1)
            for fc in range(8):
                lt = lnt[fc % 2]
                k.tt(DVE, lt[:, :], uv[:, fc, :], mean_t[:, :], ALU.subtract)
                k.tt(POOL, lt[:, :], lt[:, :], rstd_t[:, :], ALU.mult)
                k.act(uv[:, fc, :], lt[:, :], AF.Silu, bias=prm[:, LNB + fc:LNB + fc + 1], scale=prm[:, LNW + fc:LNW + fc + 1])
            zproj(2)
            zproj(3)
            if DBG_STOP == "p2b":
                break
            for half in range(2):
                wc = W("co%d" % half)
                wg = W("gc%d" % half)
                for fl in range(4):
                    fo = half * 4 + fl
                    py = psf[(fo % 3) * 2]
                    pg = psf[(fo % 3) * 2 + 1]
                    for c in range(KC):
                        k.mm(py[:, :], wc[:, c, fl * 128:(fl + 1) * 128], uv[:, c, :], start=(c == 0), stop=(c == KC - 1))
                    for kc in range(KC):
                        k.mm(pg[:, :], wg[:, kc, fl * 128:(fl + 1) * 128], hT[:, kc, :], start=(kc == 0), stop=(kc == KC - 1),
                             mark=(kc == KC - 1))
                    k.act(sgt[fo % 2][:, :], pg[:, :], AF.Sigmoid)
                    k.tt(DVE, t1[:, fo, :], py[:, :], sgt[fo % 2][:, :], ALU.mult)
            if DBG_STOP == "p2c":
                break
            xw_ = {}

            def xproj(ch):
                if ch % 4 == 0:
                    xw_["c4"] = W("c4_%d" % (ch // 4))
                    if ch < 16:
                        xw_["x"] = W("xs%d" % (ch // 4))
                    elif ch == 16:
                        xw_["x"] = W("B")
                    else:
                        xw_["x"] = W("C")
                wx = xw_["x"]
                fl = ch % 4
                px = psf[ch % 2]
                for kc in range(KC):
                    k.mm(px[:, :], wx[:, kc, fl * 128:(fl + 1) * 128], hT[:, kc, :], start=(kc == 0), stop=(kc == KC - 1),
                         mark=(kc == KC - 1))
                st = xst[ch % 3]
                k.cp(POOL, st[:, 0:3], xhalo[:, ch, 0:3])
                if ch % 2 == 0:
                    k.cp(DVE, st[:, 3:3 + TT], px[:, :])
                else:
                    k.cp(ACT, st[:, 3:3 + TT], px[:, :])
                k.cp(POOL, xhalo[:, ch, 0:3], st[:, TT:TT + 3])

            def xconv(ch):
                wc4 = xw_["c4"]
                fl = ch % 4
                cl = ch % 4
                st = xst[ch % 3]
                if ch < 20:
                    for q in range(NB):
                        pt = psf[2 + q]
                        for j in range(4):
                            k.mm(pt[:, fl * 128:(fl + 1) * 128], st[:, q * 128 + j:q * 128 + j + 128],
                                 wc4[:, cl, j * 128:(j + 1) * 128], start=(j == 0), stop=False)
                        k.mm(pt[:, fl * 128:(fl + 1) * 128], sel[0:32, ch * 128:(ch + 1) * 128], cb4mat[0:32, :], start=False,
                             stop=True, mark=(fl == 3))
                    if fl == 3:
                        for q in range(NB):
                            if ch < 16:
                                k.act(xs[:, q, (ch // 4) * 512:(ch // 4 + 1) * 512], psf[2 + q][:, :], AF.Silu)
                            else:
                                k.act(Btok[:, q, :], psf[2 + q][:, :], AF.Silu)
                if ch >= 16:
                    pf = psf[ch % 2]
                    for j in range(4):
                        k.mm(pf[:, :], wc4[:, cl, j * 128:(j + 1) * 128], st[:, j:j + TT], start=(j == 0), stop=(j == 3),
                             mark=(j == 3))
                    dst = BT if ch < 20 else CT
                    k.act(dst[:, ch % 4, :], pf[:, :], AF.Silu, bias=prm[:, CB4 + ch:CB4 + ch + 1])

            for ch in range(24):
                if ch % 4 == 0:
                    xproj(ch)
                if ch % 4 != 3:
                    xproj(ch + 1)
                xconv(ch)
            if DBG_STOP == "p2d":
                break
            wdtp = W("dt")
            pD, pA, pAT, pTot = psf[0], psf[1], psf[2], psf[3]
            for q in range(NB):
                for kc in range(KC):
                    k.mm(pD[:, q * 32:(q + 1) * 32], hT[:, kc, q * 128:(q + 1) * 128], wdtp[:, kc, 0:32], start=(kc == 0),
                         stop=(kc == KC - 1), mark=(kc == KC - 1 and q == NB - 1))
            k.tt(DVE, dtp[:, :], pD[:, 0:128], dtb[:, :], ALU.add)
            k.act(dtt[:, :], dtp[:, :], AF.Exp)
            k.act(dtt[:, :], dtt[:, :], AF.Ln, bias=1.0)
            k.act(lndt[:, :], dtt[:, :], AF.Ln)
            for q in range(NB):
                for r in range(2):
                    k.tt(DVE, da2[:, q, r * 32:(r + 1) * 32], dtt[:, q * 32:(q + 1) * 32], a_bc[:, q * 32:(q + 1) * 32], ALU.mult)
            for q in range(NB):
                k.mm(pA[:, q * 32:(q + 1) * 32], tri_f[:, :], da2[:, q, 0:32], start=True, stop=True)
                k.mm(pTot[:, q * 32:(q + 1) * 32], ones_f[:, :], da2[:, q, 0:32], start=True, stop=True)
                k.mm(pAT[0:64, q * 128:(q + 1) * 128], da2[:, q, :], tri_f[:, :], start=True, stop=True, mark=(q == NB - 1))
            k.tt(DVE, biasA[:, :], lndt[:, :], pA[:, 0:128], ALU.subtract)
            k.act(Ee[:, :], pA[:, 0:128], AF.Exp)
            k.tt(DVE, tmpd[:, :], pTot[:, 0:128], biasA[:, :], ALU.add)
            k.act(wdt[:, :], tmpd[:, :], AF.Exp)
            k.act(decA[:, :], pTot[:, 0:128], AF.Exp)
            k.cp(DVE, acsT[0:32, :], pAT[0:32, :])
            k.cp(DVE, acsTt[32:64, :], pAT[32:64, :])
            k.tt(DVE, acsT[32:64, :], pAT[32:64, :], acsTt[32:64, :], ALU.subtract)
            if DBG_STOP == "p2e":
                break
            if DBG_STOP == "p2f":
                break
            wdsk = W("dsk")

            def ssd_front(si, q, g):
                if g == 0:
                    pC = psf[0]
                    for g2 in range(NG):
                        k.mm(pC[:, g2 * 128:(g2 + 1) * 128], BT[:, g2, q * 128:(q + 1) * 128], CT[:, g2, q * 128:(q + 1) * 128],
                             start=True, stop=True, mark=(g2 == NG - 1))
                    k.tt(DVE, cbm.cust(0, [[1, 512]]), pC[:, :], caus[:, :], ALU.mult)
                ex = expo[si % 3]
                for hh in range(2):
                    pR = psf[1 + hh]
                    k.mm(pR[:, :], ident_b[:, :], negm[:, :], start=True, stop=False)
                    for jl in range(4):
                        j = g * 8 + hh * 4 + jl
                        k.mm(pR[:, jl * 128:(jl + 1) * 128], sel[0:64, j * 128:(j + 1) * 128], acsT[0:64, q * 128:(q + 1) * 128],
                             start=False, stop=(jl == 3), mark=(jl == 3))
                    for jl in range(4):
                        j = g * 8 + hh * 4 + jl
                        k.act(ex[:, hh * 4 + jl, :], pR[:, jl * 128:(jl + 1) * 128], AF.Exp,
                              bias=biasA[:, q * 32 + j:q * 32 + j + 1])

            def ssd_front_b(si, q, g):
                ex = expo[si % 3]
                cbv = cbm.cust(g * 128, [[0, 8], [1, 128]], 0, 128, g * 128, (g + 1) * 128)
                k.tt(POOL, ex[:, :, :], ex[:, :, :], cbv, ALU.mult)
                wv_ = wdt.cust(q * 32 + g * 8, [[1, 8], [0, 64]], 0, 128, q * 32 + g * 8, q * 32 + g * 8 + 8)
                k.tt(POOL, xw[si % 3].cust(0, [[64, 8], [1, 64]]), xs.cust(q * DIN + g * 512, [[64, 8], [1, 64]], 0, 128,
                                                                     q * DIN + g * 512, q * DIN + (g + 1) * 512), wv_, ALU.mult)

            def ssd_back(si, q, g):
                ex = expo[si % 3]
                pY, pO, pDl = psf[3], psf[4], psf[5]
                k.mm(pO[:, :], CT[:, g, q * 128:(q + 1) * 128], Sb[:, g * 512:(g + 1) * 512], start=True, stop=True, mark=True)
                k.mm(pDl[:, :], Btok[:, q, g * 128:(g + 1) * 128], xw[si % 3][:, :], start=True, stop=True, mark=True)
                for jl in range(8):
                    j = g * 8 + jl
                    xsl = xs[:, q, j * 64:(j + 1) * 64]
                    k.mm(pY[:, jl * 64:(jl + 1) * 64], ex[:, jl, :], xsl, start=True, stop=False)
                    k.mm(pY[:, jl * 64:(jl + 1) * 64], wdsk.cust(j * 128, [[1, 128]], 0, 128, j * 128, (j + 1) * 128), xsl,
                         start=False, stop=True, mark=(jl == 7))
                ev = Ee.cust(q * 32 + g * 8, [[1, 8], [0, 64]], 0, 128, q * 32 + g * 8, q * 32 + g * 8 + 8)
                ot = otmp[g % 2]
                ycg = yc[:, (g % 2) * 512:(g % 2 + 1) * 512]
                k.tt(DVE, ot.cust(0, [[64, 8], [1, 64]]), pO.cust(0, [[64, 8], [1, 64]]), ev, ALU.mult)
                k.tt(DVE, ycg, pY[:, :], ot[:, :], ALU.add)
                k.tt(DVE, ycg, ycg, siluz[:, q, g * 512:(g + 1) * 512], ALU.mult)
                k.tt(DVE, S[:, g * 512:(g + 1) * 512], S[:, g * 512:(g + 1) * 512], pDl[:, :], ALU.add)
                k.cp(DVE, Sb[:, g * 512:(g + 1) * 512], S[:, g * 512:(g + 1) * 512])

            def ssd_back2(si, q, g):
                ycg = yc[:, (g % 2) * 512:(g % 2 + 1) * 512]
                k.act(junk[:, 0:512], ycg, AF.Square, accum=st4[:, 8 + g:9 + g])
                k.act(st4[:, 12 + g:13 + g], st4[:, 8 + g:9 + g], AF.Ln, bias=EPS, scale=1.0 / 512)
                k.act(st4[:, 12 + g:13 + g], st4[:, 12 + g:13 + g], AF.Exp, scale=-0.5)
                yng = yn[:, (g % 2) * 512:(g % 2 + 1) * 512]
                k.ts(DVE, yng, ycg, st4[:, 12 + g:13 + g])

            def ssd_back3(si, q, g):
                pb = psb[g % 2]
                for c4 in range(4):
                    k.tr(pb[:, c4 * 128:(c4 + 1) * 128], yn[:, (g % 2) * 512 + c4 * 128:(g % 2) * 512 + (c4 + 1) * 128], ident_b[:, :],
                         mark=(c4 == 3))
                dstv = ynT.cust(g * 4 * TT + q * 128, [[TT, 4], [1, 128]], 0, 128, g * 4 * TT, (g + 1) * 4 * TT)
                srcv = pb.cust(0, [[128, 4], [1, 128]])
                k.cp(DVE, dstv, srcv)

            def ssd_back_pool(si, q, g):
                dv = decA.cust(q * 32 + g * 8, [[1, 8], [0, 64]], 0, 128, q * 32 + g * 8, q * 32 + g * 8 + 8)
                Sg = S.cust(g * 512, [[64, 8], [1, 64]], 0, 128, g * 512, (g + 1) * 512)
                k.tt(POOL, Sg, Sg, dv, ALU.mult)

            steps = [(q, g) for q in range(NB) for g in range(NG)]
            for s0 in range(2):
                ssd_front(s0, *steps[s0])
                ssd_front_b(s0, *steps[s0])
            for si, (q, g) in enumerate(steps):
                if si + 2 < len(steps):
                    ssd_front(si + 2, *steps[si + 2])
                ssd_back_pool(si, q, g)
                if si + 2 < len(steps):
                    ssd_front_b(si + 2, *steps[si + 2])
                ssd_back(si, q, g)
                if si >= 1:
                    ssd_back2(si - 1, *steps[si - 1])
                if si >= 2:
                    ssd_back3(si - 2, *steps[si - 2])
            nS = len(steps)
            ssd_back2(nS - 1, *steps[nS - 1])
            ssd_back3(nS - 2, *steps[nS - 2])
            ssd_back3(nS - 1, *steps[nS - 1])
            if DBG_STOP == "p3":
                break
            k.dma(SP, x_res[:, :, :], None, xres_sem, in_ap=xv[ti])
            for cg in range(2):
                for kh in range(2):
                    ws = W("so%d%d" % (cg, kh))
                    for fl in range(4):
                        for c8 in range(8):
                            k.mm(psf[fl][:, :], ws[:, c8, fl * 128:(fl + 1) * 128], ynT[:, kh * 8 + c8, :],
                                 start=(kh == 0 and c8 == 0), stop=(kh == 1 and c8 == 7), mark=(kh == 1 and c8 == 7))
                wg = W("gs%d" % cg)
                for fl in range(4):
                    fo = cg * 4 + fl
                    pg = psf[4 + fl % 2]
                    for kc in range(KC):
                        k.mm(pg[:, :], wg[:, kc, fl * 128:(fl + 1) * 128], hT[:, kc, :], start=(kc == 0), stop=(kc == KC - 1),
                             mark=(kc == KC - 1))
                    k.act(sl[fl % 2][:, :], pg[:, :], AF.Sigmoid)
                    k.tt(DVE, sl[fl % 2][:, :], psf[fl][:, :], sl[fl % 2][:, :], ALU.mult)
                    k.tt(ew(), mg[:, fo, :], sl[fl % 2][:, :], t1[:, fo, :], ALU.add)
            wos = [W("wo0"), W("wo1")]
            for q in range(NB):
                for pc_ in range(2):
                    wo = wos[pc_]
                    pw = psf[(q * 2 + pc_) % 6]
                    for kc in range(KC):
                        k.mm(pw[:, :], mg[:, kc, q * 128:(q + 1) * 128], wo[:, kc, :], start=(kc == 0), stop=(kc == KC - 1),
                             mark=(kc == KC - 1))
                    k.tt(DVE, x_res[:, q, pc_ * 512:(pc_ + 1) * 512], x_res[:, q, pc_ * 512:(pc_ + 1) * 512], pw[:, :], ALU.add)
                norm_block(x_res, q, G2, SH2)
                if q >= 1:
                    norm_block2(q - 1, G2, SH2)
            norm_block2(NB - 1, G2, SH2)
            if ti + 1 < NT:
                k.dma(SP, x_in[:, :, :], None, xin_sem, in_ap=xv[ti + 1])
            if DBG_STOP == "p4":
                break
            for m in range(11):
                wf = W("fi%d" % m)
                for il in range(2):
                    i = 2 * m + il
                    pgt = psf[(i % 3) * 2]
                    pup = psf[(i % 3) * 2 + 1]
                    for kc in range(KC):
                        k.mm(pgt[:, :], wf[:, kc, il * 128:(il + 1) * 128], hT[:, kc, :], start=(kc == 0), stop=(kc == KC - 1),
                             mark=(kc == KC - 1))
                    for kc in range(KC):
                        k.mm(pup[:, :], wf[:, kc, 256 + il * 128:256 + (il + 1) * 128], hT[:, kc, :], start=(kc == 0),
                             stop=(kc == KC - 1), mark=(kc == KC - 1))
                    k.act(sl[i % 2][:, :], pgt[:, :], AF.Silu)
                    k.tt(DVE, actT[:, i, :], sl[i % 2][:, :], pup[:, :], ALU.mult)
            nxt = (ti + 1 < NT) and DBG_STOP is None
            if nxt:
                normA(x_in)
            for cg in range(2):
                if cg == 1 and nxt:
                    normB(x_in, G1, SH1)
                for ks in range(3):
                    wd_ = W("fd%d%d" % (cg, ks))
                    nk = 8 if ks < 2 else 6
                    for q in range(NB):
                        for c8 in range(nk):
                            c = ks * 8 + c8
                            k.mm(psf[q][:, :], actT[:, c, q * 128:(q + 1) * 128], wd_[:, c8, :], start=(c == 0), stop=(c == NFF - 1),
                                 mark=(c == NFF - 1))
                for q in range(NB):
                    k.tt(DVE, x_res[:, q, cg * 512:(cg + 1) * 512], x_res[:, q, cg * 512:(cg + 1) * 512], psf[q][:, :], ALU.add)
            if DBG_STOP == "p5":
                break
            for q in range(NB):
                k.act(junk[:, :], x_res[:, q, :], AF.Square, accum=st4[:, q:q + 1])
            k.act(st4[:, 4:8], st4[:, 0:4], AF.Ln, bias=EPS, scale=1.0 / D)
            k.act(st4[:, 4:8], st4[:, 4:8], AF.Exp, scale=-0.5)
            for q in range(NB):
                k.stt(DVE, x_res[:, q, :], x_res[:, q, :], st4[:, 4 + q:5 + q], fw_bc[:, :], ALU.mult, ALU.mult)
            k.dma(SP, None, x_res[:, :, :], y_sem, out_ap=yv[ti])
        SP.h.wait_ge(y_sem.sem, y_sem.cnt)
      except _Stop:
        pass
    return nc


def host_consts():
    ident = np.eye(128, dtype=np.float32)
    s = np.arange(128)[:, None]
    l = np.arange(128)[None, :]
    tri = (s <= l).astype(np.float32)
    neg1 = np.where(l < s, -30000.0, 0.0).astype(np.float32)
    caus1 = (l >= s).astype(np.float32)
    negmask4 = np.tile(neg1, (1, 4))
    causal4 = np.tile(caus1, (1, 4))
    sel = np.zeros((64, 32, 128), np.float32)
    for j in range(32):
        sel[j, j, :] = 1.0
        sel[32 + j, j, :] = 1.0
    return {"ident": ident, "tri": tri, "negmask4": negmask4, "causal4": causal4, "sel": sel.reshape(64, 4096)}


def host_layout(inp, b, Lc):
    f = lambda a: np.ascontiguousarray(np.asarray(a, dtype=np.float32))
    pT = lambda v, n: f(np.asarray(v).reshape(n, 128).T)
    m = {}
    m["x"] = f(inp["x"][b, :Lc])
    m["cT"] = pT(inp["c"][b], 8)
    m["w_ada_mix"] = f(inp["w_ada_mix"][0])
    m["w_ada_ffn"] = f(inp["w_ada_ffn"][0])
    m["bT_ada_mix"] = pT(inp["b_ada_mix"][0], 24)
    m["bT_ada_ffn"] = pT(inp["b_ada_ffn"][0], 24)
    m["brow_ada_mix"] = f(np.asarray(inp["b_ada_mix"][0]).reshape(1, 3072))
    m["brow_ada_ffn"] = f(np.asarray(inp["b_ada_ffn"][0]).reshape(1, 3072))
    m["nw1T"] = pT(inp["norm_mix_w"][0], 8)
    m["nw2T"] = pT(inp["norm_ffn_w"][0], 8)
    m["w_in"] = f(inp["w_in"][0])
    m["w31T"] = f(np.asarray(inp["conv_dw_w"][0]).reshape(31, 8, 128).transpose(2, 1, 0))
    m["cb31T"] = pT(inp["conv_dw_b"][0], 8)
    m["lnwT"] = pT(inp["conv_ln_w"][0], 8)
    m["lnbT"] = pT(inp["conv_ln_b"][0], 8)
    m["w_conv_out"] = f(inp["w_conv_out"][0])
    m["w4T"] = f(np.asarray(inp["ssm_conv_w"][0]).reshape(4, 24, 128).transpose(2, 1, 0))
    m["cb4T"] = pT(inp["ssm_conv_b"][0], 24)
    cbm_ = np.zeros((32, 128), np.float32)
    cbm_[:24] = np.asarray(inp["ssm_conv_b"][0]).reshape(24, 128)
    m["cb4mat"] = cbm_
    m["dtb_bc"] = f(np.tile(np.asarray(inp["dt_bias"][0]).reshape(1, 32), (128, 4)))
    m["alog_bc"] = f(np.tile(np.asarray(inp["a_log"][0]).reshape(1, 32), (128, 4)))
    m["dsk_bc"] = f(np.tile(np.asarray(inp["d_skip"][0]).reshape(1, 32), (128, 1)))
    m["snwT"] = pT(inp["ssm_norm_w"][0], 16)
    m["w_ssm_out"] = f(inp["w_ssm_out"][0])
    m["w_out"] = f(inp["w_out"][0])
    m["w_ffn_in"] = f(inp["w_ffn_in"][0])
    m["w_ffn_down"] = f(inp["w_ffn_down"][0])
    m["fw_bc"] = f(np.tile(np.asarray(inp["final_norm_w"]).reshape(1, D), (128, 1)))
    m.update(host_consts())
    return m


def kernel(**inputs):
    nb = inputs["x"].shape[0]
    Lc = inputs["x"].shape[1]
    NT = Lc // TT
    nc = build(NT)
    in_maps = [host_layout(inputs, b, Lc) for b in range(nb)]
    res = run_bass_kernel_spmd(nc, in_maps, core_ids=list(range(nb)))
    out = np.stack([np.asarray(r["y"], dtype=np.float32) for r in res.results], axis=0)
    return out
```

```python
import contextlib
import numpy as np
import concourse.bass as bass
import concourse.mybir as mybir
from concourse.bass_utils import run_bass_kernel_spmd

F32 = mybir.dt.float32
BF16 = mybir.dt.bfloat16
AF = mybir.ActivationFunctionType
ALU = mybir.AluOpType

D = 1024
KC = 8
TT = 512
NB = 4
SEQ = 8192
DIN = 2048
NH = 32
HD = 64
NG = 4
NST = 128
DFF = 2816
NFF = 22
INC = 9248
NSLOT = 4
NDVE = 6
SAME_ENG_SYNC = True
DBG_STOP = None
EPS = 1e-6
LN_EPS = 1e-5


class _Stop(Exception):
    pass


class SemObj:
    def __init__(self, sem, owner, sid):
        self.sem = sem
        self.owner = owner
        self.id = sid
        self.cnt = 0


class Eng:
    def __init__(self, name, h, so):
        self.name = name
        self.h = h
        self.so = so
        so.owner = self
        self.waited = {}
        self.last = None


class V:
    __slots__ = ("ap", "sp", "lo", "hi")

    def __init__(self, ap, sp, lo, hi):
        self.ap = ap
        self.sp = sp
        self.lo = lo
        self.hi = hi


class Buf:
    def __init__(self, K, name, shape, dt, at=None):
        self.esz = 4 if dt == F32 else 2
        nfree = int(np.prod(shape[1:]))
        self.nbytes = nfree * self.esz
        if at is None:
            at = K.sb_alloc(self.nbytes)
        self.off = at
        self.shape = list(shape)
        self.dt = dt
        self.nfree = nfree
        self.t = K.nc.alloc_sbuf_tensor_at(name, list(shape), dt, offset=at)
        st = []
        acc = 1
        for d in reversed(shape[1:]):
            st.append(acc)
            acc *= d
        self.strides = list(reversed(st))

    def __getitem__(self, idx):
        if not isinstance(idx, tuple):
            idx = (idx,)
        ap = self.t[idx]
        lo = 0
        hi = 0
        fr = idx[1:]
        for i, d in enumerate(self.shape[1:]):
            if i < len(fr):
                x = fr[i]
                if isinstance(x, int):
                    a, b = x, x + 1
                else:
                    a = 0 if x.start is None else x.start
                    b = d if x.stop is None else x.stop
            else:
                a, b = 0, d
            lo += a * self.strides[i]
            hi += (b - 1) * self.strides[i]
        hi += 1
        return V(ap, "sb", self.off + lo * self.esz, self.off + hi * self.esz)

    def cust(self, off, dims, p0=0, npart=128, lo=None, hi=None):
        ap = bass.AP(self.t, p0 * self.nfree + off, [[self.nfree, npart]] + [list(d) for d in dims])
        if lo is None:
            lo = 0
            hi = self.nfree
        return V(ap, "sb", self.off + lo * self.esz, self.off + hi * self.esz)


class PsBank:
    def __init__(self, K, name, idx, dt):
        n = 512 if dt == F32 else 1024
        self.t = K.nc.alloc_psum_tensor(name, [128, n], dt)
        self.idx = idx
        self.n = n

    def __getitem__(self, idx):
        return V(self.t[idx], "ps", self.idx, self.idx + 1)

    def cust(self, off, dims, p0=0, npart=128):
        ap = bass.AP(self.t, p0 * self.n + off, [[self.n, npart]] + [list(d) for d in dims])
        return V(ap, "ps", self.idx, self.idx + 1)


class K:
    def __init__(self, nc, es):
        self.nc = nc
        self.es = es
        self.sb_top = 16640
        self.nsem = 0
        self.recs = {"sb": [], "ps": []}
        self.PE = Eng("pe", nc.tensor, self.newsem("pe"))
        self.ACT = Eng("act", nc.scalar, self.newsem("act"))
        self.DVE = Eng("dve", nc.vector, self.newsem("dve"))
        self.POOL = Eng("pool", nc.gpsimd, self.newsem("pool"))
        self.SP = Eng("sp", nc.sync, self.newsem("sp"))

    def newsem(self, name):
        s = self.es.enter_context(self.nc.semaphore(name + "_%d" % self.nsem))
        so = SemObj(s, None, self.nsem)
        self.nsem += 1
        return so

    def sb_alloc(self, nbytes):
        a = self.sb_top
        self.sb_top += (nbytes + 63) // 64 * 64
        return a

    def _need(self, eng, so, val):
        if so.owner is eng and not (SAME_ENG_SYNC and eng is not self.PE):
            return
        if so.owner is not None and so.cnt < val:
            o = so.owner
            o.last.then_inc(so.sem, 1)
            so.cnt += 1
            assert so.cnt >= val, (o.name, so.cnt, val)
        if eng.waited.get(so.id, 0) < val:
            eng.h.wait_ge(so.sem, val)
            eng.waited[so.id] = val

    def _sync(self, eng, reads, writes):
        needs = {}
        for v in reads:
            if v is None or v.sp is None:
                continue
            for r in self.recs[v.sp]:
                if (r[2] == "W" or (v.sp == "ps" and r[3].owner is not eng)) and r[0] < v.hi and v.lo < r[1]:
                    if needs.get(r[3], 0) < r[4]:
                        needs[r[3]] = r[4]
        for v in writes:
            if v is None or v.sp is None:
                continue
            for r in self.recs[v.sp]:
                if r[0] < v.hi and v.lo < r[1]:
                    if needs.get(r[3], 0) < r[4]:
                        needs[r[3]] = r[4]
        for so, val in needs.items():
            self._need(eng, so, val)

    def _record(self, reads, writes, so, val):
        for v in writes:
            if v is None or v.sp is None:
                continue
            L = self.recs[v.sp]
            L[:] = [r for r in L if not (v.lo <= r[0] and r[1] <= v.hi)]
            L.append([v.lo, v.hi, "W", so, val])
        for v in reads:
            if v is None or v.sp is None:
                continue
            L = self.recs[v.sp]
            L[:] = [r for r in L if not (r[2] == "R" and r[3] is so and v.lo <= r[0] and r[1] <= v.hi)]
            L.append([v.lo, v.hi, "R", so, val])

    def emit(self, eng, fn, reads, writes, mark=True):
        self._sync(eng, reads, writes)
        ins = fn()
        so = eng.so
        if mark:
            ins.then_inc(so.sem, 1)
            so.cnt += 1
            val = so.cnt
        else:
            val = so.cnt + 1
        eng.last = ins
        self._record(reads, writes, so, val)
        return ins

    def dma(self, q, out, in_, dsem, out_ap=None, in_ap=None):
        self._sync(q, [in_], [out])
        oa = out.ap if out is not None else out_ap
        ia = in_.ap if in_ is not None else in_ap
        ins = q.h.dma_start(out=oa, in_=ia)
        ins.then_inc(dsem.sem, 16)
        dsem.cnt += 16
        self._record([in_], [out], dsem, dsem.cnt)
        return ins

    def mm(self, out, lhsT, rhs, start=True, stop=True, mark=False):
        return self.emit(self.PE, lambda: self.nc.tensor.matmul(out.ap, lhsT.ap, rhs.ap, start=start, stop=stop),
                         [lhsT, rhs], [out], mark=mark)

    def tr(self, out, in_, ident, mark=False):
        return self.emit(self.PE, lambda: self.nc.tensor.transpose(out.ap, in_.ap, ident.ap),
                         [in_, ident], [out], mark=mark)

    def act(self, out, in_, func, bias=None, scale=1.0, accum=None):
        rd = [in_]
        kw = {}
        if isinstance(bias, V):
            rd.append(bias)
            kw["bias"] = bias.ap
        elif bias is not None:
            kw["bias"] = float(bias)
        if isinstance(scale, V):
            rd.append(scale)
            kw["scale"] = scale.ap
        else:
            kw["scale"] = float(scale)
        wr = [out]
        if accum is not None:
            wr.append(accum)
            kw["accum_out"] = accum.ap
        return self.emit(self.ACT, lambda: self.nc.scalar.activation(out=out.ap, in_=in_.ap, func=func, **kw), rd, wr)

    def ts(self, eng, out, in_, s1, s2=None, op0=ALU.mult, op1=None):
        rd = [in_]
        a1 = s1
        a2 = s2
        if isinstance(s1, V):
            rd.append(s1)
            a1 = s1.ap
        if isinstance(s2, V):
            rd.append(s2)
            a2 = s2.ap
        if op1 is None:
            fn = lambda: eng.h.tensor_scalar(out.ap, in_.ap, a1, None, op0)
        else:
            fn = lambda: eng.h.tensor_scalar(out.ap, in_.ap, a1, a2, op0, op1)
        return self.emit(eng, fn, rd, [out])

    def tt(self, eng, out, a, b, op):
        return self.emit(eng, lambda: eng.h.tensor_tensor(out.ap, a.ap, b.ap, op), [a, b], [out])

    def stt(self, eng, out, in0, sc, in1, op0, op1):
        rd = [in0, in1]
        a = sc
        if isinstance(sc, V):
            rd.append(sc)
            a = sc.ap
        return self.emit(eng, lambda: eng.h.scalar_tensor_tensor(out.ap, in0.ap, a, in1.ap, op0, op1), rd, [out])

    def cp(self, eng, out, in_):
        if eng is self.ACT:
            return self.emit(eng, lambda: self.nc.scalar.copy(out.ap, in_.ap), [in_], [out])
        return self.emit(eng, lambda: eng.h.tensor_copy(out.ap, in_.ap), [in_], [out])

    def memset(self, eng, out, val):
        return self.emit(eng, lambda: eng.h.memset(out.ap, val), [], [out])


def piece_table():
    P = []

    def add(kind, **kw):
        kw["kind"] = kind
        P.append(kw)
        return len(P) - 1

    ids = {}
    c_a, c_b, c_z, c_xs, c_B, c_C, c_dt, c_gc, c_gs = 0, 1024, 2048, 4096, 6144, 6656, 7168, 7200, 8224
    for i in range(2):
        ids["a%d" % i] = add("w", w="w_in", k0=0, nk=8, segs=[(c_a + 512 * i, 512)])
        ids["b%d" % i] = add("w", w="w_in", k0=0, nk=8, segs=[(c_b + 512 * i, 512)])
    for i in range(8):
        ids["c31_%d" % i] = add("c31", fc=i)
    for i in range(2):
        ids["co%d" % i] = add("w", w="w_conv_out", k0=0, nk=8, segs=[(512 * i, 512)])
        ids["gc%d" % i] = add("w", w="w_in", k0=0, nk=8, segs=[(c_gc + 512 * i, 512)])
    for i in range(6):
        ids["c4_%d" % i] = add("c4", g=i)
    for i in range(4):
        ids["xs%d" % i] = add("w", w="w_in", k0=0, nk=8, segs=[(c_xs + 512 * i, 512)])
    ids["B"] = add("w", w="w_in", k0=0, nk=8, segs=[(c_B, 512)])
    ids["C"] = add("w", w="w_in", k0=0, nk=8, segs=[(c_C, 512)])
    ids["dt"] = add("w", w="w_in", k0=0, nk=8, segs=[(c_dt, 32)])
    for i in range(4):
        ids["z%d" % i] = add("w", w="w_in", k0=0, nk=8, segs=[(c_z + 512 * i, 512)])
    ids["dsk"] = add("dsk")
    for cg in range(2):
        for kh in range(2):
            ids["so%d%d" % (cg, kh)] = add("w", w="w_ssm_out", k0=8 * kh, nk=8, segs=[(512 * cg, 512)], rs="snw")
        ids["gs%d" % cg] = add("w", w="w_in", k0=0, nk=8, segs=[(c_gs + 512 * cg, 512)])
    for i in range(2):
        ids["wo%d" % i] = add("w", w="w_out", k0=0, nk=8, segs=[(512 * i, 512)], cs=("gate1", 512 * i))
    for m in range(11):
        ids["fi%d" % m] = add("w", w="w_ffn_in", k0=0, nk=8, segs=[(256 * m, 256), (DFF + 256 * m, 256)])
    for cg in range(2):
        for ks in range(3):
            nk = 8 if ks < 2 else 6
            ids["fd%d%d" % (cg, ks)] = add("w", w="w_ffn_down", k0=8 * ks, nk=nk, segs=[(512 * cg, 512)],
                                          cs=("gate2", 512 * cg))
    return P, ids


def tile_order(ids):
    o = ["a0", "b0", "a1", "b1"] + ["c31_%d" % i for i in range(8)] + ["z0", "z1", "z2", "z3", "co0", "gc0", "co1", "gc1"]
    o += ["c4_0", "xs0", "c4_1", "xs1", "c4_2", "xs2", "c4_3", "xs3", "c4_4", "B", "c4_5", "C", "dt", "dsk"]
    o += ["so00", "so01", "gs0", "so10", "so11", "gs1", "wo0", "wo1"]
    o += ["fi%d" % m for m in range(11)]
    o += ["fd00", "fd01", "fd02", "fd10", "fd11", "fd12"]
    return o


def build(NT):
    nc = bass.Bass("TRN2", target_bir_lowering=False)
    Lc = NT * TT
    dram = {}

    def din(name, shape):
        dram[name] = nc.dram_tensor(name, list(shape), F32, kind="ExternalInput").ap()
        return dram[name]

    x_d = din("x", [Lc, D])
    cT_d = din("cT", [128, 8])
    din("w_ada_mix", [D, 3072])
    din("w_ada_ffn", [D, 3072])
    din("bT_ada_mix", [128, 24])
    din("bT_ada_ffn", [128, 24])
    din("brow_ada_mix", [1, 3072])
    din("brow_ada_ffn", [1, 3072])
    din("nw1T", [128, 8])
    din("nw2T", [128, 8])
    din("w_in", [D, INC])
    din("w31T", [128, 8, 31])
    din("cb31T", [128, 8])
    din("lnwT", [128, 8])
    din("lnbT", [128, 8])
    din("w_conv_out", [D, D])
    din("w4T", [128, 24, 4])
    din("cb4T", [128, 24])
    din("cb4mat", [32, 128])
    din("dtb_bc", [128, 128])
    din("alog_bc", [128, 128])
    din("dsk_bc", [128, 32])
    din("snwT", [128, 16])
    din("w_ssm_out", [DIN, D])
    din("w_out", [D, D])
    din("w_ffn_in", [D, 2 * DFF])
    din("w_ffn_down", [DFF, D])
    din("fw_bc", [128, D])
    din("ident", [128, 128])
    din("tri", [128, 128])
    din("negmask4", [128, 512])
    din("causal4", [128, 512])
    din("sel", [64, 4096])
    y_d = nc.dram_tensor("y", [Lc, D], F32, kind="ExternalOutput").ap()

    pieces, ids = piece_table()
    NP = len(pieces)
    wscr = nc.dram_tensor("wscr", [NP, 128, 4096], BF16).ap()

    es = contextlib.ExitStack()
    with es:
      try:
        k = K(nc, es)
        PE, ACT, DVE, POOL, SP = k.PE, k.ACT, k.DVE, k.POOL, k.SP
        EW = [DVE, POOL]
        rr = [0]

        def ew():
            rr[0] += 1
            return EW[rr[0] % 2]

        def B(name, shape, dt, at=None):
            return Buf(k, name, shape, dt, at)

        ident_f = B("ident_f", [128, 128], F32)
        ident_b = B("ident_b", [128, 128], BF16)
        tri_f = B("tri_f", [128, 128], F32)
        ones_f = B("ones_f", [128, 128], F32)
        ones_b = B("ones_b", [128, 128], BF16)
        negm = B("negm", [128, 512], BF16)
        caus = B("caus", [128, 512], BF16)
        sel = B("sel", [64, 4096], BF16)
        fw_bc = B("fw_bc", [128, D], F32)
        dtb = B("dtb", [128, 128], F32)
        a_bc = B("a_bc", [128, 128], F32)
        cb4mat = B("cb4mat", [32, 128], BF16)
        prm = B("prm", [128, 160], F32)
        G1, SH1, G2, SH2, CB31, LNW, LNB, CB4, SNW = 0, 8, 16, 24, 32, 40, 48, 56, 80
        x_res = B("x_res", [128, NB, D], F32)
        hT = B("hT", [128, KC, TT], BF16)
        S = B("S", [128, DIN], F32)
        Sb = B("Sb", [128, DIN], BF16)
        uhalo = B("uhalo", [128, 8, 32], BF16)
        xhalo = B("xhalo", [128, 24, 4], BF16)
        slots = [B("slot%d" % i, [128, 8, 512], BF16) for i in range(NSLOT)]
        slot_sem = [k.newsem("slot") for _ in range(NSLOT)]
        arB = k.sb_alloc(33 * 1024)
        x_in = B("x_in", [128, NB, D], F32, at=arB)
        uT = B("uT", [128, 8, 544], BF16, at=arB)
        uv = B("uv", [128, 8, TT], BF16, at=arB + 8704)
        sgt = [B("sgt%d" % i, [128, TT], F32, at=arB + 16896 + 2048 * i) for i in range(2)]
        usq = [B("usq%d" % i, [128, TT], BF16, at=arB + 20992 + 1024 * i) for i in range(2)]
        mean_t = B("mean_t", [128, TT], F32, at=arB + 23040)
        rstd_t = B("rstd_t", [128, TT], F32, at=arB + 25088)
        msq_t = B("msq_t", [128, TT], F32, at=arB + 27136)
        lnt = [B("lnt%d" % i, [128, TT], F32, at=arB + 29184 + 2048 * i) for i in range(2)]
        assert 29184 + 4096 <= 33 * 1024
        ynT = B("ynT", [128, 16, TT], BF16, at=arB)
        yc = B("yc", [128, DIN], F32, at=arB + 16384)
        mg = B("mg", [128, KC, TT], BF16, at=arB + 24576)
        arC = k.sb_alloc(32 * 1024)
        xs = B("xs", [128, NB, DIN], BF16, at=arC)
        siluz = B("siluz", [128, NB, DIN], BF16, at=arC + 16384)
        actT = B("actT", [128, NFF, TT], BF16, at=arC)
        sl = [B("sl%d" % i, [128, TT], F32, at=arC + 22528 + 2048 * i) for i in range(2)]
        xb = [B("xb%d" % i, [128, D], BF16) for i in range(2)]
        junk = B("junk", [128, D], BF16)
        st4 = B("st4", [128, 16], F32)
        t1 = B("t1", [128, KC, TT], BF16)
        xst = [B("xst%d" % i, [128, 520], BF16) for i in range(3)]
        BT = B("BT", [128, NG, TT], BF16)
        CT = B("CT", [128, NG, TT], BF16)
        Btok = B("Btok", [128, NB, 512], BF16)
        dtp = B("dtp", [128, 128], F32)
        dtt = B("dtt", [128, 128], F32)
        lndt = B("lndt", [128, 128], F32)
        da2 = B("da2", [128, NB, 64], F32)
        biasA = B("biasA", [128, 128], F32)
        Ee = B("Ee", [128, 128], F32)
        wdt = B("wdt", [128, 128], F32)
        decA = B("decA", [128, 128], F32)
        tmpd = B("tmpd", [128, 128], F32)
        acsT = B("acsT", [64, 512], BF16)
        acsTt = B("acsTt", [64, 512], BF16)
        cbm = B("cbm", [128, NG, 128], BF16)
        expo = [B("expo%d" % i, [128, 8, 128], BF16) for i in range(2)] + [B("expo2", [128, 8, 128], BF16, at=arB + 16384 + 4096)]
        xw = [B("xw%d" % i, [128, 512], BF16) for i in range(2)] + [B("xw2", [128, 512], BF16, at=arB + 16384 + 4096 + 2048)]
        otmp = [B("otmp%d" % i, [128, 512], F32) for i in range(2)]
        yn = B("yn", [128, DIN], BF16)
        stg_f = [B("stgf%d" % i, [128, 8, 512], F32, at=arB + 16384 * i) for i in range(2)]
        stg_f += [B("stgf%d" % (2 + i), [128, 8, 512], F32, at=slots[2 * i].off) for i in range(2)]
        stg_b = [B("stgb%d" % i, [128, 8, 512], BF16, at=arC + 8192 * i) for i in range(4)]
        NSF, NSB = 4, 4
        gate_bc = [B("gate_bc%d" % i, [128, D], F32, at=x_res.off + 4096 * i) for i in range(2)]
        sc_rep = B("sc_rep", [128, KC, 128], F32, at=arC + 24576)
        scv = B("scv", [128, 8], F32)
        print("SBUF bytes/partition used:", k.sb_top)
        assert k.sb_top <= 224 * 1024 - 2048, k.sb_top

        psf = [PsBank(k, "psf%d" % i, i, F32) for i in range(6)]
        psb = [PsBank(k, "psb%d" % i, 6 + i, BF16) for i in range(2)]

        ld_sem = k.newsem("ld")
        stgf_sem = [k.newsem("stgf") for _ in range(4)]
        stgb_sem = [k.newsem("stgb") for _ in range(4)]
        st_sem = k.newsem("st")
        xin_sem = k.newsem("xin")
        xres_sem = k.newsem("xres")
        y_sem = k.newsem("ysem")

        def load(dst, src_ap, q=SP, sem=None):
            return k.dma(q, dst, None, sem or k.newsem("l"), in_ap=src_ap)

        c_stage = stg_f[0]
        for (dstb, name, np_, ncol) in [(ident_b, "ident", 128, 128), (ones_b, None, 128, 128), (negm, "negmask4", 128, 512),
                                        (caus, "causal4", 128, 512)]:
            if name is None:
                continue
            v = c_stage.cust(0, [[1, ncol]], 0, np_, 0, ncol)
            load(v, dram[name][:, :])
            k.cp(DVE, dstb[:, :], v)
        load(ident_f[:, :], dram["ident"][:, :])
        load(tri_f[:, :], dram["tri"][:, :])
        k.memset(DVE, ones_f[:, :], 1.0)
        k.memset(DVE, ones_b[:, :], 1.0)
        for h in range(4):
            v = c_stage.cust(0, [[1, 1024]], 0, 64, 0, 1024)
            load(v, dram["sel"][:, h * 1024:(h + 1) * 1024])
            k.cp(DVE, sel[:, h * 1024:(h + 1) * 1024], v)
        v = c_stage.cust(0, [[1, 128]], 0, 32, 0, 128)
        load(v, dram["cb4mat"][:, :])
        k.cp(DVE, cb4mat[:, :], v)
        load(fw_bc[:, :], dram["fw_bc"][:, :])
        load(dtb[:, :], dram["dtb_bc"][:, :])
        load(a_bc[:, :], dram["alog_bc"][:, :])
        k.act(a_bc[:, :], a_bc[:, :], AF.Exp)
        k.ts(DVE, a_bc[:, :], a_bc[:, :], -1.0)
        for (col, name, n) in [(CB31, "cb31T", 8), (LNW, "lnwT", 8), (LNB, "lnbT", 8), (CB4, "cb4T", 24), (SNW, "snwT", 16)]:
            load(prm[:, col:col + n], dram[name][:, :])
        w31 = B("w31", [128, 8, 31], F32)
        w4 = B("w4", [128, 24, 4], F32)
        dskb = B("dskb", [128, 32], F32)
        load(w31[:, :, :], dram["w31T"][:, :, :])
        load(w4[:, :, :], dram["w4T"][:, :, :])
        load(dskb[:, :], dram["dsk_bc"][:, :])
        k.memset(DVE, S[:, :], 0.0)
        k.memset(DVE, Sb[:, :], 0.0)
        k.memset(DVE, uhalo[:, :, :], 0.0)
        k.memset(DVE, xhalo[:, :, :], 0.0)

        if DBG_STOP == "s0":
            raise _Stop()
        load(scv[:, :], cT_d[:, :])
        k.act(scv[:, :], scv[:, :], AF.Silu)
        for kc in range(KC):
            k.ts(DVE, sc_rep[:, kc, :], ones_f[:, :], scv[:, kc:kc + 1])
        nwt = B("nwt", [128, 16], F32)
        load(nwt[:, 0:8], dram["nw1T"][:, :])
        load(nwt[:, 8:16], dram["nw2T"][:, :])
        bTt = B("bTt", [128, 48], F32)
        load(bTt[:, 0:24], dram["bT_ada_mix"][:, :])
        load(bTt[:, 24:48], dram["bT_ada_ffn"][:, :])
        browt = B("browt", [1, 2048], F32, at=x_res.off + 8192)
        rowsb = B("rowsb", [1, 512], F32, at=hT.off)
        load(browt[:, 0:1024], dram["brow_ada_mix"][:, 2048:3072])
        load(browt[:, 1024:2048], dram["brow_ada_ffn"][:, 2048:3072])
        for li, wn in enumerate(["w_ada_mix", "w_ada_ffn"]):
            wv = dram[wn].rearrange("(kc p) f -> p kc f", p=128)
            gcol, scol = (G1, SH1) if li == 0 else (G2, SH2)
            for pc in range(6):
                sf = stg_f[pc % 2]
                load(sf[:, :, :], wv[:, :, pc * 512:(pc + 1) * 512], sem=stgf_sem[pc % 2])
                if pc < 4:
                    pr = psf[4 + pc % 2]
                    for kc in range(KC):
                        k.mm(pr[0:1, :], scv[:, kc:kc + 1], sf[:, kc, :], start=(kc == 0), stop=(kc == KC - 1), mark=(kc == KC - 1))
                    k.cp(DVE, rowsb[0:1, :], pr[0:1, :])
                    ps = psf[pc % 2]
                    for fcl in range(4):
                        k.mm(ps[:, fcl:fcl + 1], rowsb[0:1, fcl * 128:(fcl + 1) * 128], ones_f[0:1, 0:1], start=True, stop=True,
                             mark=(fcl == 3))
                    fc0 = pc * 4
                    if pc < 2:
                        k.tt(DVE, prm[:, scol + fc0:scol + fc0 + 4], ps[:, 0:4], bTt[:, li * 24 + fc0:li * 24 + fc0 + 4], ALU.add)
                    else:
                        f8 = fc0 - 8
                        k.tt(DVE, st4[:, 0:4], ps[:, 0:4], bTt[:, li * 24 + fc0:li * 24 + fc0 + 4], ALU.add)
                        k.stt(DVE, prm[:, gcol + f8:gcol + f8 + 4], st4[:, 0:4], 1.0, nwt[:, li * 8 + f8:li * 8 + f8 + 4],
                              ALU.add, ALU.mult)
                else:
                    ps = psf[2 + pc % 2]
                    h = pc - 4
                    for kc in range(KC):
                        k.mm(ps[:, :], sc_rep[:, kc, :], sf[:, kc, :], start=(kc == 0), stop=False)
                    k.mm(ps[:, :], ones_f[0:1, :], browt[0:1, li * 1024 + h * 512:li * 1024 + (h + 1) * 512], start=False,
                         stop=True, mark=True)
                    k.cp(DVE, gate_bc[li][:, h * 512:(h + 1) * 512], ps[:, :])

        if DBG_STOP == "s1":
            raise _Stop()
        ceng = [DVE, ACT, DVE]

        def diag(j, dst, sc):
            if j % 2 == 0:
                k.ts(DVE, dst, ident_f[:, :], sc)
            else:
                k.act(dst, ident_f[:, :], AF.Identity, scale=sc)
        stq = POOL
        w_ids = [i for i, p_ in enumerate(pieces) if p_["kind"] == "w"]
        d_ids = [i for i, p_ in enumerate(pieces) if p_["kind"] != "w"]
        prep_order = []
        while w_ids or d_ids:
            prep_order += w_ids[:3]
            w_ids = w_ids[3:]
            if d_ids:
                prep_order.append(d_ids.pop(0))
        fcnt = [0]
        piece_done = {}
        for sb0 in stg_b:
            k.memset(DVE, sb0[:, :, :], 0.0)
        for bi, pi in enumerate(prep_order):
            pc = pieces[pi]
            sb_ = stg_b[bi % NSB]
            if pc["kind"] == "w":
                fi_ = fcnt[0] % NSF
                fcnt[0] += 1
                sf = stg_f[fi_]
                wv = dram[pc["w"]].rearrange("(kc p) f -> p kc f", p=128)
                nk = pc["nk"]
                c = 0
                ncols = sum(s[1] for s in pc["segs"])
                for (c0, n) in pc["segs"]:
                    load(sf[:, 0:nk, c:c + n], wv[:, pc["k0"]:pc["k0"] + nk, c0:c0 + n], sem=stgf_sem[fi_])
                    c += n
                eng = ceng[bi % 3]
                if "rs" in pc:
                    base = {"g1": G1, "g2": G2, "snw": SNW + pc["k0"]}[pc["rs"]]
                    for kc in range(nk):
                        if eng is ACT:
                            k.act(sb_[:, kc, 0:ncols], sf[:, kc, 0:ncols], AF.Identity, scale=prm[:, base + kc:base + kc + 1])
                        else:
                            k.ts(eng, sb_[:, kc, 0:ncols], sf[:, kc, 0:ncols], prm[:, base + kc:base + kc + 1])
                elif "cs" in pc:
                    gb = gate_bc[0] if pc["cs"][0] == "gate1" else gate_bc[1]
                    c0 = pc["cs"][1]
                    e2 = DVE if eng is ACT else eng
                    gv = gb.cust(c0, [[0, nk], [1, 512]], 0, 128, c0, c0 + 512)
                    k.tt(e2, sb_[:, 0:nk, :], sf[:, 0:nk, :], gv, ALU.mult)
                else:
                    k.cp(eng, sb_[:, 0:nk, :], sf[:, 0:nk, :])
            elif pc["kind"] == "c31":
                fc = pc["fc"]
                sflat = sb_.cust(0, [[1, 4096]])
                for j in range(31):
                    diag(j, sb_.cust(j * 128, [[1, 128]], 0, 128, j * 128, (j + 1) * 128), w31[:, fc, j:j + 1])
            elif pc["kind"] == "c4":
                for cl in range(4):
                    ch = pc["g"] * 4 + cl
                    for j in range(4):
                        diag(j, sb_[:, cl, j * 128:(j + 1) * 128], w4[:, ch, j:j + 1])
            elif pc["kind"] == "dsk":
                for j in range(NH):
                    diag(j, sb_.cust(j * 128, [[1, 128]], 0, 128, j * 128, (j + 1) * 128), dskb[:, j:j + 1])
            k.dma(stq, None, sb_[:, :, :], stgb_sem[bi % NSB], out_ap=wscr[pi].rearrange("p (a b) -> p a b", b=512))
            piece_done[pi] = (stgb_sem[bi % NSB], stgb_sem[bi % NSB].cnt)

        if DBG_STOP == "s2":
            raise _Stop()
        order = tile_order(ids)
        NPT = len(order)
        wstate = {"next": 0}
        total_loads = NT * NPT

        def issue_loads(upto):
            while wstate["next"] < min(upto, total_loads):
                n = wstate["next"]
                pid = ids[order[n % NPT]]
                s = n % NSLOT
                if n < NPT:
                    so_, v_ = piece_done[pid]
                    if SP.waited.get(("st", so_.id), 0) < v_:
                        SP.h.wait_ge(so_.sem, v_)
                        SP.waited[("st", so_.id)] = v_
                k.dma(SP, slots[s][:, :, :], None, slot_sem[s], in_ap=wscr[pid].rearrange("p (a b) -> p a b", b=512))
                wstate["next"] += 1

        wpos = {"n": 0}

        def W(name):
            n = wpos["n"]
            assert order[n % NPT] == name, (order[n % NPT], name)
            issue_loads(n + NSLOT - 1)
            wpos["n"] += 1
            return slots[n % NSLOT]

        def normA(src):
            for q in range(NB):
                k.act(junk[:, :], src[:, q, :], AF.Square, accum=st4[:, q:q + 1])
            k.act(st4[:, 4:8], st4[:, 0:4], AF.Ln, bias=EPS, scale=1.0 / D)
            k.act(st4[:, 4:8], st4[:, 4:8], AF.Exp, scale=-0.5)
            for q in range(2):
                k.ts(DVE, xb[q % 2][:, :], src[:, q, :], st4[:, 4 + q:5 + q])

        def normB(src, gcol, scol):
            for q in range(NB):
                if q >= 2:
                    k.ts(DVE, xb[q % 2][:, :], src[:, q, :], st4[:, 4 + q:5 + q])
                for kc in range(KC):
                    pb = psb[q % 2]
                    k.tr(pb[:, kc * 128:(kc + 1) * 128], xb[q % 2][:, kc * 128:(kc + 1) * 128], ident_b[:, :], mark=(kc == KC - 1))
                for kc in range(KC):
                    e = DVE if q % 2 == 0 else ACT
                    o = hT[:, kc, q * 128:(q + 1) * 128]
                    i = psb[q % 2][:, kc * 128:(kc + 1) * 128]
                    if e is DVE:
                        k.ts(DVE, o, i, prm[:, gcol + kc:gcol + kc + 1], prm[:, scol + kc:scol + kc + 1], ALU.mult, ALU.add)
                    else:
                        k.act(o, i, AF.Identity, bias=prm[:, scol + kc:scol + kc + 1], scale=prm[:, gcol + kc:gcol + kc + 1])

        def norm_block(src, q, gcol, scol):
            k.act(junk[:, :], src[:, q, :], AF.Square, accum=st4[:, q:q + 1])
            k.act(st4[:, 4 + q:5 + q], st4[:, q:q + 1], AF.Ln, bias=EPS, scale=1.0 / D)
            k.act(st4[:, 4 + q:5 + q], st4[:, 4 + q:5 + q], AF.Exp, scale=-0.5)
            k.ts(DVE, xb[q % 2][:, :], src[:, q, :], st4[:, 4 + q:5 + q])

        def norm_block2(q, gcol, scol):
            for kc in range(KC):
                k.tr(psb[q % 2][:, kc * 128:(kc + 1) * 128], xb[q % 2][:, kc * 128:(kc + 1) * 128], ident_b[:, :], mark=(kc == KC - 1))
            for kc in range(KC):
                o = hT[:, kc, q * 128:(q + 1) * 128]
                i = psb[q % 2][:, kc * 128:(kc + 1) * 128]
                if q % 2 == 0:
                    k.ts(DVE, o, i, prm[:, gcol + kc:gcol + kc + 1], prm[:, scol + kc:scol + kc + 1], ALU.mult, ALU.add)
                else:
                    k.act(o, i, AF.Identity, bias=prm[:, scol + kc:scol + kc + 1], scale=prm[:, gcol + kc:gcol + kc + 1])

        def rmsnorm_to_hT(src, gcol, scol):
            normA(src)
            normB(src, gcol, scol)

        xv = x_d.rearrange("(t q p) d -> t p q d", p=128, q=NB)
        yv = y_d.rearrange("(t q p) d -> t p q d", p=128, q=NB)

        k.dma(SP, x_in[:, :, :], None, xin_sem, in_ap=xv[0])
        for ti in range(NT):
            if ti == 0:
                rmsnorm_to_hT(x_in, G1, SH1)
            if DBG_STOP == "p1":
                break
            k.cp(POOL, uT[:, :, 0:30], uhalo[:, :, 0:30])
            for half in range(2):
                wa = W("a%d" % half)
                wb = W("b%d" % half)
                for fl in range(4):
                    fc = half * 4 + fl
                    pa = psf[(fc % 3) * 2]
                    pb_ = psf[(fc % 3) * 2 + 1]
                    for kc in range(KC):
                        k.mm(pa[:, :], wa[:, kc, fl * 128:(fl + 1) * 128], hT[:, kc, :], start=(kc == 0), stop=(kc == KC - 1))
                    for kc in range(KC):
                        k.mm(pb_[:, :], wb[:, kc, fl * 128:(fl + 1) * 128], hT[:, kc, :], start=(kc == 0), stop=(kc == KC - 1),
                             mark=(kc == KC - 1))
                    k.act(sgt[fc % 2][:, :], pb_[:, :], AF.Sigmoid)
                    k.tt(DVE, uT[:, fc, 30:30 + TT], pa[:, :], sgt[fc % 2][:, :], ALU.mult)
            k.cp(POOL, uhalo[:, :, 0:30], uT[:, :, TT:TT + 30])
            if DBG_STOP == "p2a":
                break
            pS, pQ = psf[4], psf[5]

            def ln_stats(fc_):
                k.mm(pS[:, :], ones_b[:, :], uv[:, fc_, :], start=(fc_ == 0), stop=(fc_ == 7))
                k.mm(pQ[:, :], ones_b[:, :], usq[fc_ % 2][:, :], start=(fc_ == 0), stop=(fc_ == 7), mark=True)

            for fc in range(8):
                wd = W("c31_%d" % fc)
                pu = psf[fc % 4]
                acc = lnt[fc % 2]
                k.ts(DVE, acc[:, :], uT[:, fc, 0:TT], w31[:, fc, 0:1])
                for j in range(1, NDVE):
                    k.stt(DVE, acc[:, :], uT[:, fc, j:j + TT], w31[:, fc, j:j + 1], acc[:, :], ALU.mult, ALU.add)
                for j in range(NDVE, 31):
                    k.mm(pu[:, :], wd.cust(j * 128, [[1, 128]], 0, 128, j * 128, (j + 1) * 128), uT[:, fc, j:j + TT],
                         start=(j == NDVE), stop=(j == 30), mark=(j == 30))
                k.stt(DVE, uv[:, fc, :], pu[:, :], prm[:, CB31 + fc:CB31 + fc + 1], acc[:, :], ALU.add, ALU.add)
                k.act(usq[fc % 2][:, :], uv[:, fc, :], AF.Square)
                if fc > 0:
                    ln_stats(fc - 1)
            ln_stats(7)
            k.ts(DVE, mean_t[:, :], pS[:, :], 1.0 / D)
            k.tt(DVE, msq_t[:, :], mean_t[:, :], mean_t[:, :], ALU.mult)
            k.stt(DVE, rstd_t[:, :], pQ[:, :], 1.0 / D, msq_t[:, :], ALU.mult, ALU.subtract)
            k.act(rstd_t[:, :], rstd_t[:, :], AF.Ln, bias=LN_EPS)
            k.act(rstd_t[:, :], rstd_t[:, :], AF.Exp, scale=-0.5)
            def zproj(zi):
                wz = W("z%d" % zi)
                for q in range(NB):
                    pz = psf[(zi * NB + q) % 6]
                    for kc in range(KC):
                        k.mm(pz[:, :], hT[:, kc, q * 128:(q + 1) * 128], wz[:, kc, :], start=(kc == 0), stop=(kc == KC - 1),
                             mark=(kc == KC - 1))
                    k.act(siluz[:, q, zi * 512:(zi + 1) * 512], pz[:, :], AF.Silu)

            zproj(0)
            zproj(1)
            for fc in range(8):
                lt = lnt[fc % 2]
                k.tt(DVE, lt[:, :], uv[:, fc, :], mean_t[:, :], ALU.subtract)
                k.tt(POOL, lt[:, :], lt[:, :], rstd_t[:, :], ALU.mult)
                k.act(uv[:, fc, :], lt[:, :], AF.Silu, bias=prm[:, LNB + fc:LNB + fc + 1], scale=prm[:, LNW + fc:LNW + fc + 1])
            zproj(2)
            zproj(3)
            if DBG_STOP == "p2b":
                break
            for half in range(2):
                wc = W("co%d" % half)
                wg = W("gc%d" % half)
                for fl in range(4):
                    fo = half * 4 + fl
                    py = psf[(fo % 3) * 2]
                    pg = psf[(fo % 3) * 2 + 1]
                    for c in range(KC):
                        k.mm(py[:, :], wc[:, c, fl * 128:(fl + 1) * 128], uv[:, c, :], start=(c == 0), stop=(c == KC - 1))
                    for kc in range(KC):
                        k.mm(pg[:, :], wg[:, kc, fl * 128:(fl + 1) * 128], hT[:, kc, :], start=(kc == 0), stop=(kc == KC - 1),
                             mark=(kc == KC - 1))
                    k.act(sgt[fo % 2][:, :], pg[:, :], AF.Sigmoid)
                    k.tt(DVE, t1[:, fo, :], py[:, :], sgt[fo % 2][:, :], ALU.mult)
            if DBG_STOP == "p2c":
                break
            xw_ = {}

            def xproj(ch):
                if ch % 4 == 0:
                    xw_["c4"] = W("c4_%d" % (ch // 4))
                    if ch < 16:
                        xw_["x"] = W("xs%d" % (ch // 4))
                    elif ch == 16:
                        xw_["x"] = W("B")
                    else:
                        xw_["x"] = W("C")
                wx = xw_["x"]
                fl = ch % 4
                px = psf[ch % 2]
                for kc in range(KC):
                    k.mm(px[:, :], wx[:, kc, fl * 128:(fl + 1) * 128], hT[:, kc, :], start=(kc == 0), stop=(kc == KC - 1),
                         mark=(kc == KC - 1))
                st = xst[ch % 3]
                k.cp(POOL, st[:, 0:3], xhalo[:, ch, 0:3])
                if ch % 2 == 0:
                    k.cp(DVE, st[:, 3:3 + TT], px[:, :])
                else:
                    k.cp(ACT, st[:, 3:3 + TT], px[:, :])
                k.cp(POOL, xhalo[:, ch, 0:3], st[:, TT:TT + 3])

            def xconv(ch):
                wc4 = xw_["c4"]
                fl = ch % 4
                cl = ch % 4
                st = xst[ch % 3]
                if ch < 20:
                    for q in range(NB):
                        pt = psf[2 + q]
                        for j in range(4):
                            k.mm(pt[:, fl * 128:(fl + 1) * 128], st[:, q * 128 + j:q * 128 + j + 128],
                                 wc4[:, cl, j * 128:(j + 1) * 128], start=(j == 0), stop=False)
                        k.mm(pt[:, fl * 128:(fl + 1) * 128], sel[0:32, ch * 128:(ch + 1) * 128], cb4mat[0:32, :], start=False,
                             stop=True, mark=(fl == 3))
                    if fl == 3:
                        for q in range(NB):
                            if ch < 16:
                                k.act(xs[:, q, (ch // 4) * 512:(ch // 4 + 1) * 512], psf[2 + q][:, :], AF.Silu)
                            else:
                                k.act(Btok[:, q, :], psf[2 + q][:, :], AF.Silu)
                if ch >= 16:
                    pf = psf[ch % 2]
                    for j in range(4):
                        k.mm(pf[:, :], wc4[:, cl, j * 128:(j + 1) * 128], st[:, j:j + TT], start=(j == 0), stop=(j == 3),
                             mark=(j == 3))
                    dst = BT if ch < 20 else CT
                    k.act(dst[:, ch % 4, :], pf[:, :], AF.Silu, bias=prm[:, CB4 + ch:CB4 + ch + 1])

            for ch in range(24):
                if ch % 4 == 0:
                    xproj(ch)
                if ch % 4 != 3:
                    xproj(ch + 1)
                xconv(ch)
            if DBG_STOP == "p2d":
                break
            wdtp = W("dt")
            pD, pA, pAT, pTot = psf[0], psf[1], psf[2], psf[3]
            for q in range(NB):
                for kc in range(KC):
                    k.mm(pD[:, q * 32:(q + 1) * 32], hT[:, kc, q * 128:(q + 1) * 128], wdtp[:, kc, 0:32], start=(kc == 0),
                         stop=(kc == KC - 1), mark=(kc == KC - 1 and q == NB - 1))
            k.tt(DVE, dtp[:, :], pD[:, 0:128], dtb[:, :], ALU.add)
            k.act(dtt[:, :], dtp[:, :], AF.Exp)
            k.act(dtt[:, :], dtt[:, :], AF.Ln, bias=1.0)
            k.act(lndt[:, :], dtt[:, :], AF.Ln)
            for q in range(NB):
                for r in range(2):
                    k.tt(DVE, da2[:, q, r * 32:(r + 1) * 32], dtt[:, q * 32:(q + 1) * 32], a_bc[:, q * 32:(q + 1) * 32], ALU.mult)
            for q in range(NB):
                k.mm(pA[:, q * 32:(q + 1) * 32], tri_f[:, :], da2[:, q, 0:32], start=True, stop=True)
                k.mm(pTot[:, q * 32:(q + 1) * 32], ones_f[:, :], da2[:, q, 0:32], start=True, stop=True)
                k.mm(pAT[0:64, q * 128:(q + 1) * 128], da2[:, q, :], tri_f[:, :], start=True, stop=True, mark=(q == NB - 1))
            k.tt(DVE, biasA[:, :], lndt[:, :], pA[:, 0:128], ALU.subtract)
            k.act(Ee[:, :], pA[:, 0:128], AF.Exp)
            k.tt(DVE, tmpd[:, :], pTot[:, 0:128], biasA[:, :], ALU.add)
            k.act(wdt[:, :], tmpd[:, :], AF.Exp)
            k.act(decA[:, :], pTot[:, 0:128], AF.Exp)
            k.cp(DVE, acsT[0:32, :], pAT[0:32, :])
            k.cp(DVE, acsTt[32:64, :], pAT[32:64, :])
            k.tt(DVE, acsT[32:64, :], pAT[32:64, :], acsTt[32:64, :], ALU.subtract)
            if DBG_STOP == "p2e":
                break
            if DBG_STOP == "p2f":
                break
            wdsk = W("dsk")

            def ssd_front(si, q, g):
                if g == 0:
                    pC = psf[0]
                    for g2 in range(NG):
                        k.mm(pC[:, g2 * 128:(g2 + 1) * 128], BT[:, g2, q * 128:(q + 1) * 128], CT[:, g2, q * 128:(q + 1) * 128],
                             start=True, stop=True, mark=(g2 == NG - 1))
                    k.tt(DVE, cbm.cust(0, [[1, 512]]), pC[:, :], caus[:, :], ALU.mult)
                ex = expo[si % 3]
                for hh in range(2):
                    pR = psf[1 + hh]
                    k.mm(pR[:, :], ident_b[:, :], negm[:, :], start=True, stop=False)
                    for jl in range(4):
                        j = g * 8 + hh * 4 + jl
                        k.mm(pR[:, jl * 128:(jl + 1) * 128], sel[0:64, j * 128:(j + 1) * 128], acsT[0:64, q * 128:(q + 1) * 128],
                             start=False, stop=(jl == 3), mark=(jl == 3))
                    for jl in range(4):
                        j = g * 8 + hh * 4 + jl
                        k.act(ex[:, hh * 4 + jl, :], pR[:, jl * 128:(jl + 1) * 128], AF.Exp,
                              bias=biasA[:, q * 32 + j:q * 32 + j + 1])

            def ssd_front_b(si, q, g):
                ex = expo[si % 3]
                cbv = cbm.cust(g * 128, [[0, 8], [1, 128]], 0, 128, g * 128, (g + 1) * 128)
                k.tt(POOL, ex[:, :, :], ex[:, :, :], cbv, ALU.mult)
                wv_ = wdt.cust(q * 32 + g * 8, [[1, 8], [0, 64]], 0, 128, q * 32 + g * 8, q * 32 + g * 8 + 8)
                k.tt(POOL, xw[si % 3].cust(0, [[64, 8], [1, 64]]), xs.cust(q * DIN + g * 512, [[64, 8], [1, 64]], 0, 128,
                                                                     q * DIN + g * 512, q * DIN + (g + 1) * 512), wv_, ALU.mult)

            def ssd_back(si, q, g):
                ex = expo[si % 3]
                pY, pO, pDl = psf[3], psf[4], psf[5]
                k.mm(pO[:, :], CT[:, g, q * 128:(q + 1) * 128], Sb[:, g * 512:(g + 1) * 512], start=True, stop=True, mark=True)
                k.mm(pDl[:, :], Btok[:, q, g * 128:(g + 1) * 128], xw[si % 3][:, :], start=True, stop=True, mark=True)
                for jl in range(8):
                    j = g * 8 + jl
                    xsl = xs[:, q, j * 64:(j + 1) * 64]
                    k.mm(pY[:, jl * 64:(jl + 1) * 64], ex[:, jl, :], xsl, start=True, stop=False)
                    k.mm(pY[:, jl * 64:(jl + 1) * 64], wdsk.cust(j * 128, [[1, 128]], 0, 128, j * 128, (j + 1) * 128), xsl,
                         start=False, stop=True, mark=(jl == 7))
                ev = Ee.cust(q * 32 + g * 8, [[1, 8], [0, 64]], 0, 128, q * 32 + g * 8, q * 32 + g * 8 + 8)
                ot = otmp[g % 2]
                ycg = yc[:, (g % 2) * 512:(g % 2 + 1) * 512]
                k.tt(DVE, ot.cust(0, [[64, 8], [1, 64]]), pO.cust(0, [[64, 8], [1, 64]]), ev, ALU.mult)
                k.tt(DVE, ycg, pY[:, :], ot[:, :], ALU.add)
                k.tt(DVE, ycg, ycg, siluz[:, q, g * 512:(g + 1) * 512], ALU.mult)
                k.tt(DVE, S[:, g * 512:(g + 1) * 512], S[:, g * 512:(g + 1) * 512], pDl[:, :], ALU.add)
                k.cp(DVE, Sb[:, g * 512:(g + 1) * 512], S[:, g * 512:(g + 1) * 512])

            def ssd_back2(si, q, g):
                ycg = yc[:, (g % 2) * 512:(g % 2 + 1) * 512]
                k.act(junk[:, 0:512], ycg, AF.Square, accum=st4[:, 8 + g:9 + g])
                k.act(st4[:, 12 + g:13 + g], st4[:, 8 + g:9 + g], AF.Ln, bias=EPS, scale=1.0 / 512)
                k.act(st4[:, 12 + g:13 + g], st4[:, 12 + g:13 + g], AF.Exp, scale=-0.5)
                yng = yn[:, (g % 2) * 512:(g % 2 + 1) * 512]
                k.ts(DVE, yng, ycg, st4[:, 12 + g:13 + g])

            def ssd_back3(si, q, g):
                pb = psb[g % 2]
                for c4 in range(4):
                    k.tr(pb[:, c4 * 128:(c4 + 1) * 128], yn[:, (g % 2) * 512 + c4 * 128:(g % 2) * 512 + (c4 + 1) * 128], ident_b[:, :],
                         mark=(c4 == 3))
                dstv = ynT.cust(g * 4 * TT + q * 128, [[TT, 4], [1, 128]], 0, 128, g * 4 * TT, (g + 1) * 4 * TT)
                srcv = pb.cust(0, [[128, 4], [1, 128]])
                k.cp(ACT, dstv, srcv)

            def ssd_back_pool(si, q, g):
                dv = decA.cust(q * 32 + g * 8, [[1, 8], [0, 64]], 0, 128, q * 32 + g * 8, q * 32 + g * 8 + 8)
                Sg = S.cust(g * 512, [[64, 8], [1, 64]], 0, 128, g * 512, (g + 1) * 512)
                k.tt(POOL, Sg, Sg, dv, ALU.mult)

            steps = [(q, g) for q in range(NB) for g in range(NG)]
            for s0 in range(2):
                ssd_front(s0, *steps[s0])
                ssd_front_b(s0, *steps[s0])
            for si, (q, g) in enumerate(steps):
                if si + 2 < len(steps):
                    ssd_front(si + 2, *steps[si + 2])
                ssd_back_pool(si, q, g)
                if si + 2 < len(steps):
                    ssd_front_b(si + 2, *steps[si + 2])
                ssd_back(si, q, g)
                if si >= 1:
                    ssd_back2(si - 1, *steps[si - 1])
                if si >= 2:
                    ssd_back3(si - 2, *steps[si - 2])
            nS = len(steps)
            ssd_back2(nS - 1, *steps[nS - 1])
            ssd_back3(nS - 2, *steps[nS - 2])
            ssd_back3(nS - 1, *steps[nS - 1])
            if DBG_STOP == "p3":
                break
            k.dma(SP, x_res[:, :, :], None, xres_sem, in_ap=xv[ti])
            for cg in range(2):
                for kh in range(2):
                    ws = W("so%d%d" % (cg, kh))
                    for fl in range(4):
                        for c8 in range(8):
                            k.mm(psf[fl][:, :], ws[:, c8, fl * 128:(fl + 1) * 128], ynT[:, kh * 8 + c8, :],
                                 start=(kh == 0 and c8 == 0), stop=(kh == 1 and c8 == 7), mark=(kh == 1 and c8 == 7))
                wg = W("gs%d" % cg)
                for fl in range(4):
                    fo = cg * 4 + fl
                    pg = psf[4 + fl % 2]
                    for kc in range(KC):
                        k.mm(pg[:, :], wg[:, kc, fl * 128:(fl + 1) * 128], hT[:, kc, :], start=(kc == 0), stop=(kc == KC - 1),
                             mark=(kc == KC - 1))
                    k.act(sl[fl % 2][:, :], pg[:, :], AF.Sigmoid)
                    k.tt(DVE, sl[fl % 2][:, :], psf[fl][:, :], sl[fl % 2][:, :], ALU.mult)
                    k.tt(ew(), mg[:, fo, :], sl[fl % 2][:, :], t1[:, fo, :], ALU.add)
            wos = [W("wo0"), W("wo1")]
            for q in range(NB):
                for pc_ in range(2):
                    wo = wos[pc_]
                    pw = psf[(q * 2 + pc_) % 6]
                    for kc in range(KC):
                        k.mm(pw[:, :], mg[:, kc, q * 128:(q + 1) * 128], wo[:, kc, :], start=(kc == 0), stop=(kc == KC - 1),
                             mark=(kc == KC - 1))
                    k.tt(DVE, x_res[:, q, pc_ * 512:(pc_ + 1) * 512], x_res[:, q, pc_ * 512:(pc_ + 1) * 512], pw[:, :], ALU.add)
                norm_block(x_res, q, G2, SH2)
                if q >= 1:
                    norm_block2(q - 1, G2, SH2)
            norm_block2(NB - 1, G2, SH2)
            if ti + 1 < NT:
                k.dma(SP, x_in[:, :, :], None, xin_sem, in_ap=xv[ti + 1])
            if DBG_STOP == "p4":
                break
            for m in range(11):
                wf = W("fi%d" % m)
                for il in range(2):
                    i = 2 * m + il
                    pgt = psf[(i % 3) * 2]
                    pup = psf[(i % 3) * 2 + 1]
                    for kc in range(KC):
                        k.mm(pgt[:, :], wf[:, kc, il * 128:(il + 1) * 128], hT[:, kc, :], start=(kc == 0), stop=(kc == KC - 1),
                             mark=(kc == KC - 1))
                    for kc in range(KC):
                        k.mm(pup[:, :], wf[:, kc, 256 + il * 128:256 + (il + 1) * 128], hT[:, kc, :], start=(kc == 0),
                             stop=(kc == KC - 1), mark=(kc == KC - 1))
                    k.act(sl[i % 2][:, :], pgt[:, :], AF.Silu)
                    k.tt(DVE, actT[:, i, :], sl[i % 2][:, :], pup[:, :], ALU.mult)
            nxt = (ti + 1 < NT) and DBG_STOP is None
            if nxt:
                normA(x_in)
            for cg in range(2):
                if cg == 1 and nxt:
                    normB(x_in, G1, SH1)
                for ks in range(3):
                    wd_ = W("fd%d%d" % (cg, ks))
                    nk = 8 if ks < 2 else 6
                    for q in range(NB):
                        for c8 in range(nk):
                            c = ks * 8 + c8
                            k.mm(psf[q][:, :], actT[:, c, q * 128:(q + 1) * 128], wd_[:, c8, :], start=(c == 0), stop=(c == NFF - 1),
                                 mark=(c == NFF - 1))
                for q in range(NB):
                    k.tt(DVE, x_res[:, q, cg * 512:(cg + 1) * 512], x_res[:, q, cg * 512:(cg + 1) * 512], psf[q][:, :], ALU.add)
            if DBG_STOP == "p5":
                break
            for q in range(NB):
                k.act(junk[:, :], x_res[:, q, :], AF.Square, accum=st4[:, q:q + 1])
            k.act(st4[:, 4:8], st4[:, 0:4], AF.Ln, bias=EPS, scale=1.0 / D)
            k.act(st4[:, 4:8], st4[:, 4:8], AF.Exp, scale=-0.5)
            for q in range(NB):
                k.stt(DVE, x_res[:, q, :], x_res[:, q, :], st4[:, 4 + q:5 + q], fw_bc[:, :], ALU.mult, ALU.mult)
            k.dma(SP, None, x_res[:, :, :], y_sem, out_ap=yv[ti])
        SP.h.wait_ge(y_sem.sem, y_sem.cnt)
      except _Stop:
        pass
    return nc


def host_consts():
    ident = np.eye(128, dtype=np.float32)
    s = np.arange(128)[:, None]
    l = np.arange(128)[None, :]
    tri = (s <= l).astype(np.float32)
    neg1 = np.where(l < s, -30000.0, 0.0).astype(np.float32)
    caus1 = (l >= s).astype(np.float32)
    negmask4 = np.tile(neg1, (1, 4))
    causal4 = np.tile(caus1, (1, 4))
    sel = np.zeros((64, 32, 128), np.float32)
    for j in range(32):
        sel[j, j, :] = 1.0
        sel[32 + j, j, :] = 1.0
    return {"ident": ident, "tri": tri, "negmask4": negmask4, "causal4": causal4, "sel": sel.reshape(64, 4096)}


def host_layout(inp, b, Lc):
    f = lambda a: np.ascontiguousarray(np.asarray(a, dtype=np.float32))
    pT = lambda v, n: f(np.asarray(v).reshape(n, 128).T)
    m = {}
    m["x"] = f(inp["x"][b, :Lc])
    m["cT"] = pT(inp["c"][b], 8)
    m["w_ada_mix"] = f(inp["w_ada_mix"][0])
    m["w_ada_ffn"] = f(inp["w_ada_ffn"][0])
    m["bT_ada_mix"] = pT(inp["b_ada_mix"][0], 24)
    m["bT_ada_ffn"] = pT(inp["b_ada_ffn"][0], 24)
    m["brow_ada_mix"] = f(np.asarray(inp["b_ada_mix"][0]).reshape(1, 3072))
    m["brow_ada_ffn"] = f(np.asarray(inp["b_ada_ffn"][0]).reshape(1, 3072))
    m["nw1T"] = pT(inp["norm_mix_w"][0], 8)
    m["nw2T"] = pT(inp["norm_ffn_w"][0], 8)
    m["w_in"] = f(inp["w_in"][0])
    m["w31T"] = f(np.asarray(inp["conv_dw_w"][0]).reshape(31, 8, 128).transpose(2, 1, 0))
    m["cb31T"] = pT(inp["conv_dw_b"][0], 8)
    m["lnwT"] = pT(inp["conv_ln_w"][0], 8)
    m["lnbT"] = pT(inp["conv_ln_b"][0], 8)
    m["w_conv_out"] = f(inp["w_conv_out"][0])
    m["w4T"] = f(np.asarray(inp["ssm_conv_w"][0]).reshape(4, 24, 128).transpose(2, 1, 0))
    m["cb4T"] = pT(inp["ssm_conv_b"][0], 24)
    cbm_ = np.zeros((32, 128), np.float32)
    cbm_[:24] = np.asarray(inp["ssm_conv_b"][0]).reshape(24, 128)
    m["cb4mat"] = cbm_
    m["dtb_bc"] = f(np.tile(np.asarray(inp["dt_bias"][0]).reshape(1, 32), (128, 4)))
    m["alog_bc"] = f(np.tile(np.asarray(inp["a_log"][0]).reshape(1, 32), (128, 4)))
    m["dsk_bc"] = f(np.tile(np.asarray(inp["d_skip"][0]).reshape(1, 32), (128, 1)))
    m["snwT"] = pT(inp["ssm_norm_w"][0], 16)
    m["w_ssm_out"] = f(inp["w_ssm_out"][0])
    m["w_out"] = f(inp["w_out"][0])
    m["w_ffn_in"] = f(inp["w_ffn_in"][0])
    m["w_ffn_down"] = f(inp["w_ffn_down"][0])
    m["fw_bc"] = f(np.tile(np.asarray(inp["final_norm_w"]).reshape(1, D), (128, 1)))
    m.update(host_consts())
    return m


def kernel(**inputs):
    nb = inputs["x"].shape[0]
    Lc = inputs["x"].shape[1]
    NT = Lc // TT
    nc = build(NT)
    in_maps = [host_layout(inputs, b, Lc) for b in range(nb)]
    res = run_bass_kernel_spmd(nc, in_maps, core_ids=list(range(nb)))
    out = np.stack([np.asarray(r["y"], dtype=np.float32) for r in res.results], axis=0)
    return out
```

```python
import contextlib
import numpy as np
import concourse.bass as bass
import concourse.mybir as mybir
from concourse.bass_utils import run_bass_kernel_spmd

F32 = mybir.dt.float32
BF16 = mybir.dt.bfloat16
AF = mybir.ActivationFunctionType
ALU = mybir.AluOpType

D = 1024
KC = 8
TT = 512
NB = 4
SEQ = 8192
DIN = 2048
NH = 32
HD = 64
NG = 4
NST = 128
DFF = 2816
NFF = 22
INC = 9248
NSLOT = 4
NDVE = 6
SAME_ENG_SYNC = True
DBG_STOP = None
EPS = 1e-6
LN_EPS = 1e-5


class _Stop(Exception):
    pass


class SemObj:
    def __init__(self, sem, owner, sid):
        self.sem = sem
        self.owner = owner
        self.id = sid
        self.cnt = 0


class Eng:
    def __init__(self, name, h, so):
        self.name = name
        self.h = h
        self.so = so
        so.owner = self
        self.waited = {}
        self.last = None


class V:
    __slots__ = ("ap", "sp", "lo", "hi")

    def __init__(self, ap, sp, lo, hi):
        self.ap = ap
        self.sp = sp
        self.lo = lo
        self.hi = hi


class Buf:
    def __init__(self, K, name, shape, dt, at=None):
        self.esz = 4 if dt == F32 else 2
        nfree = int(np.prod(shape[1:]))
        self.nbytes = nfree * self.esz
        if at is None:
            at = K.sb_alloc(self.nbytes)
        self.off = at
        self.shape = list(shape)
        self.dt = dt
        self.nfree = nfree
        self.t = K.nc.alloc_sbuf_tensor_at(name, list(shape), dt, offset=at)
        st = []
        acc = 1
        for d in reversed(shape[1:]):
            st.append(acc)
            acc *= d
        self.strides = list(reversed(st))

    def __getitem__(self, idx):
        if not isinstance(idx, tuple):
            idx = (idx,)
        ap = self.t[idx]
        lo = 0
        hi = 0
        fr = idx[1:]
        for i, d in enumerate(self.shape[1:]):
            if i < len(fr):
                x = fr[i]
                if isinstance(x, int):
                    a, b = x, x + 1
                else:
                    a = 0 if x.start is None else x.start
                    b = d if x.stop is None else x.stop
            else:
                a, b = 0, d
            lo += a * self.strides[i]
            hi += (b - 1) * self.strides[i]
        hi += 1
        return V(ap, "sb", self.off + lo * self.esz, self.off + hi * self.esz)

    def cust(self, off, dims, p0=0, npart=128, lo=None, hi=None):
        ap = bass.AP(self.t, p0 * self.nfree + off, [[self.nfree, npart]] + [list(d) for d in dims])
        if lo is None:
            lo = 0
            hi = self.nfree
        return V(ap, "sb", self.off + lo * self.esz, self.off + hi * self.esz)


class PsBank:
    def __init__(self, K, name, idx, dt):
        n = 512 if dt == F32 else 1024
        self.t = K.nc.alloc_psum_tensor(name, [128, n], dt)
        self.idx = idx
        self.n = n

    def __getitem__(self, idx):
        return V(self.t[idx], "ps", self.idx, self.idx + 1)

    def cust(self, off, dims, p0=0, npart=128):
        ap = bass.AP(self.t, p0 * self.n + off, [[self.n, npart]] + [list(d) for d in dims])
        return V(ap, "ps", self.idx, self.idx + 1)


class K:
    def __init__(self, nc, es):
        self.nc = nc
        self.es = es
        self.sb_top = 16640
        self.nsem = 0
        self.recs = {"sb": [], "ps": []}
        self.PE = Eng("pe", nc.tensor, self.newsem("pe"))
        self.ACT = Eng("act", nc.scalar, self.newsem("act"))
        self.DVE = Eng("dve", nc.vector, self.newsem("dve"))
        self.POOL = Eng("pool", nc.gpsimd, self.newsem("pool"))
        self.SP = Eng("sp", nc.sync, self.newsem("sp"))

    def newsem(self, name):
        s = self.es.enter_context(self.nc.semaphore(name + "_%d" % self.nsem))
        so = SemObj(s, None, self.nsem)
        self.nsem += 1
        return so

    def sb_alloc(self, nbytes):
        a = self.sb_top
        self.sb_top += (nbytes + 63) // 64 * 64
        return a

    def _need(self, eng, so, val):
        if so.owner is eng and not (SAME_ENG_SYNC and eng is not self.PE):
            return
        if so.owner is not None and so.cnt < val:
            o = so.owner
            o.last.then_inc(so.sem, 1)
            so.cnt += 1
            assert so.cnt >= val, (o.name, so.cnt, val)
        if eng.waited.get(so.id, 0) < val:
            eng.h.wait_ge(so.sem, val)
            eng.waited[so.id] = val

    def _sync(self, eng, reads, writes):
        needs = {}
        for v in reads:
            if v is None or v.sp is None:
                continue
            for r in self.recs[v.sp]:
                if (r[2] == "W" or (v.sp == "ps" and r[3].owner is not eng)) and r[0] < v.hi and v.lo < r[1]:
                    if needs.get(r[3], 0) < r[4]:
                        needs[r[3]] = r[4]
        for v in writes:
            if v is None or v.sp is None:
                continue
            for r in self.recs[v.sp]:
                if r[0] < v.hi and v.lo < r[1]:
                    if needs.get(r[3], 0) < r[4]:
                        needs[r[3]] = r[4]
        for so, val in needs.items():
            self._need(eng, so, val)

    def _record(self, reads, writes, so, val):
        for v in writes:
            if v is None or v.sp is None:
                continue
            L = self.recs[v.sp]
            L[:] = [r for r in L if not (v.lo <= r[0] and r[1] <= v.hi)]
            L.append([v.lo, v.hi, "W", so, val])
        for v in reads:
            if v is None or v.sp is None:
                continue
            L = self.recs[v.sp]
            L[:] = [r for r in L if not (r[2] == "R" and r[3] is so and v.lo <= r[0] and r[1] <= v.hi)]
            L.append([v.lo, v.hi, "R", so, val])

    def emit(self, eng, fn, reads, writes, mark=True):
        self._sync(eng, reads, writes)
        ins = fn()
        so = eng.so
        if mark:
            ins.then_inc(so.sem, 1)
            so.cnt += 1
            val = so.cnt
        else:
            val = so.cnt + 1
        eng.last = ins
        self._record(reads, writes, so, val)
        return ins

    def dma(self, q, out, in_, dsem, out_ap=None, in_ap=None):
        self._sync(q, [in_], [out])
        oa = out.ap if out is not None else out_ap
        ia = in_.ap if in_ is not None else in_ap
        ins = q.h.dma_start(out=oa, in_=ia)
        ins.then_inc(dsem.sem, 16)
        dsem.cnt += 16
        self._record([in_], [out], dsem, dsem.cnt)
        return ins

    def mm(self, out, lhsT, rhs, start=True, stop=True, mark=False):
        return self.emit(self.PE, lambda: self.nc.tensor.matmul(out.ap, lhsT.ap, rhs.ap, start=start, stop=stop),
                         [lhsT, rhs], [out], mark=mark)

    def tr(self, out, in_, ident, mark=False):
        return self.emit(self.PE, lambda: self.nc.tensor.transpose(out.ap, in_.ap, ident.ap),
                         [in_, ident], [out], mark=mark)

    def act(self, out, in_, func, bias=None, scale=1.0, accum=None):
        rd = [in_]
        kw = {}
        if isinstance(bias, V):
            rd.append(bias)
            kw["bias"] = bias.ap
        elif bias is not None:
            kw["bias"] = float(bias)
        if isinstance(scale, V):
            rd.append(scale)
            kw["scale"] = scale.ap
        else:
            kw["scale"] = float(scale)
        wr = [out]
        if accum is not None:
            wr.append(accum)
            kw["accum_out"] = accum.ap
        return self.emit(self.ACT, lambda: self.nc.scalar.activation(out=out.ap, in_=in_.ap, func=func, **kw), rd, wr)

    def ts(self, eng, out, in_, s1, s2=None, op0=ALU.mult, op1=None):
        rd = [in_]
        a1 = s1
        a2 = s2
        if isinstance(s1, V):
            rd.append(s1)
            a1 = s1.ap
        if isinstance(s2, V):
            rd.append(s2)
            a2 = s2.ap
        if op1 is None:
            fn = lambda: eng.h.tensor_scalar(out.ap, in_.ap, a1, None, op0)
        else:
            fn = lambda: eng.h.tensor_scalar(out.ap, in_.ap, a1, a2, op0, op1)
        return self.emit(eng, fn, rd, [out])

    def tt(self, eng, out, a, b, op):
        return self.emit(eng, lambda: eng.h.tensor_tensor(out.ap, a.ap, b.ap, op), [a, b], [out])

    def stt(self, eng, out, in0, sc, in1, op0, op1):
        rd = [in0, in1]
        a = sc
        if isinstance(sc, V):
            rd.append(sc)
            a = sc.ap
        return self.emit(eng, lambda: eng.h.scalar_tensor_tensor(out.ap, in0.ap, a, in1.ap, op0, op1), rd, [out])

    def cp(self, eng, out, in_):
        if eng is self.ACT:
            return self.emit(eng, lambda: self.nc.scalar.copy(out.ap, in_.ap), [in_], [out])
        return self.emit(eng, lambda: eng.h.tensor_copy(out.ap, in_.ap), [in_], [out])

    def memset(self, eng, out, val):
        return self.emit(eng, lambda: eng.h.memset(out.ap, val), [], [out])


def piece_table():
    P = []

    def add(kind, **kw):
        kw["kind"] = kind
        P.append(kw)
        return len(P) - 1

    ids = {}
    c_a, c_b, c_z, c_xs, c_B, c_C, c_dt, c_gc, c_gs = 0, 1024, 2048, 4096, 6144, 6656, 7168, 7200, 8224
    for i in range(2):
        ids["a%d" % i] = add("w", w="w_in", k0=0, nk=8, segs=[(c_a + 512 * i, 512)])
        ids["b%d" % i] = add("w", w="w_in", k0=0, nk=8, segs=[(c_b + 512 * i, 512)])
    for i in range(8):
        ids["c31_%d" % i] = add("c31", fc=i)
    for i in range(2):
        ids["co%d" % i] = add("w", w="w_conv_out", k0=0, nk=8, segs=[(512 * i, 512)])
        ids["gc%d" % i] = add("w", w="w_in", k0=0, nk=8, segs=[(c_gc + 512 * i, 512)])
    for i in range(6):
        ids["c4_%d" % i] = add("c4", g=i)
    for i in range(4):
        ids["xs%d" % i] = add("w", w="w_in", k0=0, nk=8, segs=[(c_xs + 512 * i, 512)])
    ids["B"] = add("w", w="w_in", k0=0, nk=8, segs=[(c_B, 512)])
    ids["C"] = add("w", w="w_in", k0=0, nk=8, segs=[(c_C, 512)])
    ids["dt"] = add("w", w="w_in", k0=0, nk=8, segs=[(c_dt, 32)])
    for i in range(4):
        ids["z%d" % i] = add("w", w="w_in", k0=0, nk=8, segs=[(c_z + 512 * i, 512)])
    ids["dsk"] = add("dsk")
    for cg in range(2):
        for kh in range(2):
            ids["so%d%d" % (cg, kh)] = add("w", w="w_ssm_out", k0=8 * kh, nk=8, segs=[(512 * cg, 512)], rs="snw")
        ids["gs%d" % cg] = add("w", w="w_in", k0=0, nk=8, segs=[(c_gs + 512 * cg, 512)])
    for i in range(2):
        ids["wo%d" % i] = add("w", w="w_out", k0=0, nk=8, segs=[(512 * i, 512)], cs=("gate1", 512 * i))
    for m in range(11):
        ids["fi%d" % m] = add("w", w="w_ffn_in", k0=0, nk=8, segs=[(256 * m, 256), (DFF + 256 * m, 256)])
    for cg in range(2):
        for ks in range(3):
            nk = 8 if ks < 2 else 6
            ids["fd%d%d" % (cg, ks)] = add("w", w="w_ffn_down", k0=8 * ks, nk=nk, segs=[(512 * cg, 512)],
                                          cs=("gate2", 512 * cg))
    return P, ids


def tile_order(ids):
    o = ["a0", "b0", "a1", "b1"] + ["c31_%d" % i for i in range(8)] + ["z0", "z1", "z2", "z3", "co0", "gc0", "co1", "gc1"]
    o += ["c4_0", "xs0", "c4_1", "xs1", "c4_2", "xs2", "c4_3", "xs3", "c4_4", "B", "c4_5", "C", "dt", "dsk"]
    o += ["so00", "so01", "gs0", "so10", "so11", "gs1", "wo0", "wo1"]
    o += ["fi%d" % m for m in range(11)]
    o += ["fd00", "fd01", "fd02", "fd10", "fd11", "fd12"]
    return o


def build(NT):
    nc = bass.Bass("TRN2", target_bir_lowering=False)
    Lc = NT * TT
    dram = {}

    def din(name, shape):
        dram[name] = nc.dram_tensor(name, list(shape), F32, kind="ExternalInput").ap()
        return dram[name]

    x_d = din("x", [Lc, D])
    cT_d = din("cT", [128, 8])
    din("w_ada_mix", [D, 3072])
    din("w_ada_ffn", [D, 3072])
    din("bT_ada_mix", [128, 24])
    din("bT_ada_ffn", [128, 24])
    din("brow_ada_mix", [1, 3072])
    din("brow_ada_ffn", [1, 3072])
    din("nw1T", [128, 8])
    din("nw2T", [128, 8])
    din("w_in", [D, INC])
    din("w31T", [128, 8, 31])
    din("cb31T", [128, 8])
    din("lnwT", [128, 8])
    din("lnbT", [128, 8])
    din("w_conv_out", [D, D])
    din("w4T", [128, 24, 4])
    din("cb4T", [128, 24])
    din("cb4mat", [32, 128])
    din("dtb_bc", [128, 128])
    din("alog_bc", [128, 128])
    din("dsk_bc", [128, 32])
    din("snwT", [128, 16])
    din("w_ssm_out", [DIN, D])
    din("w_out", [D, D])
    din("w_ffn_in", [D, 2 * DFF])
    din("w_ffn_down", [DFF, D])
    din("fw_bc", [128, D])
    din("ident", [128, 128])
    din("tri", [128, 128])
    din("negmask4", [128, 512])
    din("causal4", [128, 512])
    din("sel", [64, 4096])
    y_d = nc.dram_tensor("y", [Lc, D], F32, kind="ExternalOutput").ap()

    pieces, ids = piece_table()
    NP = len(pieces)
    wscr = nc.dram_tensor("wscr", [NP, 128, 4096], BF16).ap()

    es = contextlib.ExitStack()
    with es:
      try:
        k = K(nc, es)
        PE, ACT, DVE, POOL, SP = k.PE, k.ACT, k.DVE, k.POOL, k.SP
        EW = [DVE, POOL]
        rr = [0]

        def ew():
            rr[0] += 1
            return EW[rr[0] % 2]

        def B(name, shape, dt, at=None):
            return Buf(k, name, shape, dt, at)

        ident_f = B("ident_f", [128, 128], F32)
        ident_b = B("ident_b", [128, 128], BF16)
        tri_f = B("tri_f", [128, 128], F32)
        ones_f = B("ones_f", [128, 128], F32)
        ones_b = B("ones_b", [128, 128], BF16)
        negm = B("negm", [128, 512], BF16)
        caus = B("caus", [128, 512], BF16)
        sel = B("sel", [64, 4096], BF16)
        fw_bc = B("fw_bc", [128, D], F32)
        dtb = B("dtb", [128, 128], F32)
        a_bc = B("a_bc", [128, 128], F32)
        cb4mat = B("cb4mat", [32, 128], BF16)
        prm = B("prm", [128, 160], F32)
        G1, SH1, G2, SH2, CB31, LNW, LNB, CB4, SNW = 0, 8, 16, 24, 32, 40, 48, 56, 80
        x_res = B("x_res", [128, NB, D], F32)
        hT = B("hT", [128, KC, TT], BF16)
        S = B("S", [128, DIN], F32)
        Sb = B("Sb", [128, DIN], BF16)
        uhalo = B("uhalo", [128, 8, 32], BF16)
        xhalo = B("xhalo", [128, 24, 4], BF16)
        slots = [B("slot%d" % i, [128, 8, 512], BF16) for i in range(NSLOT)]
        slot_sem = [k.newsem("slot") for _ in range(NSLOT)]
        arB = k.sb_alloc(33 * 1024)
        x_in = B("x_in", [128, NB, D], F32, at=arB)
        uT = B("uT", [128, 8, 544], BF16, at=arB)
        uv = B("uv", [128, 8, TT], BF16, at=arB + 8704)
        sgt = [B("sgt%d" % i, [128, TT], F32, at=arB + 16896 + 2048 * i) for i in range(2)]
        usq = [B("usq%d" % i, [128, TT], BF16, at=arB + 20992 + 1024 * i) for i in range(2)]
        mean_t = B("mean_t", [128, TT], F32, at=arB + 23040)
        rstd_t = B("rstd_t", [128, TT], F32, at=arB + 25088)
        msq_t = B("msq_t", [128, TT], F32, at=arB + 27136)
        lnt = [B("lnt%d" % i, [128, TT], F32, at=arB + 29184 + 2048 * i) for i in range(2)]
        assert 29184 + 4096 <= 33 * 1024
        ynT = B("ynT", [128, 16, TT], BF16, at=arB)
        yc = B("yc", [128, DIN], F32, at=arB + 16384)
        mg = B("mg", [128, KC, TT], BF16, at=arB + 24576)
        arC = k.sb_alloc(32 * 1024)
        xs = B("xs", [128, NB, DIN], BF16, at=arC)
        siluz = B("siluz", [128, NB, DIN], BF16, at=arC + 16384)
        actT = B("actT", [128, NFF, TT], BF16, at=arC)
        sl = [B("sl%d" % i, [128, TT], F32, at=arC + 22528 + 2048 * i) for i in range(2)]
        xb = [B("xb%d" % i, [128, D], BF16) for i in range(2)]
        junk = B("junk", [128, D], BF16)
        st4 = B("st4", [128, 16], F32)
        t1 = B("t1", [128, KC, TT], BF16)
        xst = [B("xst%d" % i, [128, 520], BF16) for i in range(3)]
        BT = B("BT", [128, NG, TT], BF16)
        CT = B("CT", [128, NG, TT], BF16)
        Btok = B("Btok", [128, NB, 512], BF16)
        dtp = B("dtp", [128, 128], F32)
        dtt = B("dtt", [128, 128], F32)
        lndt = B("lndt", [128, 128], F32)
        da2 = B("da2", [128, NB, 64], F32)
        biasA = B("biasA", [128, 128], F32)
        Ee = B("Ee", [128, 128], F32)
        wdt = B("wdt", [128, 128], F32)
        decA = B("decA", [128, 128], F32)
        tmpd = B("tmpd", [128, 128], F32)
        acsT = B("acsT", [64, 512], BF16)
        acsTt = B("acsTt", [64, 512], BF16)
        cbm = B("cbm", [128, NG, 128], BF16)
        expo = [B("expo%d" % i, [128, 8, 128], BF16) for i in range(2)] + [B("expo2", [128, 8, 128], BF16, at=arB + 16384 + 4096)]
        xw = [B("xw%d" % i, [128, 512], BF16) for i in range(2)] + [B("xw2", [128, 512], BF16, at=arB + 16384 + 4096 + 2048)]
        otmp = [B("otmp%d" % i, [128, 512], F32) for i in range(2)]
        yn = B("yn", [128, DIN], BF16)
        stg_f = [B("stgf%d" % i, [128, 8, 512], F32, at=arB + 16384 * i) for i in range(2)]
        stg_f += [B("stgf%d" % (2 + i), [128, 8, 512], F32, at=slots[2 * i].off) for i in range(2)]
        stg_b = [B("stgb%d" % i, [128, 8, 512], BF16, at=arC + 8192 * i) for i in range(4)]
        NSF, NSB = 4, 4
        gate_bc = [B("gate_bc%d" % i, [128, D], F32, at=x_res.off + 4096 * i) for i in range(2)]
        sc_rep = B("sc_rep", [128, KC, 128], F32, at=arC + 24576)
        scv = B("scv", [128, 8], F32)
        print("SBUF bytes/partition used:", k.sb_top)
        assert k.sb_top <= 224 * 1024 - 2048, k.sb_top

        psf = [PsBank(k, "psf%d" % i, i, F32) for i in range(6)]
        psb = [PsBank(k, "psb%d" % i, 6 + i, BF16) for i in range(2)]

        ld_sem = k.newsem("ld")
        stgf_sem = [k.newsem("stgf") for _ in range(4)]
        stgb_sem = [k.newsem("stgb") for _ in range(4)]
        st_sem = k.newsem("st")
        xin_sem = k.newsem("xin")
        xres_sem = k.newsem("xres")
        y_sem = k.newsem("ysem")

        def load(dst, src_ap, q=SP, sem=None):
            return k.dma(q, dst, None, sem or k.newsem("l"), in_ap=src_ap)

        c_stage = stg_f[0]
        for (dstb, name, np_, ncol) in [(ident_b, "ident", 128, 128), (ones_b, None, 128, 128), (negm, "negmask4", 128, 512),
                                        (caus, "causal4", 128, 512)]:
            if name is None:
                continue
            v = c_stage.cust(0, [[1, ncol]], 0, np_, 0, ncol)
            load(v, dram[name][:, :])
            k.cp(DVE, dstb[:, :], v)
        load(ident_f[:, :], dram["ident"][:, :])
        load(tri_f[:, :], dram["tri"][:, :])
        k.memset(DVE, ones_f[:, :], 1.0)
        k.memset(DVE, ones_b[:, :], 1.0)
        for h in range(4):
            v = c_stage.cust(0, [[1, 1024]], 0, 64, 0, 1024)
            load(v, dram["sel"][:, h * 1024:(h + 1) * 1024])
            k.cp(DVE, sel[:, h * 1024:(h + 1) * 1024], v)
        v = c_stage.cust(0, [[1, 128]], 0, 32, 0, 128)
        load(v, dram["cb4mat"][:, :])
        k.cp(DVE, cb4mat[:, :], v)
        load(fw_bc[:, :], dram["fw_bc"][:, :])
        load(dtb[:, :], dram["dtb_bc"][:, :])
        load(a_bc[:, :], dram["alog_bc"][:, :])
        k.act(a_bc[:, :], a_bc[:, :], AF.Exp)
        k.ts(DVE, a_bc[:, :], a_bc[:, :], -1.0)
        for (col, name, n) in [(CB31, "cb31T", 8), (LNW, "lnwT", 8), (LNB, "lnbT", 8), (CB4, "cb4T", 24), (SNW, "snwT", 16)]:
            load(prm[:, col:col + n], dram[name][:, :])
        w31 = B("w31", [128, 8, 31], F32)
        w4 = B("w4", [128, 24, 4], F32)
        dskb = B("dskb", [128, 32], F32)
        load(w31[:, :, :], dram["w31T"][:, :, :])
        load(w4[:, :, :], dram["w4T"][:, :, :])
        load(dskb[:, :], dram["dsk_bc"][:, :])
        k.memset(DVE, S[:, :], 0.0)
        k.memset(DVE, Sb[:, :], 0.0)
        k.memset(DVE, uhalo[:, :, :], 0.0)
        k.memset(DVE, xhalo[:, :, :], 0.0)

        if DBG_STOP == "s0":
            raise _Stop()
        load(scv[:, :], cT_d[:, :])
        k.act(scv[:, :], scv[:, :], AF.Silu)
        for kc in range(KC):
            k.ts(DVE, sc_rep[:, kc, :], ones_f[:, :], scv[:, kc:kc + 1])
        nwt = B("nwt", [128, 16], F32)
        load(nwt[:, 0:8], dram["nw1T"][:, :])
        load(nwt[:, 8:16], dram["nw2T"][:, :])
        bTt = B("bTt", [128, 48], F32)
        load(bTt[:, 0:24], dram["bT_ada_mix"][:, :])
        load(bTt[:, 24:48], dram["bT_ada_ffn"][:, :])
        browt = B("browt", [1, 2048], F32, at=x_res.off + 8192)
        rowsb = B("rowsb", [1, 512], F32, at=hT.off)
        load(browt[:, 0:1024], dram["brow_ada_mix"][:, 2048:3072])
        load(browt[:, 1024:2048], dram["brow_ada_ffn"][:, 2048:3072])
        for li, wn in enumerate(["w_ada_mix", "w_ada_ffn"]):
            wv = dram[wn].rearrange("(kc p) f -> p kc f", p=128)
            gcol, scol = (G1, SH1) if li == 0 else (G2, SH2)
            for pc in range(6):
                sf = stg_f[pc % 2]
                load(sf[:, :, :], wv[:, :, pc * 512:(pc + 1) * 512], sem=stgf_sem[pc % 2])
                if pc < 4:
                    pr = psf[4 + pc % 2]
                    for kc in range(KC):
                        k.mm(pr[0:1, :], scv[:, kc:kc + 1], sf[:, kc, :], start=(kc == 0), stop=(kc == KC - 1), mark=(kc == KC - 1))
                    k.cp(DVE, rowsb[0:1, :], pr[0:1, :])
                    ps = psf[pc % 2]
                    for fcl in range(4):
                        k.mm(ps[:, fcl:fcl + 1], rowsb[0:1, fcl * 128:(fcl + 1) * 128], ones_f[0:1, 0:1], start=True, stop=True,
                             mark=(fcl == 3))
                    fc0 = pc * 4
                    if pc < 2:
                        k.tt(DVE, prm[:, scol + fc0:scol + fc0 + 4], ps[:, 0:4], bTt[:, li * 24 + fc0:li * 24 + fc0 + 4], ALU.add)
                    else:
                        f8 = fc0 - 8
                        k.tt(DVE, st4[:, 0:4], ps[:, 0:4], bTt[:, li * 24 + fc0:li * 24 + fc0 + 4], ALU.add)
                        k.stt(DVE, prm[:, gcol + f8:gcol + f8 + 4], st4[:, 0:4], 1.0, nwt[:, li * 8 + f8:li * 8 + f8 + 4],
                              ALU.add, ALU.mult)
                else:
                    ps = psf[2 + pc % 2]
                    h = pc - 4
                    for kc in range(KC):
                        k.mm(ps[:, :], sc_rep[:, kc, :], sf[:, kc, :], start=(kc == 0), stop=False)
                    k.mm(ps[:, :], ones_f[0:1, :], browt[0:1, li * 1024 + h * 512:li * 1024 + (h + 1) * 512], start=False,
                         stop=True, mark=True)
                    k.cp(DVE, gate_bc[li][:, h * 512:(h + 1) * 512], ps[:, :])

        if DBG_STOP == "s1":
            raise _Stop()
        ceng = [DVE, ACT, DVE]

        def diag(j, dst, sc):
            if j % 2 == 0:
                k.ts(DVE, dst, ident_f[:, :], sc)
            else:
                k.act(dst, ident_f[:, :], AF.Identity, scale=sc)
        stq = POOL
        w_ids = [i for i, p_ in enumerate(pieces) if p_["kind"] == "w"]
        d_ids = [i for i, p_ in enumerate(pieces) if p_["kind"] != "w"]
        prep_order = []
        while w_ids or d_ids:
            prep_order += w_ids[:3]
            w_ids = w_ids[3:]
            if d_ids:
                prep_order.append(d_ids.pop(0))
        fcnt = [0]
        piece_done = {}
        for sb0 in stg_b:
            k.memset(DVE, sb0[:, :, :], 0.0)
        fslot = {}
        lstate = {"next": 0}

        def prep_loads(upto):
            while lstate["next"] < min(upto, len(prep_order)):
                pos = lstate["next"]
                lstate["next"] += 1
                pcl = pieces[prep_order[pos]]
                if pcl["kind"] != "w":
                    continue
                fi2 = fcnt[0] % NSF
                fcnt[0] += 1
                fslot[pos] = fi2
                wv2 = dram[pcl["w"]].rearrange("(kc p) f -> p kc f", p=128)
                c2 = 0
                for (c0, n) in pcl["segs"]:
                    load(stg_f[fi2][:, 0:pcl["nk"], c2:c2 + n], wv2[:, pcl["k0"]:pcl["k0"] + pcl["nk"], c0:c0 + n], sem=stgf_sem[fi2])
                    c2 += n

        for bi, pi in enumerate(prep_order):
            pc = pieces[pi]
            sb_ = stg_b[bi % NSB]
            prep_loads(bi + 3)
            if pc["kind"] == "w":
                fi_ = fslot[bi]
                sf = stg_f[fi_]
                nk = pc["nk"]
                ncols = sum(s[1] for s in pc["segs"])
                eng = ceng[bi % 3]
                if "rs" in pc:
                    base = {"g1": G1, "g2": G2, "snw": SNW + pc["k0"]}[pc["rs"]]
                    for kc in range(nk):
                        if eng is ACT:
                            k.act(sb_[:, kc, 0:ncols], sf[:, kc, 0:ncols], AF.Identity, scale=prm[:, base + kc:base + kc + 1])
                        else:
                            k.ts(eng, sb_[:, kc, 0:ncols], sf[:, kc, 0:ncols], prm[:, base + kc:base + kc + 1])
                elif "cs" in pc:
                    gb = gate_bc[0] if pc["cs"][0] == "gate1" else gate_bc[1]
                    c0 = pc["cs"][1]
                    e2 = DVE if eng is ACT else eng
                    gv = gb.cust(c0, [[0, nk], [1, 512]], 0, 128, c0, c0 + 512)
                    k.tt(e2, sb_[:, 0:nk, :], sf[:, 0:nk, :], gv, ALU.mult)
                else:
                    k.cp(eng, sb_[:, 0:nk, :], sf[:, 0:nk, :])
            elif pc["kind"] == "c31":
                fc = pc["fc"]
                sflat = sb_.cust(0, [[1, 4096]])
                for j in range(31):
                    diag(j, sb_.cust(j * 128, [[1, 128]], 0, 128, j * 128, (j + 1) * 128), w31[:, fc, j:j + 1])
            elif pc["kind"] == "c4":
                for cl in range(4):
                    ch = pc["g"] * 4 + cl
                    for j in range(4):
                        diag(j, sb_[:, cl, j * 128:(j + 1) * 128], w4[:, ch, j:j + 1])
            elif pc["kind"] == "dsk":
                for j in range(NH):
                    diag(j, sb_.cust(j * 128, [[1, 128]], 0, 128, j * 128, (j + 1) * 128), dskb[:, j:j + 1])
            k.dma(SP, None, sb_[:, :, :], stgb_sem[bi % NSB], out_ap=wscr[pi].rearrange("p (a b) -> p a b", b=512))
            piece_done[pi] = (stgb_sem[bi % NSB], stgb_sem[bi % NSB].cnt)

        if DBG_STOP == "s2":
            raise _Stop()
        order = tile_order(ids)
        NPT = len(order)
        wstate = {"next": 0}
        total_loads = NT * NPT

        def issue_loads(upto):
            while wstate["next"] < min(upto, total_loads):
                n = wstate["next"]
                pid = ids[order[n % NPT]]
                s = n % NSLOT
                if n < NPT:
                    so_, v_ = piece_done[pid]
                    if SP.waited.get(("st", so_.id), 0) < v_:
                        SP.h.wait_ge(so_.sem, v_)
                        SP.waited[("st", so_.id)] = v_
                k.dma(SP, slots[s][:, :, :], None, slot_sem[s], in_ap=wscr[pid].rearrange("p (a b) -> p a b", b=512))
                wstate["next"] += 1

        wpos = {"n": 0}

        def W(name):
            n = wpos["n"]
            assert order[n % NPT] == name, (order[n % NPT], name)
            issue_loads(n + NSLOT - 1)
            wpos["n"] += 1
            return slots[n % NSLOT]

        def normA(src):
            for q in range(NB):
                k.act(junk[:, :], src[:, q, :], AF.Square, accum=st4[:, q:q + 1])
            k.act(st4[:, 4:8], st4[:, 0:4], AF.Ln, bias=EPS, scale=1.0 / D)
            k.act(st4[:, 4:8], st4[:, 4:8], AF.Exp, scale=-0.5)
            for q in range(2):
                k.ts(DVE, xb[q % 2][:, :], src[:, q, :], st4[:, 4 + q:5 + q])

        def normB(src, gcol, scol):
            for q in range(NB):
                if q >= 2:
                    k.ts(DVE, xb[q % 2][:, :], src[:, q, :], st4[:, 4 + q:5 + q])
                for kc in range(KC):
                    pb = psb[q % 2]
                    k.tr(pb[:, kc * 128:(kc + 1) * 128], xb[q % 2][:, kc * 128:(kc + 1) * 128], ident_b[:, :], mark=(kc == KC - 1))
                for kc in range(KC):
                    e = DVE if q % 2 == 0 else ACT
                    o = hT[:, kc, q * 128:(q + 1) * 128]
                    i = psb[q % 2][:, kc * 128:(kc + 1) * 128]
                    if e is DVE:
                        k.ts(DVE, o, i, prm[:, gcol + kc:gcol + kc + 1], prm[:, scol + kc:scol + kc + 1], ALU.mult, ALU.add)
                    else:
                        k.act(o, i, AF.Identity, bias=prm[:, scol + kc:scol + kc + 1], scale=prm[:, gcol + kc:gcol + kc + 1])

        def norm_block(src, q, gcol, scol):
            k.act(junk[:, :], src[:, q, :], AF.Square, accum=st4[:, q:q + 1])
            k.act(st4[:, 4 + q:5 + q], st4[:, q:q + 1], AF.Ln, bias=EPS, scale=1.0 / D)
            k.act(st4[:, 4 + q:5 + q], st4[:, 4 + q:5 + q], AF.Exp, scale=-0.5)
            k.ts(DVE, xb[q % 2][:, :], src[:, q, :], st4[:, 4 + q:5 + q])

        def norm_block2(q, gcol, scol):
            for kc in range(KC):
                k.tr(psb[q % 2][:, kc * 128:(kc + 1) * 128], xb[q % 2][:, kc * 128:(kc + 1) * 128], ident_b[:, :], mark=(kc == KC - 1))
            for kc in range(KC):
                o = hT[:, kc, q * 128:(q + 1) * 128]
                i = psb[q % 2][:, kc * 128:(kc + 1) * 128]
                if q % 2 == 0:
                    k.ts(DVE, o, i, prm[:, gcol + kc:gcol + kc + 1], prm[:, scol + kc:scol + kc + 1], ALU.mult, ALU.add)
                else:
                    k.act(o, i, AF.Identity, bias=prm[:, scol + kc:scol + kc + 1], scale=prm[:, gcol + kc:gcol + kc + 1])

        def rmsnorm_to_hT(src, gcol, scol):
            normA(src)
            normB(src, gcol, scol)

        xv = x_d.rearrange("(t q p) d -> t p q d", p=128, q=NB)
        yv = y_d.rearrange("(t q p) d -> t p q d", p=128, q=NB)

        k.dma(SP, x_in[:, :, :], None, xin_sem, in_ap=xv[0])
        for ti in range(NT):
            if ti == 0:
                rmsnorm_to_hT(x_in, G1, SH1)
            if DBG_STOP == "p1":
                break
            k.cp(POOL, uT[:, :, 0:30], uhalo[:, :, 0:30])
            for half in range(2):
                wa = W("a%d" % half)
                wb = W("b%d" % half)
                for fl in range(4):
                    fc = half * 4 + fl
                    pa = psf[(fc % 3) * 2]
                    pb_ = psf[(fc % 3) * 2 + 1]
                    for kc in range(KC):
                        k.mm(pa[:, :], wa[:, kc, fl * 128:(fl + 1) * 128], hT[:, kc, :], start=(kc == 0), stop=(kc == KC - 1))
                    for kc in range(KC):
                        k.mm(pb_[:, :], wb[:, kc, fl * 128:(fl + 1) * 128], hT[:, kc, :], start=(kc == 0), stop=(kc == KC - 1),
                             mark=(kc == KC - 1))
                    k.act(sgt[fc % 2][:, :], pb_[:, :], AF.Sigmoid)
                    k.tt(DVE, uT[:, fc, 30:30 + TT], pa[:, :], sgt[fc % 2][:, :], ALU.mult)
            k.cp(POOL, uhalo[:, :, 0:30], uT[:, :, TT:TT + 30])
            if DBG_STOP == "p2a":
                break
            pS, pQ = psf[4], psf[5]

            def ln_stats(fc_):
                k.mm(pS[:, :], ones_b[:, :], uv[:, fc_, :], start=(fc_ == 0), stop=(fc_ == 7))
                k.mm(pQ[:, :], ones_b[:, :], usq[fc_ % 2][:, :], start=(fc_ == 0), stop=(fc_ == 7), mark=True)

            for fc in range(8):
                wd = W("c31_%d" % fc)
                pu = psf[fc % 4]
                acc = lnt[fc % 2]
                k.ts(DVE, acc[:, :], uT[:, fc, 0:TT], w31[:, fc, 0:1])
                for j in range(1, NDVE):
                    k.stt(DVE, acc[:, :], uT[:, fc, j:j + TT], w31[:, fc, j:j + 1], acc[:, :], ALU.mult, ALU.add)
                for j in range(NDVE, 31):
                    k.mm(pu[:, :], wd.cust(j * 128, [[1, 128]], 0, 128, j * 128, (j + 1) * 128), uT[:, fc, j:j + TT],
                         start=(j == NDVE), stop=(j == 30), mark=(j == 30))
                k.stt(DVE, uv[:, fc, :], pu[:, :], prm[:, CB31 + fc:CB31 + fc + 1], acc[:, :], ALU.add, ALU.add)
                k.act(usq[fc % 2][:, :], uv[:, fc, :], AF.Square)
                if fc > 0:
                    ln_stats(fc - 1)
            ln_stats(7)
            k.ts(DVE, mean_t[:, :], pS[:, :], 1.0 / D)
            k.tt(DVE, msq_t[:, :], mean_t[:, :], mean_t[:, :], ALU.mult)
            k.stt(DVE, rstd_t[:, :], pQ[:, :], 1.0 / D, msq_t[:, :], ALU.mult, ALU.subtract)
            k.act(rstd_t[:, :], rstd_t[:, :], AF.Ln, bias=LN_EPS)
            k.act(rstd_t[:, :], rstd_t[:, :], AF.Exp, scale=-0.5)
            def zproj(zi):
                wz = W("z%d" % zi)
                for q in range(NB):
                    pz = psf[(zi * NB + q) % 6]
                    for kc in range(KC):
                        k.mm(pz[:, :], hT[:, kc, q * 128:(q + 1) * 128], wz[:, kc, :], start=(kc == 0), stop=(kc == KC - 1),
                             mark=(kc == KC - 1))
                    k.act(siluz[:, q, zi * 512:(zi + 1) * 512], pz[:, :], AF.Silu)

            zproj(0)
            zproj(1)
            for fc in range(8):
                lt = lnt[fc % 2]
                k.tt(DVE, lt[:, :], uv[:, fc, :], mean_t[:, :], ALU.subtract)
                k.tt(POOL, lt[:, :], lt[:, :], rstd_t[:, :], ALU.mult)
                k.act(uv[:, fc, :], lt[:, :], AF.Silu, bias=prm[:, LNB + fc:LNB + fc + 1], scale=prm[:, LNW + fc:LNW + fc + 1])
            zproj(2)
            zproj(3)
            if DBG_STOP == "p2b":
                break
            for half in range(2):
                wc = W("co%d" % half)
                wg = W("gc%d" % half)
                for fl in range(4):
                    fo = half * 4 + fl
                    py = psf[(fo % 3) * 2]
                    pg = psf[(fo % 3) * 2 + 1]
                    for c in range(KC):
                        k.mm(py[:, :], wc[:, c, fl * 128:(fl + 1) * 128], uv[:, c, :], start=(c == 0), stop=(c == KC - 1))
                    for kc in range(KC):
                        k.mm(pg[:, :], wg[:, kc, fl * 128:(fl + 1) * 128], hT[:, kc, :], start=(kc == 0), stop=(kc == KC - 1),
                             mark=(kc == KC - 1))
                    k.act(sgt[fo % 2][:, :], pg[:, :], AF.Sigmoid)
                    k.tt(DVE, t1[:, fo, :], py[:, :], sgt[fo % 2][:, :], ALU.mult)
            if DBG_STOP == "p2c":
                break
            xw_ = {}

            def xproj(ch):
                if ch % 4 == 0:
                    xw_["c4"] = W("c4_%d" % (ch // 4))
                    if ch < 16:
                        xw_["x"] = W("xs%d" % (ch // 4))
                    elif ch == 16:
                        xw_["x"] = W("B")
                    else:
                        xw_["x"] = W("C")
                wx = xw_["x"]
                fl = ch % 4
                px = psf[ch % 2]
                for kc in range(KC):
                    k.mm(px[:, :], wx[:, kc, fl * 128:(fl + 1) * 128], hT[:, kc, :], start=(kc == 0), stop=(kc == KC - 1),
                         mark=(kc == KC - 1))
                st = xst[ch % 3]
                k.cp(POOL, st[:, 0:3], xhalo[:, ch, 0:3])
                if ch % 2 == 0:
                    k.cp(DVE, st[:, 3:3 + TT], px[:, :])
                else:
                    k.cp(ACT, st[:, 3:3 + TT], px[:, :])
                k.cp(POOL, xhalo[:, ch, 0:3], st[:, TT:TT + 3])

            def xconv(ch):
                wc4 = xw_["c4"]
                fl = ch % 4
                cl = ch % 4
                st = xst[ch % 3]
                if ch < 20:
                    for q in range(NB):
                        pt = psf[2 + q]
                        for j in range(4):
                            k.mm(pt[:, fl * 128:(fl + 1) * 128], st[:, q * 128 + j:q * 128 + j + 128],
                                 wc4[:, cl, j * 128:(j + 1) * 128], start=(j == 0), stop=False)
                        k.mm(pt[:, fl * 128:(fl + 1) * 128], sel[0:32, ch * 128:(ch + 1) * 128], cb4mat[0:32, :], start=False,
                             stop=True, mark=(fl == 3))
                    if fl == 3:
                        for q in range(NB):
                            if ch < 16:
                                k.act(xs[:, q, (ch // 4) * 512:(ch // 4 + 1) * 512], psf[2 + q][:, :], AF.Silu)
                            else:
                                k.act(Btok[:, q, :], psf[2 + q][:, :], AF.Silu)
                if ch >= 16:
                    pf = psf[ch % 2]
                    for j in range(4):
                        k.mm(pf[:, :], wc4[:, cl, j * 128:(j + 1) * 128], st[:, j:j + TT], start=(j == 0), stop=(j == 3),
                             mark=(j == 3))
                    dst = BT if ch < 20 else CT
                    k.act(dst[:, ch % 4, :], pf[:, :], AF.Silu, bias=prm[:, CB4 + ch:CB4 + ch + 1])

            for ch in range(24):
                if ch % 4 == 0:
                    xproj(ch)
                if ch % 4 != 3:
                    xproj(ch + 1)
                xconv(ch)
            if DBG_STOP == "p2d":
                break
            wdtp = W("dt")
            pD, pA, pAT, pTot = psf[0], psf[1], psf[2], psf[3]
            for q in range(NB):
                for kc in range(KC):
                    k.mm(pD[:, q * 32:(q + 1) * 32], hT[:, kc, q * 128:(q + 1) * 128], wdtp[:, kc, 0:32], start=(kc == 0),
                         stop=(kc == KC - 1), mark=(kc == KC - 1 and q == NB - 1))
            k.tt(DVE, dtp[:, :], pD[:, 0:128], dtb[:, :], ALU.add)
            k.act(dtt[:, :], dtp[:, :], AF.Exp)
            k.act(dtt[:, :], dtt[:, :], AF.Ln, bias=1.0)
            k.act(lndt[:, :], dtt[:, :], AF.Ln)
            for q in range(NB):
                for r in range(2):
                    k.tt(DVE, da2[:, q, r * 32:(r + 1) * 32], dtt[:, q * 32:(q + 1) * 32], a_bc[:, q * 32:(q + 1) * 32], ALU.mult)
            for q in range(NB):
                k.mm(pA[:, q * 32:(q + 1) * 32], tri_f[:, :], da2[:, q, 0:32], start=True, stop=True)
                k.mm(pTot[:, q * 32:(q + 1) * 32], ones_f[:, :], da2[:, q, 0:32], start=True, stop=True)
                k.mm(pAT[0:64, q * 128:(q + 1) * 128], da2[:, q, :], tri_f[:, :], start=True, stop=True, mark=(q == NB - 1))
            k.tt(DVE, biasA[:, :], lndt[:, :], pA[:, 0:128], ALU.subtract)
            k.act(Ee[:, :], pA[:, 0:128], AF.Exp)
            k.tt(DVE, tmpd[:, :], pTot[:, 0:128], biasA[:, :], ALU.add)
            k.act(wdt[:, :], tmpd[:, :], AF.Exp)
            k.act(decA[:, :], pTot[:, 0:128], AF.Exp)
            k.cp(DVE, acsT[0:32, :], pAT[0:32, :])
            k.cp(DVE, acsTt[32:64, :], pAT[32:64, :])
            k.tt(DVE, acsT[32:64, :], pAT[32:64, :], acsTt[32:64, :], ALU.subtract)
            if DBG_STOP == "p2e":
                break
            if DBG_STOP == "p2f":
                break
            wdsk = W("dsk")

            def ssd_front(si, q, g):
                if g == 0:
                    pC = psf[0]
                    for g2 in range(NG):
                        k.mm(pC[:, g2 * 128:(g2 + 1) * 128], BT[:, g2, q * 128:(q + 1) * 128], CT[:, g2, q * 128:(q + 1) * 128],
                             start=True, stop=True, mark=(g2 == NG - 1))
                    k.tt(DVE, cbm.cust(0, [[1, 512]]), pC[:, :], caus[:, :], ALU.mult)
                ex = expo[si % 3]
                for hh in range(2):
                    pR = psf[1 + hh]
                    k.mm(pR[:, :], ident_b[:, :], negm[:, :], start=True, stop=False)
                    for jl in range(4):
                        j = g * 8 + hh * 4 + jl
                        k.mm(pR[:, jl * 128:(jl + 1) * 128], sel[0:64, j * 128:(j + 1) * 128], acsT[0:64, q * 128:(q + 1) * 128],
                             start=False, stop=(jl == 3), mark=(jl == 3))
                    for jl in range(4):
                        j = g * 8 + hh * 4 + jl
                        k.act(ex[:, hh * 4 + jl, :], pR[:, jl * 128:(jl + 1) * 128], AF.Exp,
                              bias=biasA[:, q * 32 + j:q * 32 + j + 1])

            def ssd_front_b(si, q, g):
                ex = expo[si % 3]
                cbv = cbm.cust(g * 128, [[0, 8], [1, 128]], 0, 128, g * 128, (g + 1) * 128)
                k.tt(POOL, ex[:, :, :], ex[:, :, :], cbv, ALU.mult)
                wv_ = wdt.cust(q * 32 + g * 8, [[1, 8], [0, 64]], 0, 128, q * 32 + g * 8, q * 32 + g * 8 + 8)
                k.tt(POOL, xw[si % 3].cust(0, [[64, 8], [1, 64]]), xs.cust(q * DIN + g * 512, [[64, 8], [1, 64]], 0, 128,
                                                                     q * DIN + g * 512, q * DIN + (g + 1) * 512), wv_, ALU.mult)

            def ssd_back(si, q, g):
                ex = expo[si % 3]
                pY, pO, pDl = psf[3], psf[4], psf[5]
                k.mm(pO[:, :], CT[:, g, q * 128:(q + 1) * 128], Sb[:, g * 512:(g + 1) * 512], start=True, stop=True, mark=True)
                k.mm(pDl[:, :], Btok[:, q, g * 128:(g + 1) * 128], xw[si % 3][:, :], start=True, stop=True, mark=True)
                for jl in range(8):
                    j = g * 8 + jl
                    xsl = xs[:, q, j * 64:(j + 1) * 64]
                    k.mm(pY[:, jl * 64:(jl + 1) * 64], ex[:, jl, :], xsl, start=True, stop=False)
                    k.mm(pY[:, jl * 64:(jl + 1) * 64], wdsk.cust(j * 128, [[1, 128]], 0, 128, j * 128, (j + 1) * 128), xsl,
                         start=False, stop=True, mark=(jl == 7))
                ev = Ee.cust(q * 32 + g * 8, [[1, 8], [0, 64]], 0, 128, q * 32 + g * 8, q * 32 + g * 8 + 8)
                ot = otmp[g % 2]
                ycg = yc[:, (g % 2) * 512:(g % 2 + 1) * 512]
                k.tt(DVE, ot.cust(0, [[64, 8], [1, 64]]), pO.cust(0, [[64, 8], [1, 64]]), ev, ALU.mult)
                k.tt(DVE, ycg, pY[:, :], ot[:, :], ALU.add)
                k.tt(DVE, ycg, ycg, siluz[:, q, g * 512:(g + 1) * 512], ALU.mult)
                k.tt(DVE, S[:, g * 512:(g + 1) * 512], S[:, g * 512:(g + 1) * 512], pDl[:, :], ALU.add)
                k.cp(DVE, Sb[:, g * 512:(g + 1) * 512], S[:, g * 512:(g + 1) * 512])

            def ssd_back2(si, q, g):
                ycg = yc[:, (g % 2) * 512:(g % 2 + 1) * 512]
                k.act(junk[:, 0:512], ycg, AF.Square, accum=st4[:, 8 + g:9 + g])
                k.act(st4[:, 12 + g:13 + g], st4[:, 8 + g:9 + g], AF.Ln, bias=EPS, scale=1.0 / 512)
                k.act(st4[:, 12 + g:13 + g], st4[:, 12 + g:13 + g], AF.Exp, scale=-0.5)
                yng = yn[:, (g % 2) * 512:(g % 2 + 1) * 512]
                k.ts(DVE, yng, ycg, st4[:, 12 + g:13 + g])

            def ssd_back3(si, q, g):
                pb = psb[g % 2]
                for c4 in range(4):
                    k.tr(pb[:, c4 * 128:(c4 + 1) * 128], yn[:, (g % 2) * 512 + c4 * 128:(g % 2) * 512 + (c4 + 1) * 128], ident_b[:, :],
                         mark=(c4 == 3))
                dstv = ynT.cust(g * 4 * TT + q * 128, [[TT, 4], [1, 128]], 0, 128, g * 4 * TT, (g + 1) * 4 * TT)
                srcv = pb.cust(0, [[128, 4], [1, 128]])
                k.cp(ACT, dstv, srcv)

            def ssd_back_pool(si, q, g):
                dv = decA.cust(q * 32 + g * 8, [[1, 8], [0, 64]], 0, 128, q * 32 + g * 8, q * 32 + g * 8 + 8)
                Sg = S.cust(g * 512, [[64, 8], [1, 64]], 0, 128, g * 512, (g + 1) * 512)
                k.tt(POOL, Sg, Sg, dv, ALU.mult)

            steps = [(q, g) for q in range(NB) for g in range(NG)]
            for s0 in range(2):
                ssd_front(s0, *steps[s0])
                ssd_front_b(s0, *steps[s0])
            for si, (q, g) in enumerate(steps):
                if si + 2 < len(steps):
                    ssd_front(si + 2, *steps[si + 2])
                ssd_back_pool(si, q, g)
                if si + 2 < len(steps):
                    ssd_front_b(si + 2, *steps[si + 2])
                ssd_back(si, q, g)
                if si >= 1:
                    ssd_back2(si - 1, *steps[si - 1])
                if si >= 2:
                    ssd_back3(si - 2, *steps[si - 2])
            nS = len(steps)
            ssd_back2(nS - 1, *steps[nS - 1])
            ssd_back3(nS - 2, *steps[nS - 2])
            ssd_back3(nS - 1, *steps[nS - 1])
            if DBG_STOP == "p3":
                break
            k.dma(SP, x_res[:, :, :], None, xres_sem, in_ap=xv[ti])
            for cg in range(2):
                for kh in range(2):
                    ws = W("so%d%d" % (cg, kh))
                    for fl in range(4):
                        for c8 in range(8):
                            k.mm(psf[fl][:, :], ws[:, c8, fl * 128:(fl + 1) * 128], ynT[:, kh * 8 + c8, :],
                                 start=(kh == 0 and c8 == 0), stop=(kh == 1 and c8 == 7), mark=(kh == 1 and c8 == 7))
                wg = W("gs%d" % cg)
                for fl in range(4):
                    fo = cg * 4 + fl
                    pg = psf[4 + fl % 2]
                    for kc in range(KC):
                        k.mm(pg[:, :], wg[:, kc, fl * 128:(fl + 1) * 128], hT[:, kc, :], start=(kc == 0), stop=(kc == KC - 1),
                             mark=(kc == KC - 1))
                    k.act(sl[fl % 2][:, :], pg[:, :], AF.Sigmoid)
                    k.tt(DVE, sl[fl % 2][:, :], psf[fl][:, :], sl[fl % 2][:, :], ALU.mult)
                    k.tt(ew(), mg[:, fo, :], sl[fl % 2][:, :], t1[:, fo, :], ALU.add)
            wos = [W("wo0"), W("wo1")]
            for q in range(NB):
                for pc_ in range(2):
                    wo = wos[pc_]
                    pw = psf[(q * 2 + pc_) % 6]
                    for kc in range(KC):
                        k.mm(pw[:, :], mg[:, kc, q * 128:(q + 1) * 128], wo[:, kc, :], start=(kc == 0), stop=(kc == KC - 1),
                             mark=(kc == KC - 1))
                    k.tt(DVE, x_res[:, q, pc_ * 512:(pc_ + 1) * 512], x_res[:, q, pc_ * 512:(pc_ + 1) * 512], pw[:, :], ALU.add)
                norm_block(x_res, q, G2, SH2)
                if q >= 1:
                    norm_block2(q - 1, G2, SH2)
            norm_block2(NB - 1, G2, SH2)
            if ti + 1 < NT:
                k.dma(SP, x_in[:, :, :], None, xin_sem, in_ap=xv[ti + 1])
            if DBG_STOP == "p4":
                break
            for m in range(11):
                wf = W("fi%d" % m)
                for il in range(2):
                    i = 2 * m + il
                    pgt = psf[(i % 3) * 2]
                    pup = psf[(i % 3) * 2 + 1]
                    for kc in range(KC):
                        k.mm(pgt[:, :], wf[:, kc, il * 128:(il + 1) * 128], hT[:, kc, :], start=(kc == 0), stop=(kc == KC - 1),
                             mark=(kc == KC - 1))
                    for kc in range(KC):
                        k.mm(pup[:, :], wf[:, kc, 256 + il * 128:256 + (il + 1) * 128], hT[:, kc, :], start=(kc == 0),
                             stop=(kc == KC - 1), mark=(kc == KC - 1))
                    k.act(sl[i % 2][:, :], pgt[:, :], AF.Silu)
                    k.tt(DVE, actT[:, i, :], sl[i % 2][:, :], pup[:, :], ALU.mult)
            nxt = (ti + 1 < NT) and DBG_STOP is None
            if nxt:
                normA(x_in)
            for cg in range(2):
                if cg == 1 and nxt:
                    normB(x_in, G1, SH1)
                for ks in range(3):
                    wd_ = W("fd%d%d" % (cg, ks))
                    nk = 8 if ks < 2 else 6
                    for q in range(NB):
                        for c8 in range(nk):
                            c = ks * 8 + c8
                            k.mm(psf[q][:, :], actT[:, c, q * 128:(q + 1) * 128], wd_[:, c8, :], start=(c == 0), stop=(c == NFF - 1),
                                 mark=(c == NFF - 1))
                for q in range(NB):
                    k.tt(DVE, x_res[:, q, cg * 512:(cg + 1) * 512], x_res[:, q, cg * 512:(cg + 1) * 512], psf[q][:, :], ALU.add)
            if DBG_STOP == "p5":
                break
            for q in range(NB):
                k.act(junk[:, :], x_res[:, q, :], AF.Square, accum=st4[:, q:q + 1])
            k.act(st4[:, 4:8], st4[:, 0:4], AF.Ln, bias=EPS, scale=1.0 / D)
            k.act(st4[:, 4:8], st4[:, 4:8], AF.Exp, scale=-0.5)
            for q in range(NB):
                k.stt(DVE, x_res[:, q, :], x_res[:, q, :], st4[:, 4 + q:5 + q], fw_bc[:, :], ALU.mult, ALU.mult)
            k.dma(SP, None, x_res[:, :, :], y_sem, out_ap=yv[ti])
        SP.h.wait_ge(y_sem.sem, y_sem.cnt)
      except _Stop:
        pass
    return nc


def host_consts():
    ident = np.eye(128, dtype=np.float32)
    s = np.arange(128)[:, None]
    l = np.arange(128)[None, :]
    tri = (s <= l).astype(np.float32)
    neg1 = np.where(l < s, -30000.0, 0.0).astype(np.float32)
    caus1 = (l >= s).astype(np.float32)
    negmask4 = np.tile(neg1, (1, 4))
    causal4 = np.tile(caus1, (1, 4))
    sel = np.zeros((64, 32, 128), np.float32)
    for j in range(32):
        sel[j, j, :] = 1.0
        sel[32 + j, j, :] = 1.0
    return {"ident": ident, "tri": tri, "negmask4": negmask4, "causal4": causal4, "sel": sel.reshape(64, 4096)}


def host_layout(inp, b, Lc):
    f = lambda a: np.ascontiguousarray(np.asarray(a, dtype=np.float32))
    pT = lambda v, n: f(np.asarray(v).reshape(n, 128).T)
    m = {}
    m["x"] = f(inp["x"][b, :Lc])
    m["cT"] = pT(inp["c"][b], 8)
    m["w_ada_mix"] = f(inp["w_ada_mix"][0])
    m["w_ada_ffn"] = f(inp["w_ada_ffn"][0])
    m["bT_ada_mix"] = pT(inp["b_ada_mix"][0], 24)
    m["bT_ada_ffn"] = pT(inp["b_ada_ffn"][0], 24)
    m["brow_ada_mix"] = f(np.asarray(inp["b_ada_mix"][0]).reshape(1, 3072))
    m["brow_ada_ffn"] = f(np.asarray(inp["b_ada_ffn"][0]).reshape(1, 3072))
    m["nw1T"] = pT(inp["norm_mix_w"][0], 8)
    m["nw2T"] = pT(inp["norm_ffn_w"][0], 8)
    m["w_in"] = f(inp["w_in"][0])
    m["w31T"] = f(np.asarray(inp["conv_dw_w"][0]).reshape(31, 8, 128).transpose(2, 1, 0))
    m["cb31T"] = pT(inp["conv_dw_b"][0], 8)
    m["lnwT"] = pT(inp["conv_ln_w"][0], 8)
    m["lnbT"] = pT(inp["conv_ln_b"][0], 8)
    m["w_conv_out"] = f(inp["w_conv_out"][0])
    m["w4T"] = f(np.asarray(inp["ssm_conv_w"][0]).reshape(4, 24, 128).transpose(2, 1, 0))
    m["cb4T"] = pT(inp["ssm_conv_b"][0], 24)
    cbm_ = np.zeros((32, 128), np.float32)
    cbm_[:24] = np.asarray(inp["ssm_conv_b"][0]).reshape(24, 128)
    m["cb4mat"] = cbm_
    m["dtb_bc"] = f(np.tile(np.asarray(inp["dt_bias"][0]).reshape(1, 32), (128, 4)))
    m["alog_bc"] = f(np.tile(np.asarray(inp["a_log"][0]).reshape(1, 32), (128, 4)))
    m["dsk_bc"] = f(np.tile(np.asarray(inp["d_skip"][0]).reshape(1, 32), (128, 1)))
    m["snwT"] = pT(inp["ssm_norm_w"][0], 16)
    m["w_ssm_out"] = f(inp["w_ssm_out"][0])
    m["w_out"] = f(inp["w_out"][0])
    m["w_ffn_in"] = f(inp["w_ffn_in"][0])
    m["w_ffn_down"] = f(inp["w_ffn_down"][0])
    m["fw_bc"] = f(np.tile(np.asarray(inp["final_norm_w"]).reshape(1, D), (128, 1)))
    m.update(host_consts())
    return m


def kernel(**inputs):
    nb = inputs["x"].shape[0]
    Lc = inputs["x"].shape[1]
    NT = Lc // TT
    nc = build(NT)
    in_maps = [host_layout(inputs, b, Lc) for b in range(nb)]
    res = run_bass_kernel_spmd(nc, in_maps, core_ids=list(range(nb)))
    out = np.stack([np.asarray(r["y"], dtype=np.float32) for r in res.results], axis=0)
    return out
```
